# Optimizing a Trainium2 kernel written in Bass

```python
import math
import jax, jax.numpy as jnp
from jax import lax
import numpy as np

D_MODEL = 1024
BATCH = 8
SEQ = 2048
DEPTH = 1
DEC_BATCH = 32
DEC_SEQ = 8
PAST_LEN = 16384
PAGE_SIZE = 128

MLA_HEADS = 8
QK_NOPE_DIM = 64
ROPE_DIM = 32
V_HEAD_DIM = 64
Q_RANK = 256
KV_RANK = 256
ROPE_THETA = 10000.0
D_ATTN = MLA_HEADS * V_HEAD_DIM
D_MIX = D_MODEL
D_SSM = D_MIX - D_ATTN
SSM_CH = 16
SSM_GROUPS = D_SSM // SSM_CH
SSM_STATE = 64
D_IN = Q_RANK + KV_RANK + ROPE_DIM + D_SSM
N_MEM = 256
MEM_HEADS = 4
MEM_HEAD_DIM = 128
D_FF = 2816
CONV_W = 3

Q_BLOCK = 128
EPS = 1e-6
ATTN_SCALE = (QK_NOPE_DIM + ROPE_DIM) ** -0.5
MEM_SCALE = MEM_HEAD_DIM ** -0.5

kernel_name = 'hybrid_mla_s5_memxattn_convffn_step'

F32 = jnp.float32


def rmsnorm(x, g):
    xf = x.astype(F32)
    y = xf * lax.rsqrt(jnp.mean(xf * xf, axis=-1, keepdims=True) + EPS)
    return (y * g.astype(F32)).astype(x.dtype)


def rope(x, pos):
    half = ROPE_DIM // 2
    inv = ROPE_THETA ** (-jnp.arange(half, dtype=F32) * (2.0 / ROPE_DIM))
    ang = pos.astype(F32)[:, None] * inv[None, :]
    shape = (ang.shape[0],) + (1,) * (x.ndim - 3) + (half,)
    cos = jnp.cos(ang).reshape(shape)
    sin = jnp.sin(ang).reshape(shape)
    xf = x.astype(F32)
    x1, x2 = xf[..., :half], xf[..., half:]
    return jnp.concatenate([x1 * cos - x2 * sin, x1 * sin + x2 * cos], axis=-1).astype(x.dtype)


def attend(q_lat, q_pe, q_pos, k_lat, k_pe, k_pos):
    b, l, h, r = q_lat.shape
    blk = min(Q_BLOCK, l)
    nb = l // blk
    ql = q_lat.reshape(b, nb, blk, h, r).transpose(1, 0, 2, 3, 4)
    qp = q_pe.reshape(b, nb, blk, h, ROPE_DIM).transpose(1, 0, 2, 3, 4)
    qpos = q_pos.reshape(nb, blk)

    def one_block(args):
        ql_b, qp_b, pos_b = args
        s = (jnp.einsum('bqhr,bkr->bhqk', ql_b, k_lat)
             + jnp.einsum('bqhe,bke->bhqk', qp_b, k_pe)).astype(F32) * ATTN_SCALE
        mask = k_pos[None, :] <= pos_b[:, None]
        s = jnp.where(mask[None, None], s, -jnp.inf)
        pr = jax.nn.softmax(s, axis=-1).astype(k_lat.dtype)
        return jnp.einsum('bhqk,bkr->bqhr', pr, k_lat)

    o = lax.map(one_block, (ql, qp, qpos))
    return o.transpose(1, 0, 2, 3, 4).reshape(b, l, h, r)


def mla_mixer(c_q, c_kv, k_r, pos, p, past):
    b, l, _ = c_q.shape
    c_kv = rmsnorm(c_kv, p['kv_norm'])
    k_r = rope(k_r, pos)
    q = jnp.einsum('blr,rhe->blhe', rmsnorm(c_q, p['q_norm']), p['w_uq'])
    q_nope = q[..., :QK_NOPE_DIM]
    q_pe = rope(q[..., QK_NOPE_DIM:], pos)
    q_lat = jnp.einsum('blhd,rhd->blhr', q_nope, p['w_uk'])
    if past is None:
        keys_lat, keys_pe = c_kv, k_r
    else:
        keys_lat = jnp.concatenate([past[0], c_kv], axis=1)
        keys_pe = jnp.concatenate([past[1], k_r], axis=1)
    k_pos = jnp.arange(keys_lat.shape[1], dtype=jnp.int32)
    o_lat = attend(q_lat, q_pe, pos, keys_lat, keys_pe, k_pos)
    out = jnp.einsum('blhr,rhv->blhv', o_lat, p['w_uv']).reshape(b, l, D_ATTN)
    return out, c_kv, k_r


def _cmul_combine(e1, e2):
    a1r, a1i, b1r, b1i = e1
    a2r, a2i, b2r, b2i = e2
    ar = a2r * a1r - a2i * a1i
    ai = a2r * a1i + a2i * a1r
    br = a2r * b1r - a2i * b1i + b2r
    bi = a2r * b1i + a2i * b1r + b2i
    return ar, ai, br, bi


def s5_mixer(u, p, h0):
    b, l, _ = u.shape
    uf = u.astype(F32)
    ug = uf.reshape(b, l, SSM_GROUPS, SSM_CH)
    a_re = p['ssm_a_re'].astype(F32)
    a_im = p['ssm_a_im'].astype(F32)
    dt = jnp.exp(p['ssm_log_dt'].astype(F32))[:, None]
    mag = jnp.exp(dt * a_re)
    abr = mag * jnp.cos(dt * a_im)
    abi = mag * jnp.sin(dt * a_im)
    den = a_re * a_re + a_im * a_im
    nr, ni = abr - 1.0, abi
    fr = (nr * a_re + ni * a_im) / den
    fi = (ni * a_re - nr * a_im) / den
    b_re = p['ssm_b_re'].astype(F32)
    b_im = p['ssm_b_im'].astype(F32)
    bbr = fr[..., None] * b_re - fi[..., None] * b_im
    bbi = fr[..., None] * b_im + fi[..., None] * b_re
    bur = jnp.einsum('blgc,gnc->blgn', ug, bbr)
    bui = jnp.einsum('blgc,gnc->blgn', ug, bbi)
    if h0 is not None:
        h0r = h0[0].astype(F32)
        h0i = h0[1].astype(F32)
        bur = bur.at[:, 0].add(abr * h0r - abi * h0i)
        bui = bui.at[:, 0].add(abr * h0i + abi * h0r)
    a_r = jnp.broadcast_to(abr, bur.shape)
    a_i = jnp.broadcast_to(abi, bui.shape)
    _, _, hr, hi = lax.associative_scan(_cmul_combine, (a_r, a_i, bur, bui), axis=1)
    y = (jnp.einsum('blgn,gcn->blgc', hr, p['ssm_c_re'].astype(F32))
         - jnp.einsum('blgn,gcn->blgc', hi, p['ssm_c_im'].astype(F32)))
    y = y.reshape(b, l, D_SSM) + p['ssm_d'].astype(F32) * uf
    g = jax.nn.gelu(y)
    out = g * jax.nn.sigmoid(g @ p['ssm_w_glu'].astype(F32))
    return out.astype(u.dtype), hr[:, -1], hi[:, -1]


def mem_kv(mem, p):
    m = rmsnorm(mem, p['mem_norm'])
    b, n, _ = m.shape
    k = (m @ p['w_k_mem']).reshape(b, n, MEM_HEADS, MEM_HEAD_DIM)
    v = (m @ p['w_v_mem']).reshape(b, n, MEM_HEADS, MEM_HEAD_DIM)
    return k, v


def mem_attend(h, mk, mv, p):
    b, l, _ = h.shape
    q = (h @ p['w_q_mem']).reshape(b, l, MEM_HEADS, MEM_HEAD_DIM)
    s = jnp.einsum('blhd,bmhd->bhlm', q, mk).astype(F32) * MEM_SCALE
    pr = jax.nn.softmax(s, axis=-1).astype(mv.dtype)
    o = jnp.einsum('bhlm,bmhd->blhd', pr, mv).reshape(b, l, MEM_HEADS * MEM_HEAD_DIM)
    return o @ p['w_o_mem']


def conv_ffn(h, p, conv_prev):
    b, l, _ = h.shape
    g = h @ p['w_gate']
    up = h @ p['w_up']
    if conv_prev is None:
        conv_prev = jnp.zeros((b, CONV_W - 1, D_FF), g.dtype)
    gp = jnp.concatenate([conv_prev.astype(g.dtype), g], axis=1)
    w = p['ffn_conv_w']
    gc = p['ffn_conv_b'] + sum(w[k] * gp[:, k:k + l] for k in range(CONV_W))
    out = (jax.nn.silu(gc) * up) @ p['w_down']
    return out, gp[:, -(CONV_W - 1):]


def layer(x, pos, mk, mv, p, past_kv, ssm_h0, conv_prev):
    h = rmsnorm(x, p['norm_mix_pre'])
    z = h @ p['w_in']
    o1 = Q_RANK
    o2 = o1 + KV_RANK
    o3 = o2 + ROPE_DIM
    c_q, c_kv, k_r, u = z[..., :o1], z[..., o1:o2], z[..., o2:o3], z[..., o3:]
    attn_out, c_kv_n, k_r_n = mla_mixer(c_q, c_kv, k_r, pos, p, past_kv)
    ssm_out, hr, hi = s5_mixer(u, p, ssm_h0)
    mix = jnp.concatenate([rmsnorm(attn_out, p['norm_attn_out']),
                           rmsnorm(ssm_out, p['norm_ssm_out'])], axis=-1) @ p['w_out']
    x = x + rmsnorm(mix, p['norm_mix_post'])
    h = rmsnorm(x, p['norm_mem_pre'])
    x = x + rmsnorm(mem_attend(h, mk, mv, p), p['norm_mem_post'])
    h = rmsnorm(x, p['norm_ffn_pre'])
    f, conv_new = conv_ffn(h, p, conv_prev)
    x = x + rmsnorm(f, p['norm_ffn_post'])
    return x, c_kv_n, k_r_n, hr, hi, conv_new


def setup_inputs(seed: int = 0) -> dict:
    key = jax.random.key(seed)
    ks = iter(jax.random.split(key, 64))

    def nrm(shape, scale=1.0):
        return scale * jax.random.normal(next(ks), shape, F32)

    def gain(n):
        return 1.0 + 0.05 * nrm((DEPTH, n))

    n_pages = PAST_LEN // PAGE_SIZE
    n_used = DEC_BATCH * n_pages
    n_pool = n_used + n_used // 4
    perm = jax.random.permutation(next(ks), n_pool)[:n_used]
    page_table = perm.reshape(DEC_BATCH, n_pages).astype(jnp.int32)
    n_idx = jnp.arange(SSM_STATE, dtype=F32)
    G, N, C = SSM_GROUPS, SSM_STATE, SSM_CH
    return {
        'x_prompt': nrm((BATCH, SEQ, D_MODEL)),
        'x_sample': nrm((DEC_BATCH, DEC_SEQ, D_MODEL)),
        'mem_prompt': nrm((BATCH, N_MEM, D_MODEL)),
        'cache_kv_latent': nrm((DEPTH, n_pool, PAGE_SIZE, KV_RANK)),
        'cache_k_rope': nrm((DEPTH, n_pool, PAGE_SIZE, ROPE_DIM)),
        'page_table': page_table,
        'state_ssm_re': nrm((DEPTH, DEC_BATCH, G, N), 0.3),
        'state_ssm_im': nrm((DEPTH, DEC_BATCH, G, N), 0.3),
        'state_ffn_conv': nrm((DEPTH, DEC_BATCH, CONV_W - 1, D_FF)),
        'cache_mem_k': nrm((DEPTH, DEC_BATCH, N_MEM, MEM_HEADS, MEM_HEAD_DIM)),
        'cache_mem_v': nrm((DEPTH, DEC_BATCH, N_MEM, MEM_HEADS, MEM_HEAD_DIM)),
        'norm_mix_pre': gain(D_MODEL),
        'w_in': nrm((DEPTH, D_MODEL, D_IN), D_MODEL ** -0.5),
        'q_norm': gain(Q_RANK),
        'kv_norm': gain(KV_RANK),
        'w_uq': nrm((DEPTH, Q_RANK, MLA_HEADS, QK_NOPE_DIM + ROPE_DIM), Q_RANK ** -0.5),
        'w_uk': nrm((DEPTH, KV_RANK, MLA_HEADS, QK_NOPE_DIM), KV_RANK ** -0.5),
        'w_uv': nrm((DEPTH, KV_RANK, MLA_HEADS, V_HEAD_DIM), KV_RANK ** -0.5),
        'ssm_a_re': -0.5 + 0.01 * nrm((DEPTH, G, N)),
        'ssm_a_im': math.pi * n_idx + 0.01 * nrm((DEPTH, G, N)),
        'ssm_log_dt': jax.random.uniform(next(ks), (DEPTH, G), F32, math.log(1e-3), math.log(1e-1)),
        'ssm_b_re': nrm((DEPTH, G, N, C), C ** -0.5),
        'ssm_b_im': nrm((DEPTH, G, N, C), C ** -0.5),
        'ssm_c_re': nrm((DEPTH, G, C, N), N ** -0.5),
        'ssm_c_im': nrm((DEPTH, G, C, N), N ** -0.5),
        'ssm_d': nrm((DEPTH, D_SSM)),
        'ssm_w_glu': nrm((DEPTH, D_SSM, D_SSM), D_SSM ** -0.5),
        'norm_attn_out': gain(D_ATTN),
        'norm_ssm_out': gain(D_SSM),
        'w_out': nrm((DEPTH, D_MIX, D_MODEL), D_MIX ** -0.5),
        'norm_mix_post': gain(D_MODEL),
        'norm_mem_pre': gain(D_MODEL),
        'mem_norm': gain(D_MODEL),
        'w_q_mem': nrm((DEPTH, D_MODEL, MEM_HEADS * MEM_HEAD_DIM), D_MODEL ** -0.5),
        'w_k_mem': nrm((DEPTH, D_MODEL, MEM_HEADS * MEM_HEAD_DIM), D_MODEL ** -0.5),
        'w_v_mem': nrm((DEPTH, D_MODEL, MEM_HEADS * MEM_HEAD_DIM), D_MODEL ** -0.5),
        'w_o_mem': nrm((DEPTH, MEM_HEADS * MEM_HEAD_DIM, D_MODEL), (MEM_HEADS * MEM_HEAD_DIM) ** -0.5),
        'norm_mem_post': gain(D_MODEL),
        'norm_ffn_pre': gain(D_MODEL),
        'w_gate': nrm((DEPTH, D_MODEL, D_FF), D_MODEL ** -0.5),
        'w_up': nrm((DEPTH, D_MODEL, D_FF), D_MODEL ** -0.5),
        'ffn_conv_w': nrm((DEPTH, CONV_W, D_FF), CONV_W ** -0.5),
        'ffn_conv_b': nrm((DEPTH, D_FF), 0.02),
        'w_down': nrm((DEPTH, D_FF, D_MODEL), D_FF ** -0.5),
        'norm_ffn_post': gain(D_MODEL),
    }


def reference(x_prompt, x_sample, mem_prompt, cache_kv_latent, cache_k_rope, page_table,
              state_ssm_re, state_ssm_im, state_ffn_conv, cache_mem_k, cache_mem_v,
              norm_mix_pre, w_in, q_norm, kv_norm, w_uq, w_uk, w_uv,
              ssm_a_re, ssm_a_im, ssm_log_dt, ssm_b_re, ssm_b_im, ssm_c_re, ssm_c_im,
              ssm_d, ssm_w_glu, norm_attn_out, norm_ssm_out, w_out, norm_mix_post,
              norm_mem_pre, mem_norm, w_q_mem, w_k_mem, w_v_mem, w_o_mem, norm_mem_post,
              norm_ffn_pre, w_gate, w_up, ffn_conv_w, ffn_conv_b, w_down, norm_ffn_post):
    weights = dict(
        norm_mix_pre=norm_mix_pre, w_in=w_in, q_norm=q_norm, kv_norm=kv_norm,
        w_uq=w_uq, w_uk=w_uk, w_uv=w_uv, ssm_a_re=ssm_a_re, ssm_a_im=ssm_a_im,
        ssm_log_dt=ssm_log_dt, ssm_b_re=ssm_b_re, ssm_b_im=ssm_b_im, ssm_c_re=ssm_c_re,
        ssm_c_im=ssm_c_im, ssm_d=ssm_d, ssm_w_glu=ssm_w_glu, norm_attn_out=norm_attn_out,
        norm_ssm_out=norm_ssm_out, w_out=w_out, norm_mix_post=norm_mix_post,
        norm_mem_pre=norm_mem_pre, mem_norm=mem_norm, w_q_mem=w_q_mem, w_k_mem=w_k_mem,
        w_v_mem=w_v_mem, w_o_mem=w_o_mem, norm_mem_post=norm_mem_post,
        norm_ffn_pre=norm_ffn_pre, w_gate=w_gate, w_up=w_up, ffn_conv_w=ffn_conv_w,
        ffn_conv_b=ffn_conv_b, w_down=w_down, norm_ffn_post=norm_ffn_post)
    db = x_sample.shape[0]
    past_len = page_table.shape[1] * cache_kv_latent.shape[2]
    pos_p = jnp.arange(x_prompt.shape[1], dtype=jnp.int32)
    pos_s = past_len + jnp.arange(x_sample.shape[1], dtype=jnp.int32)

    xp, xs = x_prompt, x_sample
    p_kv, p_kr, p_sr, p_si, p_cv, p_mk, p_mv = [], [], [], [], [], [], []
    s_kv, s_kr, s_sr, s_si, s_cv = [], [], [], [], []
    for l in range(DEPTH):
        p = {name: w[l] for name, w in weights.items()}
        mk, mv = mem_kv(mem_prompt, p)
        xp, ckv, kr, hr, hi, cv = layer(xp, pos_p, mk, mv, p, None, None, None)
        p_kv.append(ckv); p_kr.append(kr); p_sr.append(hr); p_si.append(hi)
        p_cv.append(cv); p_mk.append(mk); p_mv.append(mv)
        past_lat = jnp.take(cache_kv_latent[l], page_table, axis=0).reshape(db, past_len, KV_RANK)
        past_pe = jnp.take(cache_k_rope[l], page_table, axis=0).reshape(db, past_len, ROPE_DIM)
        xs, ckv, kr, hr, hi, cv = layer(
            xs, pos_s, cache_mem_k[l], cache_mem_v[l], p, (past_lat, past_pe),
            (state_ssm_re[l], state_ssm_im[l]), state_ffn_conv[l])
        s_kv.append(ckv); s_kr.append(kr); s_sr.append(hr); s_si.append(hi); s_cv.append(cv)

    return (xp, xs,
            jnp.stack(p_kv), jnp.stack(p_kr), jnp.stack(p_sr), jnp.stack(p_si),
            jnp.stack(p_cv), jnp.stack(p_mk), jnp.stack(p_mv),
            jnp.stack(s_kv), jnp.stack(s_kr), jnp.stack(s_sr), jnp.stack(s_si),
            jnp.stack(s_cv))
```

```python
import math, os
import os
from contextlib import ExitStack
import numpy as np
import concourse.bass as bass
import concourse.mybir as mybir
from concourse.bass_utils import run_bass_kernel_spmd

F32 = mybir.dt.float32
BF16 = mybir.dt.bfloat16
I32 = mybir.dt.int32
AF = mybir.ActivationFunctionType
ALU = mybir.AluOpType

NCORES = 8
NP_, NSMP, NTOK = 2048, 32, 2080
BLOCKS = [(0, 512), (512, 512), (1024, 512), (1536, 512), (2048, 32)]
ATTN_SCALE = 96.0 ** -0.5
MEM_SCALE = 128.0 ** -0.5
EPS = 1e-6
DSIZE = {F32: 4, BF16: 2, I32: 4}
ENGS = ['pe', 'act', 'dve', 'pool', 'sp']
NSQ = {'sp': int(os.environ.get('KNSP', '24')), 'pool': int(os.environ.get('KNPL', '24'))}


class Prog:
    def __init__(self):
        self.ops = []
        self.per = {e: [] for e in ENGS}
        self.lastw = {}
        self.readers = {}
        self.dma_since = []

    def add(self, eng, fn, r=(), w=(), dma=False):
        oid = len(self.ops)
        deps = set()
        for t in list(r) + list(w):
            if t in self.lastw:
                deps.add(self.lastw[t])
        for t in w:
            deps.update(self.readers.get(t, {}).values())
            deps.update(self.readers.get((t, 'dma'), []))
        op = dict(id=oid, eng=eng, fn=fn, deps=deps, dma=dma)
        self.ops.append(op)
        self.per[eng].append(op)
        if dma:
            self.dma_since.append(oid)
        for t in w:
            self.lastw[t] = oid
            self.readers[t] = {}
            self.readers[(t, 'dma')] = []
        for t in r:
            if dma:
                self.readers.setdefault((t, 'dma'), []).append(oid)
            else:
                self.readers.setdefault(t, {})[eng] = oid
        return oid

    def barrier(self):
        last = set()
        for e in ENGS:
            comp = [op['id'] for op in self.per[e] if not op['dma'] and op['fn'] is not None]
            if comp:
                last.add(comp[-1])
        last.update(self.dma_since)
        self.dma_since = []
        for e in ENGS:
            oid = len(self.ops)
            op = dict(id=oid, eng=e, fn=None, deps=set(last), dma=False)
            self.ops.append(op)
            self.per[e].append(op)

    def emit(self, nc, es):
        ops = self.ops
        need = set()
        for op in ops:
            for d in op['deps']:
                dop = ops[d]
                if dop['dma']:
                    continue
                if dop['eng'] == 'pe' and op['eng'] == 'pe' and not op['dma']:
                    continue
                need.add(d)
        esem = {e: es.enter_context(nc.semaphore('se_' + e)) for e in ENGS}
        dsem = {q: [es.enter_context(nc.semaphore('sd_%s%d' % (q, i))) for i in range(n)]
                for q, n in NSQ.items()}
        for e in ENGS:
            c = 0
            k = 0
            for op in self.per[e]:
                if op['dma']:
                    op['sem'] = dsem[e][k % NSQ[e]]
                    op['val'] = 16 * (k // NSQ[e] + 1)
                    k += 1
                elif op['id'] in need:
                    c += 1
                    op['sig'] = c
        block = es.enter_context(nc.Block())

        def run(e, eng):
            waited = {}
            last_tsz = [None]
            for op in self.per[e]:
                waits = {}
                for d in op['deps']:
                    dop = ops[d]
                    if dop['dma']:
                        s, v = dop['sem'], dop['val']
                    else:
                        if dop['eng'] == 'pe' and e == 'pe' and not op['dma']:
                            continue
                        s, v = esem[dop['eng']], dop['sig']
                    if waits.get(s, (None, 0))[1] < v:
                        waits[s] = (s, v)
                if op['dma'] and op['val'] > 16:
                    s, v = op['sem'], op['val'] - 16
                    if waits.get(s, (None, 0))[1] < v:
                        waits[s] = (s, v)
                for s, v in waits.values():
                    if waited.get(s, 0) >= v:
                        continue
                    waited[s] = v
                    eng.wait_ge(s, v)
                if op['fn'] is None:
                    continue
                if e == 'pe' and 'tsz' in op and os.environ.get('KDRAIN', '0') == '1':
                    if last_tsz[0] is not None and last_tsz[0] != op['tsz']:
                        eng.drain()
                    last_tsz[0] = op['tsz']
                ins = op['fn'](eng)
                if op['dma']:
                    ins.then_inc(op['sem'], 16)
                elif op['id'] in need:
                    ins.then_inc(esem[e], 1)

        @block.tensor
        def _(eng):
            run('pe', eng)

        @block.scalar
        def _(eng):
            run('act', eng)

        @block.vector
        def _(eng):
            run('dve', eng)

        @block.gpsimd
        def _(eng):
            run('pool', eng)

        @block.sync
        def _(eng):
            run('sp', eng)


class Arena:
    def __init__(self, ap, nwords):
        self.ap, self.n, self.off = ap, nwords, 0
        self.hi = 0

    def alloc(self, free, dtype=F32):
        n = 1
        for s in free:
            n *= s
        words = (n * DSIZE[dtype] + 3) // 4
        words = (words + 15) // 16 * 16
        a = self.off
        self.off += words
        self.hi = max(self.hi, self.off)
        assert self.off <= self.n, ('arena overflow', self.off, self.n)
        v = self.ap[:, a:a + words]
        if dtype != F32:
            v = v.bitcast(dtype)
        v = v[:, 0:n]
        if len(free) == 2:
            v = v.rearrange('p (a b) -> p a b', a=free[0])
        elif len(free) == 3:
            v = v.rearrange('p (a b c) -> p a b c', a=free[0], b=free[1])
        elif len(free) == 4:
            v = v.rearrange('p (a b c d) -> p a b c d', a=free[0], b=free[1], c=free[2])
        return v


def build(npool=5120, debug=False):
    nc = bass.Bass('TRN2', target_bir_lowering=False)
    pr = Prog()

    def din(name, shape, dt=F32):
        return nc.dram_tensor(name, list(shape), dt, kind='ExternalInput').ap()

    def dout(name, shape, dt=F32):
        return nc.dram_tensor(name, list(shape), dt, kind='ExternalOutput').ap()

    x_p = din('x_p', [NP_, 1024]); x_s = din('x_s', [NSMP, 1024]); mem_p = din('mem_p', [256, 1024])
    c_lat = din('c_lat', [npool, 128, 256]); c_pe = din('c_pe', [npool, 128, 32])
    ptab = din('ptab', [1, 512], I32)
    st_re = din('st_re', [64, 128]); st_im = din('st_im', [64, 128])
    st_cv = din('st_cv', [8, 2816])
    memk = din('memk', [4, 256, 512]); memv = din('memv', [4, 256, 512])
    gains = din('gains', [160, 128])
    w_in = din('w_in', [1024, 1056]); w_uq = din('w_uq', [256, 768]); w_uk = din('w_uk', [256, 512])
    w_uv = din('w_uv', [256, 512])
    a_re = din('a_re', [16, 128]); a_im = din('a_im', [16, 128]); log_dt = din('log_dt', [16, 2])
    b_re = din('b_re', [2048, 16]); b_im = din('b_im', [2048, 16])
    cc_re = din('cc_re', [32, 16, 64]); cc_im = din('cc_im', [32, 16, 64])
    w_glu = din('w_glu', [512, 512]); w_out = din('w_out', [1024, 1024])
    w_qm = din('w_qm', [1024, 512]); w_km = din('w_km', [1024, 512]); w_vm = din('w_vm', [1024, 512])
    w_om = din('w_om', [512, 1024])
    w_gate = din('w_gate', [22, 128, 1024]); w_up = din('w_up', [22, 128, 1024]); w_down = din('w_down', [8, 128, 2816])
    k_ident = din('k_ident', [128, 128]); k_rope = din('k_rope', [32, 2, NTOK])
    k_tri = din('k_tri', [128, 128]); k_mskn = din('k_mskn', [8, 64]); k_iota = din('k_iota', [1, 2048])
    k_smask = din('k_smask', [1, 32]); k_pidx32 = din('k_pidx32', [128, 1])

    y_p = dout('y_p', [NP_, 1024]); y_s = dout('y_s', [NSMP, 1024])
    o_kvl_p = dout('o_kvl_p', [NP_, 256]); o_kr_p = dout('o_kr_p', [NP_, 32])
    o_sre_p = dout('o_sre_p', [16, 128]); o_sim_p = dout('o_sim_p', [16, 128])
    o_cv_p = dout('o_cv_p', [2, 2816])
    o_mk_p = dout('o_mk_p', [256, 512]); o_mv_p = dout('o_mv_p', [256, 512])
    o_kvl_s = dout('o_kvl_s', [NSMP, 256]); o_kr_s = dout('o_kr_s', [NSMP, 32])
    o_sre_s = dout('o_sre_s', [64, 128]); o_sim_s = dout('o_sim_s', [64, 128])
    o_cv_s = dout('o_cv_s', [8, 2816])

    es = ExitStack()
    NW = 53000
    arena_t = es.enter_context(nc.sbuf_tensor('arena', [128, NW], F32))
    AR = Arena(arena_t[:], NW)
    PS = [es.enter_context(nc.psum_tensor('ps%d' % i, [128, 512], F32)) for i in range(8)]
    rg = es.enter_context(nc.sync.register('rg'))
    out_dmas = []

    def _rnd(x):
        return 32 if x <= 32 else (64 if x <= 64 else 128)

    def _tsz(ap):
        sh = ap.shape
        fsz = 1
        for d_ in sh[1:]:
            fsz *= d_
        return (_rnd(sh[0]), _rnd(fsz))

    def mm(out, lhsT, rhs, start, stop, r, w):
        oid = pr.add('pe', lambda e: e.matmul(out, lhsT=lhsT, rhs=rhs, start=start, stop=stop), r, w)
        pr.ops[oid]['tsz'] = _tsz(lhsT)

    def tr(out, in_, ident, r, w):
        oid = pr.add('pe', lambda e: e.transpose(out, in_, ident), r, w)
        pr.ops[oid]['tsz'] = _tsz(in_)

    def act(out, in_, func, r, w, scale=1.0, bias=None):
        if bias is None:
            pr.add('act', lambda e: e.activation(out=out, in_=in_, func=func, scale=scale), r, w)
        else:
            pr.add('act', lambda e: e.activation(out=out, in_=in_, func=func, scale=scale, bias=bias), r, w)

    def cp(eng, out, in_, r, w):
        if eng == 'pool' and os.environ.get('KNOPOOL', '0') == '1':
            eng = 'dve'
        if eng == 'act':
            pr.add('act', lambda e: e.activation(out=out, in_=in_, func=AF.Copy), r, w)
        else:
            pr.add(eng, lambda e: e.tensor_copy(out=out, in_=in_), r, w)

    def tt(out, in0, in1, op, r, w, eng='dve'):
        pr.add(eng, lambda e: e.tensor_tensor(out=out, in0=in0, in1=in1, op=op), r, w)

    def ts(out, in0, s1, s2, op0, op1, r, w, eng='dve'):
        if s2 is None:
            pr.add(eng, lambda e: e.tensor_scalar(out=out, in0=in0, scalar1=s1, scalar2=None, op0=op0), r, w)
        else:
            pr.add(eng, lambda e: e.tensor_scalar(out=out, in0=in0, scalar1=s1, scalar2=s2, op0=op0, op1=op1), r, w)

    def stt(out, in0, scalar, in1, op0, op1, r, w):
        pr.add('dve', lambda e: e.scalar_tensor_tensor(out=out, in0=in0, scalar=scalar, in1=in1, op0=op0, op1=op1), r, w)

    def scan(out, d0, d1, init, r, w):
        pr.add('dve', lambda e: e.tensor_tensor_scan(out=out, data0=d0, data1=d1, initial=init,
                                                     op0=ALU.mult, op1=ALU.add), r, w)

    def recip(out, in_, r, w):
        pr.add('dve', lambda e: e.reciprocal(out=out, in_=in_), r, w)

    def memset(eng, ap, val, w):
        if eng == 'pool' and os.environ.get('KNOPOOL', '0') == '1':
            eng = 'dve'
        pr.add(eng, lambda e: e.memset(ap, val), (), w)

    def dma(out, in_, r, w, q='sp', slow=False):
        if slow:
            return pr.add(q, lambda e: e.dma_start(out=out, in_=in_, allow_slow_non_contiguous=True), r, w, dma=True)
        return pr.add(q, lambda e: e.dma_start(out=out, in_=in_), r, w, dma=True)

    def dma_out(out, in_, r):
        out_dmas.append(dma(out, in_, r, ['__out%d' % len(out_dmas)]))

    rot = {}

    def nxt(key, n):
        rot[key] = (rot.get(key, -1) + 1) % n
        return rot[key]

    xT = AR.alloc([8, NTOK]); t_xT = ['xT%d' % b for b in range(5)]
    ident = AR.alloc([128]); identb = AR.alloc([128], BF16)
    ones_b = AR.alloc([128], BF16); ones_f = AR.alloc([128])
    G = AR.alloc([160])
    epsT = AR.alloc([1]); hpiT = AR.alloc([1])
    tri = AR.alloc([128], BF16); mskn = AR.alloc([64], BF16)
    GO = dict(mix_pre=0, q=8, kv=10, attn=12, ssm=16, mix_post=20, mem_pre=28, mem=36, mem_post=44,
              ffn_pre=52, ffn_post=60, d=68, cw0=72, cw1=94, cw2=116, cb=138)

    dma(ident, k_ident, [], ['ident'])
    cp('act', identb, ident, ['ident'], ['identb'])
    memset('pool', ones_b, 1.0, ['ones_b']); memset('pool', ones_f, 1.0, ['ones_f'])
    memset('pool', epsT, EPS, ['epsT']); memset('pool', hpiT, math.pi / 2, ['hpiT'])
    KOLD = os.environ.get('KOLD', '0')
    if '1' in KOLD:
        trf = AR.alloc([128]); mnf = AR.alloc([64]); graw = AR.alloc([2, 128])
    if '2' not in KOLD:
        KmT = AR.alloc([4, 256], BF16); Vm = AR.alloc([2, 512], BF16)
    mark_persist = AR.off

    def cast_load(dst, src, tok, q='pool'):
        dma(dst, src, [], [tok], q=q)

    def norm_fm(src, C, n, gcol, D, dst, r, w, sq, ssb, rstd, psb, resid=None, tmp=None, sfx=''):
        tq_, ts_, tr_, tt_ = 'nsq' + sfx, 'nss' + sfx, 'nrstd' + sfx, 'ntmp' + sfx
        act(sq[:, 0:C, 0:n], src, AF.Square, r, [tq_])
        for c in range(C):
            mm(PS[psb][:, 0:n], ones_b, sq[:, c, 0:n], c == 0, c == C - 1, [tq_, 'ones_b'], ['ps%d' % psb])
        act(ssb[:, 0:n], PS[psb][:, 0:n], AF.Ln, ['ps%d' % psb, 'epsT'], [ts_], scale=1.0 / D, bias=epsT)
        act(rstd[:, 0:n], ssb[:, 0:n], AF.Exp, [ts_], [tr_], scale=-0.5)
        for c in range(C):
            if resid is None:
                stt(dst[:, c, :], src[:, c, :], G[:, gcol + c:gcol + c + 1], rstd[:, 0:n], ALU.mult, ALU.mult,
                    list(r) + [tr_, 'G'], w)
            else:
                stt(tmp[:, 0:n], src[:, c, :], G[:, gcol + c:gcol + c + 1], rstd[:, 0:n], ALU.mult, ALU.mult,
                    list(r) + [tr_, 'G'], [tt_])
                tt(resid[:, c, :], resid[:, c, :], tmp[:, 0:n], ALU.add, [tt_] + list(w), w)

    n_sq = AR.alloc([8, 512], BF16); n_ss = AR.alloc([512]); n_rstd = AR.alloc([512]); n_tmp = AR.alloc([512])

    def NORM(src, C, n, gname, D, dst, r, w, psb=7, resid=None, ns=None):
        if ns is None:
            norm_fm(src, C, n, GO[gname], D, dst, r, w, n_sq, n_ss, n_rstd, psb, resid=resid, tmp=n_tmp)
        else:
            norm_fm(src, C, n, GO[gname], D, dst, r, w, ns[0], ns[1], ns[2], 6, resid=resid, tmp=ns[3], sfx='B')

    def norm_set():
        return (AR.alloc([8, 512], BF16), AR.alloc([512]), AR.alloc([512]), AR.alloc([512]))

    m0 = AR.off
    if '1' not in KOLD:
        trf = AR.alloc([128]); mnf = AR.alloc([64])
    dma(trf, k_tri, [], ['trf']); dma(mnf[0:8, :], k_mskn, [], ['mnf'])
    cp('dve', tri, trf, ['trf'], ['tri']); cp('dve', mskn[0:8, :], mnf[0:8, :], ['mnf'], ['mskn'])
    if '1' not in KOLD:
        graw = AR.alloc([2, 128])
    dma(graw[:, 0, :], gains[0:128, :], [], ['graw0']); dma(graw[0:32, 1, :], gains[128:160, :], [], ['graw1'])
    tr(PS[0][:, 0:128], graw[:, 0, :], ident, ['graw0', 'ident'], ['ps0'])
    tr(PS[0][:, 128:160], graw[0:32, 1, :], ident[0:32, 0:32], ['graw1', 'ident'], ['ps0'])
    cp('dve', G, PS[0][:, 0:160], ['ps0'], ['G'])
    xtm = AR.alloc([2, 1024])
    for t in range(17):
        sl = t % 2
        nt = 128 if t < 16 else 32
        src = x_p[t * 128:(t + 1) * 128, :] if t < 16 else x_s[:, :]
        dma(xtm[0:nt, sl, :], src, [], ['xtm%d' % sl])
        b = min(t // 4, 4)
        col0 = t * 128
        for half in range(2):
            pb = nxt('x0', 2)
            for c4 in range(4):
                c = half * 4 + c4
                tr(PS[pb][:, c4 * 128:c4 * 128 + nt], xtm[0:nt, sl, c * 128:(c + 1) * 128], ident[0:nt, 0:nt],
                   ['xtm%d' % sl, 'ident'], ['ps%d' % pb])
            cp('act' if half == 0 else 'dve', xT[:, half * 4:half * 4 + 4, col0:col0 + nt],
               PS[pb][:].rearrange('p (a b) -> p a b', a=4)[:, :, 0:nt], ['ps%d' % pb], [t_xT[b]])
    pr.barrier()
    AR.off = m0

    def out_rows(src_fm, C, n, dst_rows, width, r, base=0):
        for t0 in range(0, n, 128):
            nt = min(128, n - t0)
            pb = nxt('orow', 2)
            sl = nxt('otm', 2)
            for c in range(C):
                if base == 0:
                    tr(PS[pb][0:nt, c * 128:(c + 1) * 128], src_fm[:, c, t0:t0 + nt], ident, list(r) + ['ident'], ['ps%d' % pb])
                else:
                    tr(PS[pb][0:nt, 0:32], src_fm[base:base + 32, c, t0:t0 + nt], ident[base:base + 32, base:base + 32],
                       list(r) + ['ident'], ['ps%d' % pb])
            cp('act', otm[0:nt, sl, 0:width], PS[pb][0:nt, 0:width], ['ps%d' % pb], ['otm%d' % sl])
            dma_out(dst_rows[t0:t0 + nt, :], otm[0:nt, sl, 0:width], ['otm%d' % sl])

    otm = AR.alloc([2, 512])

    mUT = AR.off
    uT = AR.alloc([4, NTOK], BF16); attn_o = AR.alloc([4, NTOK], BF16)
    mATT = AR.off
    ropeT = AR.alloc([2, NTOK])
    dma(ropeT[64:96, :, :], k_rope, [], ['ropeT'])
    cqn = AR.alloc([2, NTOK], BF16); ckvn = AR.alloc([2, NTOK], BF16)
    KpeT = AR.alloc([NTOK], BF16)
    mA = AR.off
    Win = AR.alloc([8, 1056], BF16); WinSw = AR.alloc([8, 32], BF16)
    w_in_v = w_in.rearrange('(c p) n -> p c n', p=128)
    for c in range(8):
        cast_load(Win[:, c, :], w_in_v[:, c, :], 'Win')
    cast_load(WinSw[:, :, 0:16], w_in_v[:, :, 528:544], 'WinSw')
    cast_load(WinSw[:, :, 16:32], w_in_v[:, :, 512:528], 'WinSw')
    ts(WinSw[:, :, 0:16], WinSw[:, :, 0:16], -1.0, None, ALU.mult, None, ['WinSw'], ['WinSw'])
    hT = AR.alloc([8, 512], BF16)
    zq = AR.alloc([2, 512]); zkv = zq; zkvn = AR.alloc([2, 512])
    krf = AR.alloc([1, 512]); rt1 = AR.alloc([512]); rt2 = AR.alloc([512])
    for bi, (off, n) in enumerate(BLOCKS):
        NORM(xT[:, :, off:off + n], 8, n, 'mix_pre', 1024, hT[:, :, 0:n], [t_xT[bi]], ['hT'])

        def proj(col0, M, outp, pb, wt=Win, wtok='Win'):
            for k in range(8):
                mm(outp, wt[:, k, col0:col0 + M], hT[:, k, 0:n], k == 0, k == 7, [wtok, 'hT'], ['ps%d' % pb])
        for c in range(2):
            pb = nxt('pj', 4)
            proj(c * 128, 128, PS[pb][:, 0:n], pb)
            cp('act', zq[:, c, 0:n], PS[pb][:, 0:n], ['ps%d' % pb], ['zq'])
        NORM(zq[:, :, 0:n], 2, n, 'q', 256, cqn[:, :, off:off + n], ['zq'], ['cqn'])
        for c in range(2):
            pb = nxt('pj', 4)
            proj(256 + c * 128, 128, PS[pb][:, 0:n], pb)
            cp('act', zkv[:, c, 0:n], PS[pb][:, 0:n], ['ps%d' % pb], ['zq'])
        NORM(zkv[:, :, 0:n], 2, n, 'kv', 256, zkvn[:, :, 0:n], ['zq'], ['zkvn'])
        cp('act', ckvn[:, :, off:off + n], zkvn[:, :, 0:n], ['zkvn'], ['ckvn'])
        out_rows(zkvn, 2, n, (o_kvl_p[off:off + n, :] if bi < 4 else o_kvl_s), 256, ['zkvn'])
        pb = nxt('pj', 4); pb2 = nxt('pj', 4)
        proj(512, 32, PS[pb][64:96, 0:n], pb)
        proj(0, 32, PS[pb2][64:96, 0:n], pb2, wt=WinSw, wtok='WinSw')
        tt(rt1[64:96, 0:n], PS[pb][64:96, 0:n], ropeT[64:96, 0, off:off + n], ALU.mult, ['ps%d' % pb, 'ropeT'], ['rt1'])
        tt(rt2[64:96, 0:n], PS[pb2][64:96, 0:n], ropeT[64:96, 1, off:off + n], ALU.mult, ['ps%d' % pb2, 'ropeT'], ['rt2'])
        tt(krf[64:96, 0, 0:n], rt1[64:96, 0:n], rt2[64:96, 0:n], ALU.add, ['rt1', 'rt2'], ['krf'])
        cp('act', KpeT[64:96, off:off + n], krf[64:96, 0, 0:n], ['krf'], ['KpeT'])
        out_rows(krf, 1, n, (o_kr_p[off:off + n, :] if bi < 4 else o_kr_s), 32, ['krf'], base=64)
        for c in range(4):
            pb = nxt('pj', 4)
            proj(544 + c * 128, 128, PS[pb][:, 0:n], pb)
            cp('act' if c % 2 else 'dve', uT[:, c, off:off + n], PS[pb][:, 0:n], ['ps%d' % pb], ['uT'])
    pr.barrier()
    AR.off = mA

    Wuv = AR.alloc([2, 512], BF16); QA = AR.alloc([4, 2, 64], BF16); QAp = AR.alloc([4, 64], BF16)
    mSA = AR.off
    Wuq = AR.alloc([2, 768], BF16); WuqSw = AR.alloc([2, 8, 32], BF16)
    Wuk = AR.alloc([2, 512], BF16); WukT = AR.alloc([8, 256], BF16)
    w_uq_v = w_uq.rearrange('(c p) n -> p c n', p=128)
    cast_load(Wuq, w_uq_v, 'Wuq')
    w_uq_4 = w_uq.rearrange('(c p) (h e) -> p c h e', p=128, e=96)
    for c in range(2):
        cast_load(WuqSw[:, c, :, 0:16], w_uq_4[:, c, :, 80:96], 'WuqSw')
        cast_load(WuqSw[:, c, :, 16:32], w_uq_4[:, c, :, 64:80], 'WuqSw')
    ts(WuqSw[:, :, :, 0:16], WuqSw[:, :, :, 0:16], -1.0, None, ALU.mult, None, ['WuqSw'], ['WuqSw'])
    cast_load(Wuk, w_uk.rearrange('(c p) n -> p c n', p=128), 'Wuk')
    cast_load(Wuv, w_uv.rearrange('(c p) n -> p c n', p=128), 'Wuv')
    for h in range(8):
        pb = nxt('pj', 4)
        psb = PS[pb][:].bitcast(BF16)
        for c in range(2):
            tr(psb[0:64, c * 128:(c + 1) * 128], Wuk[:, c, h * 64:(h + 1) * 64], identb, ['Wuk', 'identb'], ['ps%d' % pb])
        cp('dve', WukT[0:64, h, :], psb[0:64, 0:256], ['ps%d' % pb], ['WukT'])
    QT = [AR.alloc([NTOK], BF16) for _ in range(2)]
    KT = [AR.alloc([NTOK], BF16) for _ in range(2)]
    Vh = [AR.alloc([17, 65], BF16) for _ in range(2)]
    PT = [AR.alloc([512], BF16) for _ in range(3)]
    qr1 = AR.alloc([512]); qr2 = AR.alloc([512])
    osb = AR.alloc([512]); rsb = AR.alloc([512]); otmpb = AR.alloc([512], BF16)
    for i in range(2):
        memset('pool', Vh[i][:, :, 64:65], 1.0, ['Vh%d' % i])
    def att_prep(h):
        hs = h % 2
        tq, tk, tv = 'QT%d' % hs, 'KT%d' % hs, 'Vh%d' % hs
        for bi, (off, n) in enumerate(BLOCKS):
            pb = nxt('pj', 4); pb2 = nxt('pj', 4); pb3 = nxt('pj', 4)
            for c in range(2):
                mm(PS[pb][0:96, 0:n], Wuq[:, c, h * 96:(h + 1) * 96], cqn[:, c, off:off + n], c == 0, c == 1,
                   ['Wuq', 'cqn'], ['ps%d' % pb])
            for c in range(2):
                mm(PS[pb2][64:96, 0:n], WuqSw[:, c, h, :], cqn[:, c, off:off + n], c == 0, c == 1,
                   ['WuqSw', 'cqn'], ['ps%d' % pb2])
            cp('act', QT[hs][0:64, off:off + n], PS[pb][0:64, 0:n], ['ps%d' % pb], [tq])
            tt(qr1[64:96, 0:n], PS[pb][64:96, 0:n], ropeT[64:96, 0, off:off + n], ALU.mult, ['ps%d' % pb, 'ropeT'], ['qr1'])
            tt(qr2[64:96, 0:n], PS[pb2][64:96, 0:n], ropeT[64:96, 1, off:off + n], ALU.mult, ['ps%d' % pb2, 'ropeT'], ['qr2'])
            tt(QT[hs][64:96, off:off + n], qr1[64:96, 0:n], qr2[64:96, 0:n], ALU.add, ['qr1', 'qr2'], [tq])
            for c in range(2):
                mm(PS[pb3][0:64, 0:n], Wuk[:, c, h * 64:(h + 1) * 64], ckvn[:, c, off:off + n], c == 0, c == 1,
                   ['Wuk', 'ckvn'], ['ps%d' % pb3])
            cp('act', KT[hs][0:64, off:off + n], PS[pb3][0:64, 0:n], ['ps%d' % pb3], [tk])
        cp('pool', KT[hs][64:96, :], KpeT[64:96, :], ['KpeT'], [tk])
        for g4 in range(5):
            pb = nxt('pj', 4)
            ntl = 4 if g4 < 4 else 1
            for j in range(ntl):
                t = g4 * 4 + j
                nt = 128 if t < 16 else 32
                for c in range(2):
                    mm(PS[pb][0:nt, j * 64:(j + 1) * 64], ckvn[:, c, t * 128:t * 128 + nt], Wuv[:, c, h * 64:(h + 1) * 64],
                       c == 0, c == 1, ['ckvn', 'Wuv'], ['ps%d' % pb])
            npart = 128 if g4 < 4 else 32
            cp('dve', Vh[hs][0:npart, g4 * 4:g4 * 4 + ntl, 0:64],
               PS[pb][0:npart, 0:ntl * 64].rearrange('p (a b) -> p a b', a=ntl), ['ps%d' % pb], [tv])
        for c in range(2):
            pb = nxt('pj', 4)
            mm(PS[pb][:, 0:32], WukT[0:64, h, c * 128:(c + 1) * 128], QT[hs][0:64, 2048:2080], True, True,
               ['WukT', tq], ['ps%d' % pb])
            cp('dve', QA[:, :, c, h * 8:(h + 1) * 8], PS[pb][:, 0:32].rearrange('p (s t) -> p s t', s=4),
               ['ps%d' % pb], ['QA'])
        cp('dve', QAp[64:96, :, h * 8:(h + 1) * 8], QT[hs][64:96, 2048:2080].rearrange('p (s t) -> p s t', s=4), [tq], ['QA'])

    def att_main(h, qts):
        hs = h % 2
        tq, tk, tv = 'QT%d' % hs, 'KT%d' % hs, 'Vh%d' % hs
        for qt in qts:
            ob = 4 + nxt('ob', 2)
            nk = 4 * qt + 4
            pend = []
            for kt in range(nk):
                d = kt - 4 * qt
                c0 = 128 * d if d > 0 else 0
                sb = 6 + nxt('sb', 2)
                pi = nxt('PT', 3)
                qs = qt * 512
                mm(PS[sb][:, c0:512], KT[hs][0:96, kt * 128:(kt + 1) * 128], QT[hs][0:96, qs + c0:qs + 512], True, True,
                   [tk, tq], ['ps%d' % sb])
                act(PT[pi][:, c0:512], PS[sb][:, c0:512], AF.Exp, ['ps%d' % sb], ['PT%d' % pi], scale=ATTN_SCALE)
                if d >= 0:
                    tt(PT[pi][:, c0:c0 + 128], PT[pi][:, c0:c0 + 128], tri, ALU.mult, ['PT%d' % pi, 'tri'], ['PT%d' % pi])

                def pv(kt=kt, c0=c0, pi=pi):
                    mm(PS[ob][0:65, c0:512], Vh[hs][:, kt, :], PT[pi][:, c0:512], kt == 0, kt == nk - 1,
                       [tv, 'PT%d' % pi], ['ps%d' % ob])
                pend.append(pv)
                if len(pend) > 1:
                    pend.pop(0)()
            while pend:
                pend.pop(0)()
            recip(rsb[64:65, :], PS[ob][64:65, :], ['ps%d' % ob], ['rsb'])
            cp('act', osb[0:64, :], PS[ob][0:64, :], ['ps%d' % ob], ['osb'])
            pb = nxt('pj', 4)
            mm(PS[pb][0:64, :], ones_f[64:65, 0:64], rsb[64:65, :], True, True, ['ones_f', 'rsb'], ['ps%d' % pb])
            if h % 2 == 0:
                tt(attn_o[0:64, h // 2, qs:qs + 512], osb[0:64, :], PS[pb][0:64, :], ALU.mult, ['osb', 'ps%d' % pb], ['attn_o'])
            else:
                tt(otmpb[0:64, :], osb[0:64, :], PS[pb][0:64, :], ALU.mult, ['osb', 'ps%d' % pb], ['otmpb'])
                cp('dve', attn_o[64:128, h // 2, qs:qs + 512], otmpb[0:64, :], ['otmpb'], ['attn_o'])

    att_prep(0)
    for h in range(8):
        att_main(h, [0, 1, 2])
        if h + 1 < 8:
            att_prep(h + 1)
        att_main(h, [3])
    pr.barrier()
    AR.off = mSA
    NSL = 4
    ptS = AR.alloc([512], I32); ptF = AR.alloc([512]); pidx = AR.alloc([1]); idxG = AR.alloc([128], I32); idxF = AR.alloc([128])
    dma(ptS, ptab.partition_broadcast(128), [], ['ptS'])
    dma(pidx, k_pidx32, [], ['pidx'])
    cp('dve', ptF, ptS, ['ptS'], ['ptF'])
    for a in range(4):
        rows = slice(32 * a, 32 * a + 32)
        ts(idxF[rows, :], ptF[rows, :].rearrange('p (j a) -> p j a', a=4)[:, :, a], 32.0, pidx[rows, 0:1], ALU.mult, ALU.add,
           ['ptF', 'pidx'], ['idxF'])
    cp('dve', idxG, idxF, ['idxF'], ['idxG'])
    Kl = [AR.alloc([4, 256], BF16) for _ in range(NSL)]; Kp = [AR.alloc([4, 32], BF16) for _ in range(NSL)]
    KTl = [AR.alloc([4, 2, 128], BF16) for _ in range(NSL)]
    KTp = [AR.alloc([4, 128], BF16) for _ in range(NSL)]
    PTs = [AR.alloc([4, 64], BF16) for _ in range(NSL)]
    KnN = AR.alloc([257], BF16); PTn = AR.alloc([64], BF16)
    onb = AR.alloc([256], BF16); olT = AR.alloc([2, 64], BF16); rs1 = AR.alloc([1])
    memset('pool', KnN[0:8, 256:257], 1.0, ['KnN'])
    c_lat_r = c_lat.rearrange('a (r t) d -> (a r) (t d)', t=4); c_pe_r = c_pe.rearrange('a (r t) d -> (a r) (t d)', t=4)

    def page_dma(dst, src_rows, col, w):
        pr.add('pool', lambda e: e.indirect_dma_start(out=dst, out_offset=None, in_=src_rows,
                                                      in_offset=bass.IndirectOffsetOnAxis(ap=idxG[:, col:col + 1], axis=0)),
               ['idxG'], w, dma=True)

    memset('pool', attn_o[:, :, 2048:2080], 0.0, ['attn_o'])
    NSEQ = int(os.environ.get('KSEQ', '4'))

    def st0(s_, g, sl):
        col = s_ * 32 + g
        page_dma(Kl[sl].rearrange('p t d -> p (t d)'), c_lat_r, col, ['Kb%d' % sl])
        page_dma(Kp[sl].rearrange('p t d -> p (t d)'), c_pe_r, col, ['Kb%d' % sl])
        pa = nxt('pj', 4); pbb = nxt('pj', 4)
        psa = PS[pa][:].bitcast(BF16); psp = PS[pbb][:].bitcast(BF16)
        for p in range(4):
            for c in range(2):
                tr(psa[:, (p * 2 + c) * 128:(p * 2 + c + 1) * 128], Kl[sl][:, p, c * 128:(c + 1) * 128], identb,
                   ['Kb%d' % sl, 'identb'], ['ps%d' % pa])
            tr(psp[64:96, p * 128:(p + 1) * 128], Kp[sl][:, p, :], identb, ['Kb%d' % sl, 'identb'], ['ps%d' % pbb])
        cp('act', KTl[sl], psa[:, 0:1024].rearrange('p (a c k) -> p a c k', a=4, c=2), ['ps%d' % pa], ['KTl%d' % sl])
        cp('dve', KTp[sl][64:96, :, :], psp[64:96, 0:512].rearrange('p (a k) -> p a k', a=4), ['ps%d' % pbb], ['KTp%d' % sl])

    def st1(s_, g, sl):
        sb = 6 + nxt('sb', 2)
        for p in range(4):
            o = PS[sb][:, p * 64:(p + 1) * 64]
            mm(o, KTl[sl][:, p, 0, :], QA[:, s_, 0, :], True, False, ['KTl%d' % sl, 'QA'], ['ps%d' % sb])
            mm(o, KTl[sl][:, p, 1, :], QA[:, s_, 1, :], False, False, ['KTl%d' % sl, 'QA'], ['ps%d' % sb])
            mm(o, KTp[sl][64:96, p, :], QAp[64:96, s_, :], False, True, ['KTp%d' % sl, 'QA'], ['ps%d' % sb])
        act(PTs[sl], PS[sb][:, 0:256].rearrange('p (a q) -> p a q', a=4), AF.Exp, ['ps%d' % sb], ['PTs%d' % sl],
            scale=ATTN_SCALE)

    def st2(s_, g, sl):
        for p in range(4):
            mm(PS[OBS[s_ % 2]][0:64, 0:256], PTs[sl][:, p, :], Kl[sl][:, p, :], g == 0 and p == 0, False,
               ['PTs%d' % sl, 'Kb%d' % sl], ['ps%d' % OBS[s_ % 2]])
            mm(PS[OBS[s_ % 2]][0:64, 256:257], PTs[sl][:, p, :], ones_b[:, 0:1], False, False,
               ['PTs%d' % sl, 'ones_b'], ['ps%d' % OBS[s_ % 2]])
        if g == 31:
            fin(s_)

    OBS = [4, 5]
    groups = [(s_, g, i % NSL) for i, (s_, g) in enumerate((s_, g) for s_ in range(NSEQ) for g in range(32))]

    def fin(s_):
        OB = OBS[s_ % 2]
        c0 = 2048 + 8 * s_
        pb = nxt('pj', 4)
        psb = PS[pb][:].bitcast(BF16)
        for c in range(2):
            tr(psb[0:8, c * 128:(c + 1) * 128], ckvn[:, c, c0:c0 + 8], identb, ['ckvn', 'identb'], ['ps%d' % pb])
        cp('dve', KnN[0:8, 0:256], psb[0:8, 0:256], ['ps%d' % pb], ['KnN'])
        sb = 6 + nxt('sb', 2)
        mm(PS[sb][0:8, 0:64], ckvn[:, 0, c0:c0 + 8], QA[:, s_, 0, :], True, False, ['ckvn', 'QA'], ['ps%d' % sb])
        mm(PS[sb][0:8, 0:64], ckvn[:, 1, c0:c0 + 8], QA[:, s_, 1, :], False, False, ['ckvn', 'QA'], ['ps%d' % sb])
        mm(PS[sb][0:8, 0:64], KpeT[64:96, c0:c0 + 8], QAp[64:96, s_, :], False, True, ['KpeT', 'QA'], ['ps%d' % sb])
        act(PTn[0:8, :], PS[sb][0:8, 0:64], AF.Exp, ['ps%d' % sb], ['PTn'], scale=ATTN_SCALE)
        tt(PTn[0:8, :], PTn[0:8, :], mskn[0:8, :], ALU.mult, ['PTn', 'mskn'], ['PTn'])
        mm(PS[OB][0:64, 0:257], PTn[0:8, :], KnN[0:8, 0:257], False, True, ['PTn', 'KnN'], ['ps%d' % OB])
        recip(rs1[0:64, :], PS[OB][0:64, 256:257], ['ps%d' % OB], ['rs1'])
        ts(onb[0:64, :], PS[OB][0:64, 0:256], rs1[0:64, 0:1], None, ALU.mult, None, ['ps%d' % OB, 'rs1'], ['onb'])
        pb = nxt('pj', 4)
        psb = PS[pb][:].bitcast(BF16)
        for c in range(2):
            tr(psb[:, c * 64:(c + 1) * 64], onb[0:64, c * 128:(c + 1) * 128], identb[0:64, 0:64], ['onb', 'identb'], ['ps%d' % pb])
        cp('dve', olT, psb[:, 0:128].rearrange('p (c q) -> p c q', c=2), ['ps%d' % pb], ['olT'])
        pb = nxt('pj', 4)
        for h in range(8):
            for c in range(2):
                mm(PS[pb][(h % 2) * 64:(h % 2) * 64 + 64, (h // 2) * 8:(h // 2) * 8 + 8], Wuv[:, c, h * 64:(h + 1) * 64],
                   olT[:, c, h * 8:(h + 1) * 8], c == 0, c == 1, ['Wuv', 'olT'], ['ps%d' % pb])
        cp('dve', attn_o[:, :, c0:c0 + 8], PS[pb][:, 0:32].rearrange('p (a t) -> p a t', a=4), ['ps%d' % pb], ['attn_o'])

    for i in range(len(groups) + 2):
        if i < len(groups):
            st0(*groups[i])
        if 0 <= i - 1 < len(groups):
            st1(*groups[i - 1])
        if 0 <= i - 2 < len(groups):
            st2(*groups[i - 2])
    if debug:
        dbg_attn = dout('dbg_attn', [128, 4, NTOK], BF16)
        dma_out(dbg_attn, attn_o, ['attn_o'])
    pr.barrier()
    AR.off = mATT

    class StopS5(Exception):
        pass

    try:
        LVL = int(os.environ.get('KLVL', '99'))
        ssm_o = AR.alloc([4, NTOK], BF16)
        mSSM = AR.off
        BreT = AR.alloc([16, 128], BF16); BimT = AR.alloc([16, 128], BF16)
        CreT = AR.alloc([16, 128], BF16); NCreT = AR.alloc([16, 128], BF16); NCimT = AR.alloc([16, 128], BF16)
        prm = AR.alloc([24, 16])
        H0re = AR.alloc([4, 16]); H0im = AR.alloc([4, 16]); HSre = AR.alloc([16]); HSim = AR.alloc([16])
        HSsre = AR.alloc([4, 16]); HSsim = AR.alloc([4, 16]); smT = AR.alloc([32]); rhoS = AR.alloc([32])
        cry = AR.alloc([4]); t4a = AR.alloc([4]); t4b = AR.alloc([4]); ah_re = AR.alloc([4]); ah_im = AR.alloc([4]); c1t = AR.alloc([4])
        PN = ['are', 'aim', 'ldt', 'dt', 'lam', 'th', 'rho', 'sth', 'cth', 'abr', 'abi', 'den', 'rden', 'nr', 'fr', 'fi',
              'u1', 'u2', 'u3', 'u4', 'a512', 's512', 'c512']
        P_ = {nm: prm[:, i, :] for i, nm in enumerate(PN)}
        mS5 = AR.off
        INV2PI = 1.0 / (2 * math.pi); MAGIC = 12582912.0; C1 = 6.28125; C2 = 2 * math.pi - 6.28125

        def sincos(s_out, c_out, ang, ta, tb, r, w_s, w_c, tok):
            ts(ta, ang, INV2PI, None, ALU.mult, None, r, [tok + 'a'])
            ts(ta, ta, MAGIC, None, ALU.add, None, [tok + 'a'], [tok + 'a'])
            ts(ta, ta, -MAGIC, None, ALU.add, None, [tok + 'a'], [tok + 'a'])
            stt(tb, ta, -C1, ang, ALU.mult, ALU.add, list(r) + [tok + 'a'], [tok + 'b'])
            stt(tb, ta, -C2, tb, ALU.mult, ALU.add, [tok + 'a', tok + 'b'], [tok + 'b'])
            ts(tb, tb, math.pi, -math.pi, ALU.min, ALU.max, [tok + 'b'], [tok + 'b'])
            act(s_out, tb, AF.Sin, [tok + 'b'], w_s)
            stt(ta, tb, -1.0, tb, ALU.mult, ALU.max, [tok + 'b'], [tok + 'a'])
            act(c_out, ta, AF.Sin, [tok + 'a', 'hpiT'], w_c, scale=-1.0, bias=hpiT)

        araw = AR.alloc([3, 128]); ldr = AR.alloc([2])
        dma(araw[0:16, 0, :], a_re, [], ['araw']); dma(araw[0:16, 1, :], a_im, [], ['araw'])
        dma(ldr[0:16, :], log_dt, [], ['ldr'])
        cp('dve', araw[0:16, 2, :].rearrange('p (g n) -> p g n', g=2), ldr[0:16, :].unsqueeze(2).to_broadcast([16, 2, 64]),
           ['ldr', 'araw'], ['araw'])
        pb = nxt('pj', 4)
        for i in range(3):
            tr(PS[pb][:, i * 16:(i + 1) * 16], araw[0:16, i, :], ident[0:16, 0:16], ['araw', 'ident'], ['ps%d' % pb])
        cp('dve', prm[:, 0:3, :], PS[pb][:, 0:48].rearrange('p (a b) -> p a b', a=3), ['ps%d' % pb], ['prm'])
        TP = ['prm']
        act(P_['dt'], P_['ldt'], AF.Exp, TP, TP)
        tt(P_['lam'], P_['dt'], P_['are'], ALU.mult, TP, TP)
        tt(P_['th'], P_['dt'], P_['aim'], ALU.mult, TP, TP)
        act(P_['rho'], P_['lam'], AF.Exp, TP, TP)
        sincos(P_['sth'], P_['cth'], P_['th'], P_['u1'], P_['u2'], TP, TP, TP, 'prm')
        ts(P_['a512'], P_['th'], 512.0, None, ALU.mult, None, TP, TP)
        sincos(P_['s512'], P_['c512'], P_['a512'], P_['u3'], P_['u4'], TP, TP, TP, 'prm')
        tt(P_['abr'], P_['rho'], P_['cth'], ALU.mult, TP, TP)
        tt(P_['abi'], P_['rho'], P_['sth'], ALU.mult, TP, TP)
        tt(P_['u1'], P_['are'], P_['are'], ALU.mult, TP, TP)
        tt(P_['u2'], P_['aim'], P_['aim'], ALU.mult, TP, TP)
        tt(P_['den'], P_['u1'], P_['u2'], ALU.add, TP, TP)
        recip(P_['rden'], P_['den'], TP, TP)
        ts(P_['nr'], P_['abr'], -1.0, None, ALU.add, None, TP, TP)
        tt(P_['u1'], P_['nr'], P_['are'], ALU.mult, TP, TP)
        tt(P_['u2'], P_['abi'], P_['aim'], ALU.mult, TP, TP)
        tt(P_['u1'], P_['u1'], P_['u2'], ALU.add, TP, TP)
        tt(P_['fr'], P_['u1'], P_['rden'], ALU.mult, TP, TP)
        tt(P_['u1'], P_['abi'], P_['are'], ALU.mult, TP, TP)
        tt(P_['u2'], P_['nr'], P_['aim'], ALU.mult, TP, TP)
        tt(P_['u1'], P_['u1'], P_['u2'], ALU.subtract, TP, TP)
        tt(P_['fi'], P_['u1'], P_['rden'], ALU.mult, TP, TP)
        if LVL == 0:
            raise StopS5()
        sraw = AR.alloc([2, 128])
        dma(sraw[0:64, 0, :], st_re, [], ['sraw']); dma(sraw[0:64, 1, :], st_im, [], ['sraw'])
        pb = nxt('pj', 4)
        tr(PS[pb][:, 0:64], sraw[0:64, 0, :], ident[0:64, 0:64], ['sraw', 'ident'], ['ps%d' % pb])
        tr(PS[pb][:, 64:128], sraw[0:64, 1, :], ident[0:64, 0:64], ['sraw', 'ident'], ['ps%d' % pb])
        cp('dve', H0re, PS[pb][:, 0:64].rearrange('p (s r) -> p s r', s=4), ['ps%d' % pb], ['H0'])
        cp('dve', H0im, PS[pb][:, 64:128].rearrange('p (s r) -> p s r', s=4), ['ps%d' % pb], ['H0'])
        dma(smT, k_smask.partition_broadcast(128), [], ['smT'])
        if LVL == -1:
            raise StopS5()
        Braw_re = AR.alloc([16, 16]); Braw_im = AR.alloc([16, 16]); t16a = AR.alloc([16]); t16b = AR.alloc([16])
        Bexp_re = AR.alloc([16, 128]); Bexp_im = AR.alloc([16, 128])
        dma(Braw_re, b_re.rearrange('(r q) c -> q r c', q=128), [], ['Braw'])
        dma(Braw_im, b_im.rearrange('(r q) c -> q r c', q=128), [], ['Braw'])
        memset('pool', Bexp_re, 0.0, ['Bexp']); memset('pool', Bexp_im, 0.0, ['Bexp'])
        for pr_ in range(16 if LVL >= 2 else 0):
            for gi in range(2):
                rows = slice(64 * gi, 64 * gi + 64)
                col0 = (pr_ % 4) * 32 + gi * 16
                fr_, fi_ = P_['fr'][rows, pr_:pr_ + 1], P_['fi'][rows, pr_:pr_ + 1]
                ts(t16a[rows, :], Braw_im[rows, pr_, :], fi_, None, ALU.mult, None, ['Braw', 'prm'], ['t16a'])
                stt(Bexp_re[rows, pr_, col0:col0 + 16], Braw_re[rows, pr_, :], fr_, t16a[rows, :], ALU.mult, ALU.subtract,
                    ['Braw', 'prm', 't16a'], ['Bexp'])
                ts(t16b[rows, :], Braw_re[rows, pr_, :], fi_, None, ALU.mult, None, ['Braw', 'prm'], ['t16b'])
                stt(Bexp_im[rows, pr_, col0:col0 + 16], Braw_im[rows, pr_, :], fr_, t16b[rows, :], ALU.mult, ALU.add,
                    ['Braw', 'prm', 't16b'], ['Bexp'])
        for src_, dst_, tok in ((Bexp_re, BreT, 'BreT'), (Bexp_im, BimT, 'BimT')):
            for q4 in range(4):
                pb = nxt('pj', 4)
                for j in range(4):
                    tr(PS[pb][:, j * 128:(j + 1) * 128], src_[:, q4 * 4 + j, :], ident, ['Bexp', 'ident'], ['ps%d' % pb])
                cp('act', dst_[:, q4 * 4:q4 * 4 + 4, :], PS[pb][:].rearrange('p (a b) -> p a b', a=4), ['ps%d' % pb], [tok])
        pr.barrier()
        AR.off = mS5
        if LVL == -2:
            raise StopS5()
        X_re = AR.alloc([4, 512]); X_im = AR.alloc([4, 512])
        memset('pool', X_re, 0.0, ['X_re']); memset('pool', X_im, 0.0, ['X_im'])
        for g_ in range(32 if LVL >= 3 else 0):
            r0 = 16 * (g_ % 8)
            cc0 = ((g_ % 8) // 2) * 128 + (g_ % 2) * 64
            dma(X_re[r0:r0 + 16, g_ // 8, cc0:cc0 + 64], cc_re[g_], [], ['X_re'])
            dma(X_im[r0:r0 + 16, g_ // 8, cc0:cc0 + 64], cc_im[g_], [], ['X_im'])
        for q4 in range(4):
            pb = nxt('pj', 4)
            for j in range(4):
                tr(PS[pb][:, j * 128:(j + 1) * 128], X_re[:, q4, j * 128:(j + 1) * 128], ident, ['X_re', 'ident'], ['ps%d' % pb])
            v = PS[pb][:].rearrange('p (a b) -> p a b', a=4)
            cp('act', CreT[:, q4 * 4:q4 * 4 + 4, :], v, ['ps%d' % pb], ['CreT'])
            act(NCreT[:, q4 * 4:q4 * 4 + 4, :], v, AF.Copy, ['ps%d' % pb], ['NCreT'], scale=-1.0)
            pb = nxt('pj', 4)
            for j in range(4):
                tr(PS[pb][:, j * 128:(j + 1) * 128], X_im[:, q4, j * 128:(j + 1) * 128], ident, ['X_im', 'ident'], ['ps%d' % pb])
            act(NCimT[:, q4 * 4:q4 * 4 + 4, :], PS[pb][:].rearrange('p (a b) -> p a b', a=4), AF.Copy, ['ps%d' % pb], ['NCimT'],
                scale=-1.0)
        pr.barrier()
        AR.off = mS5
        if LVL == -3:
            raise StopS5()
        cosT2 = [AR.alloc([512]) for _ in range(2)]; sinT2 = [AR.alloc([512]) for _ in range(2)]; iotaT = AR.alloc([512])
        tg_ang = AR.alloc([512]); tg_a = AR.alloc([512]); tg_b = AR.alloc([512])
        t_ang = AR.alloc([1024]); t_a = AR.alloc([1024]); t_b = AR.alloc([1024])
        Sre = [AR.alloc([512]) for _ in range(2)]; Sim = [AR.alloc([512]) for _ in range(2)]
        Zb = AR.alloc([4, 512], BF16)
        dma(iotaT, k_iota[0:1, 0:512].partition_broadcast(128), [], ['iotaT'])
        m1, m2, g_re, g_im = t_a[:, 0:512], t_a[:, 512:1024], t_b[:, 0:512], t_b[:, 512:1024]
        m3, m4 = t_ang[:, 0:512], t_ang[:, 512:1024]
        G2 = [t_b, AR.alloc([1024])]
        YB = [0, 1, 2, 3, 4]

        def gen_tables(p_):
            ts(tg_ang, iotaT, P_['th'][:, p_:p_ + 1], None, ALU.mult, None, ['iotaT', 'prm'], ['tg_ang'])
            sincos(sinT2[p_ % 2], cosT2[p_ % 2], tg_ang, tg_a, tg_b, ['tg_ang'], ['sinT%d' % (p_ % 2)], ['cosT%d' % (p_ % 2)], 'tg_')
        for pr_ in range({4: 1, 5: 4}.get(LVL, 16) if LVL >= 4 else 0):
            qc = pr_ // 4
            thp = P_['th'][:, pr_:pr_ + 1]
            rho_p = P_['rho'][:, pr_:pr_ + 1]
            if pr_ == 0:
                gen_tables(0)
            if pr_ + 1 < 16:
                gen_tables(pr_ + 1)
            cosT, sinT = cosT2[pr_ % 2], sinT2[pr_ % 2]
            tcs, tsn = 'cosT%d' % (pr_ % 2), 'sinT%d' % (pr_ % 2)
            ts(rhoS, smT, rho_p, None, ALU.mult, None, ['smT', 'prm'], ['rhoS'])
            state = {'prev': None}

            def s5pre(bi):
                off, n = BLOCKS[bi]
                g_re, g_im = G2[bi % 2][:, 0:512], G2[bi % 2][:, 512:1024]
                tgb = 't_b%d' % (bi % 2)
                mm(PS[5][:, 0:n], BreT[:, pr_, :], uT[:, qc, off:off + n], True, True, ['BreT', 'uT'], ['ps5'])
                mm(PS[6][:, 0:n], BimT[:, pr_, :], uT[:, qc, off:off + n], True, True, ['BimT', 'uT'], ['ps6'])
                if bi < 4:
                    cs, sn = cosT[:, 0:n], sinT[:, 0:n]
                    vw = lambda a: a
                else:
                    cs = cosT[:, 0:8].unsqueeze(1).to_broadcast([128, 4, 8])
                    sn = sinT[:, 0:8].unsqueeze(1).to_broadcast([128, 4, 8])
                    vw = lambda a: a.rearrange('p (s t) -> p s t', s=4)
                TB = [tcs, tsn]
                PE_ = 'pool' if (bi < 4 and os.environ.get('KS5POOL', '0') == '1') else 'dve'
                tt(vw(m1[:, 0:n]), vw(PS[5][:, 0:n]), cs, ALU.mult, ['ps5'] + TB, ['t_a'])
                tt(vw(m2[:, 0:n]), vw(PS[6][:, 0:n]), sn, ALU.mult, ['ps6'] + TB, ['t_a'])
                tt(g_re[:, 0:n], m1[:, 0:n], m2[:, 0:n], ALU.add, ['t_a'], [tgb], eng=PE_)
                tt(vw(m3[:, 0:n]), vw(PS[6][:, 0:n]), cs, ALU.mult, ['ps6'] + TB, ['t_ang'])
                tt(vw(m4[:, 0:n]), vw(PS[5][:, 0:n]), sn, ALU.mult, ['ps5'] + TB, ['t_ang'])
                tt(g_im[:, 0:n], m3[:, 0:n], m4[:, 0:n], ALU.subtract, ['t_ang'], [tgb], eng=PE_)

            def s5post(bi):
                off, n = BLOCKS[bi]
                g_re, g_im = G2[bi % 2][:, 0:512], G2[bi % 2][:, 512:1024]
                tgb = 't_b%d' % (bi % 2)
                if bi < 4:
                    cs, sn = cosT[:, 0:n], sinT[:, 0:n]
                    vw = lambda a: a
                else:
                    cs = cosT[:, 0:8].unsqueeze(1).to_broadcast([128, 4, 8])
                    sn = sinT[:, 0:8].unsqueeze(1).to_broadcast([128, 4, 8])
                    vw = lambda a: a.rearrange('p (s t) -> p s t', s=4)
                TB = [tcs, tsn]
                PE_ = 'dve'
                prev = state['prev']
                sl = bi % 2
                if bi < 4:
                    d0 = rho_p.to_broadcast([128, n])
                    if prev is None:
                        i_re = i_im = 0.0
                        rr = [tgb, 'prm']
                    else:
                        c5, s5 = P_['c512'][:, pr_:pr_ + 1], P_['s512'][:, pr_:pr_ + 1]
                        pr_l, pi_l = Sre[prev][:, 511:512], Sim[prev][:, 511:512]
                        ts(cry[:, 2:3], pi_l, s5, None, ALU.mult, None, ['S%d' % prev, 'prm'], ['cry'])
                        stt(cry[:, 0:1], pr_l, c5, cry[:, 2:3], ALU.mult, ALU.subtract, ['S%d' % prev, 'prm', 'cry'], ['cry'])
                        ts(cry[:, 3:4], pr_l, s5, None, ALU.mult, None, ['S%d' % prev, 'prm'], ['cry'])
                        stt(cry[:, 1:2], pi_l, c5, cry[:, 3:4], ALU.mult, ALU.add, ['S%d' % prev, 'prm', 'cry'], ['cry'])
                        i_re, i_im = cry[:, 0:1], cry[:, 1:2]
                        rr = [tgb, 'prm', 'cry']
                else:
                    ts(t4a, H0im[:, :, pr_], P_['abi'][:, pr_:pr_ + 1], None, ALU.mult, None, ['H0', 'prm'], ['t4a'])
                    stt(ah_re, H0re[:, :, pr_], P_['abr'][:, pr_:pr_ + 1], t4a, ALU.mult, ALU.subtract, ['H0', 'prm', 't4a'], ['ah'])
                    ts(t4b, H0re[:, :, pr_], P_['abi'][:, pr_:pr_ + 1], None, ALU.mult, None, ['H0', 'prm'], ['t4b'])
                    stt(ah_im, H0im[:, :, pr_], P_['abr'][:, pr_:pr_ + 1], t4b, ALU.mult, ALU.add, ['H0', 'prm', 't4b'], ['ah'])
                    gv_re = g_re[:, 0:32].rearrange('p (s t) -> p s t', s=4)[:, :, 0]
                    gv_im = g_im[:, 0:32].rearrange('p (s t) -> p s t', s=4)[:, :, 0]
                    tt(gv_re, gv_re, ah_re, ALU.add, [tgb, 'ah'], [tgb])
                    tt(gv_im, gv_im, ah_im, ALU.add, [tgb, 'ah'], [tgb])
                    d0 = rhoS[:, 0:32]
                    i_re = i_im = 0.0
                    rr = [tgb, 'rhoS']
                scan(Sre[sl][:, 0:n], d0, g_re[:, 0:n], i_re, rr, ['S%d' % sl])
                scan(Sim[sl][:, 0:n], d0, g_im[:, 0:n], i_im, rr, ['S%d' % sl])
                TS = ['S%d' % sl] + TB
                tt(vw(Zb[:, 0, 0:n]), vw(Sre[sl][:, 0:n]), cs, ALU.mult, TS, ['Zb'], eng=PE_)
                tt(vw(Zb[:, 1, 0:n]), vw(Sim[sl][:, 0:n]), sn, ALU.mult, TS, ['Zb'], eng=PE_)
                tt(vw(Zb[:, 2, 0:n]), vw(Sim[sl][:, 0:n]), cs, ALU.mult, TS, ['Zb'], eng=PE_)
                tt(vw(Zb[:, 3, 0:n]), vw(Sre[sl][:, 0:n]), sn, ALU.mult, TS, ['Zb'], eng=PE_)
                for k, (W, wt_) in enumerate(((CreT, 'CreT'), (NCreT, 'NCreT'), (NCimT, 'NCimT'), (NCimT, 'NCimT'))):
                    mm(PS[YB[bi]][:, 0:n], W[:, pr_, :], Zb[:, k, 0:n], pr_ % 4 == 0 and k == 0, pr_ % 4 == 3 and k == 3,
                       [wt_, 'Zb'], ['ps%d' % YB[bi]])
                if bi == 3:
                    cl, sl_ = cosT[:, 511:512], sinT[:, 511:512]
                    sr, si = Sre[sl][:, 511:512], Sim[sl][:, 511:512]
                    tt(c1t[:, 0:1], cl, sr, ALU.mult, TS, ['c1t']); tt(c1t[:, 1:2], sl_, si, ALU.mult, TS, ['c1t'])
                    tt(HSre[:, pr_:pr_ + 1], c1t[:, 0:1], c1t[:, 1:2], ALU.subtract, ['c1t'], ['HS'])
                    tt(c1t[:, 2:3], cl, si, ALU.mult, TS, ['c1t']); tt(c1t[:, 3:4], sl_, sr, ALU.mult, TS, ['c1t'])
                    tt(HSim[:, pr_:pr_ + 1], c1t[:, 2:3], c1t[:, 3:4], ALU.add, ['c1t'], ['HS'])
                if bi == 4:
                    sr = Sre[sl][:, 0:32].rearrange('p (s t) -> p s t', s=4)[:, :, 7]
                    si = Sim[sl][:, 0:32].rearrange('p (s t) -> p s t', s=4)[:, :, 7]
                    c7, s7 = cosT[:, 7:8], sinT[:, 7:8]
                    ts(t4a, si, s7, None, ALU.mult, None, TS, ['t4a'])
                    stt(HSsre[:, :, pr_], sr, c7, t4a, ALU.mult, ALU.subtract, TS + ['t4a'], ['HSs'])
                    ts(t4b, sr, s7, None, ALU.mult, None, TS, ['t4b'])
                    stt(HSsim[:, :, pr_], si, c7, t4b, ALU.mult, ALU.add, TS + ['t4b'], ['HSs'])
                state['prev'] = sl if bi < 4 else None

            s5pre(0)
            for bi in range(5):
                if bi + 1 < 5:
                    s5pre(bi + 1)
                s5post(bi)
            if pr_ % 4 == 3:
                for bi, (off, n) in enumerate(BLOCKS):
                    yf, x2 = t_ang[:, 0:n], t_ang[:, 512:512 + n]
                    stt(yf, uT[:, qc, off:off + n], G[:, GO['d'] + qc:GO['d'] + qc + 1], PS[YB[bi]][:, 0:n], ALU.mult, ALU.add,
                        ['uT', 'G', 'ps%d' % YB[bi]], ['t_ang'])
                    act(x2, yf, AF.Square, ['t_ang'], ['t_ang'])
                    ts(x2, x2, 0.044715, 1.0, ALU.mult, ALU.add, ['t_ang'], ['t_ang'])
                    tt(x2, x2, yf, ALU.mult, ['t_ang'], ['t_ang'])
                    act(x2, x2, AF.Sigmoid, ['t_ang'], ['t_ang'], scale=2.0 * math.sqrt(2.0 / math.pi))
                    tt(ssm_o[:, qc, off:off + n], yf, x2, ALU.mult, ['t_ang'], ['ssm_o'])
        if LVL == -4:
            raise StopS5()
        hso = t_ang[:, 0:512].rearrange('p (a b) -> p a b', a=4)
        pb = nxt('pj', 4)
        tr(PS[pb][0:16, 0:128], HSre, ident, ['HS', 'ident'], ['ps%d' % pb])
        tr(PS[pb][0:16, 128:256], HSim, ident, ['HS', 'ident'], ['ps%d' % pb])
        tr(PS[pb][0:64, 256:384], HSsre[:].rearrange('p s r -> p (s r)'), ident, ['HSs', 'ident'], ['ps%d' % pb])
        tr(PS[pb][0:64, 384:512], HSsim[:].rearrange('p s r -> p (s r)'), ident, ['HSs', 'ident'], ['ps%d' % pb])
        cp('act', hso[0:64, :, :], PS[pb][0:64, :].rearrange('p (a b) -> p a b', a=4), ['ps%d' % pb], ['t_ang'])
        dma_out(o_sre_p, hso[0:16, 0, :], ['t_ang']); dma_out(o_sim_p, hso[0:16, 1, :], ['t_ang'])
        dma_out(o_sre_s, hso[0:64, 2, :], ['t_ang']); dma_out(o_sim_s, hso[0:64, 3, :], ['t_ang'])
        pr.barrier()
        AR.off = mS5
        if LVL == -5:
            raise StopS5()
        Wglu = AR.alloc([4, 512], BF16); gate = AR.alloc([4, 512], BF16)
        cast_load(Wglu, w_glu.rearrange('(c p) n -> p c n', p=128), 'Wglu')
        for bi, (off, n) in enumerate(BLOCKS):
            for oc in range(4):
                pb = nxt('pj', 4)
                for c in range(4):
                    mm(PS[pb][:, 0:n], Wglu[:, c, oc * 128:(oc + 1) * 128], ssm_o[:, c, off:off + n], c == 0, c == 3,
                       ['Wglu', 'ssm_o'], ['ps%d' % pb])
                act(gate[:, oc, 0:n], PS[pb][:, 0:n], AF.Sigmoid, ['ps%d' % pb], ['gate'])
            for oc in range(4):
                tt(ssm_o[:, oc, off:off + n], ssm_o[:, oc, off:off + n], gate[:, oc, 0:n], ALU.mult, ['ssm_o', 'gate'], ['ssm_o'])
        if debug:
            dbg_ssm = dout('dbg_ssm', [128, 4, NTOK], BF16)
            dma_out(dbg_ssm, ssm_o, ['ssm_o'])
        pr.barrier()
        AR.off = mS5


    except StopS5:
        pr.barrier()
        AR.off = mS5

    mM0 = AR.off
    if '2' in KOLD:
        KmT = AR.alloc([4, 256], BF16); Vm = AR.alloc([2, 512], BF16)
    memtm = AR.alloc([2, 1024]); memT = AR.alloc([8, 256]); mnT = AR.alloc([8, 256], BF16)
    Wkm = AR.alloc([8, 512], BF16); Wvm = AR.alloc([8, 512], BF16); mko = AR.alloc([2, 512])
    cast_load(Wkm, w_km.rearrange('(c p) n -> p c n', p=128), 'Wkm')
    cast_load(Wvm, w_vm.rearrange('(c p) n -> p c n', p=128), 'Wvm')
    for t in range(2):
        dma(memtm[:, t, :], mem_p[t * 128:(t + 1) * 128, :], [], ['memtm%d' % t])
        for half in range(2):
            pb = nxt('pj', 4)
            for c4 in range(4):
                c = half * 4 + c4
                tr(PS[pb][:, c4 * 128:(c4 + 1) * 128], memtm[:, t, c * 128:(c + 1) * 128], ident, ['memtm%d' % t, 'ident'],
                   ['ps%d' % pb])
            cp('act' if half else 'dve', memT[:, half * 4:half * 4 + 4, t * 128:(t + 1) * 128],
               PS[pb][:].rearrange('p (a b) -> p a b', a=4), ['ps%d' % pb], ['memT'])
    NORM(memT, 8, 256, 'mem', 1024, mnT, ['memT'], ['mnT'])
    for t in range(2):
        for wi, (W, wtok, dst) in enumerate(((Wkm, 'Wkm', o_mk_p), (Wvm, 'Wvm', o_mv_p))):
            pb = nxt('pj', 4)
            sl = nxt('mko', 2)
            for k in range(8):
                mm(PS[pb][:, :], mnT[:, k, t * 128:(t + 1) * 128], W[:, k, :], k == 0, k == 7, ['mnT', wtok], ['ps%d' % pb])
            cp('act', mko[:, sl, :], PS[pb][:, :], ['ps%d' % pb], ['mko%d' % sl])
            if wi == 1 and os.environ.get('KM0', '1') == '1':
                cp('dve', Vm[:, t, :], mko[:, sl, :], ['mko%d' % sl], ['Vm'])
            dma_out(dst[t * 128:(t + 1) * 128, :], mko[:, sl, :], ['mko%d' % sl])
    for hd in range(4 if os.environ.get('KM0', '1') == '1' else 0):
        pb = nxt('pj', 4)
        for k in range(8):
            mm(PS[pb][:, 0:256], Wkm[:, k, hd * 128:(hd + 1) * 128], mnT[:, k, :], k == 0, k == 7, ['Wkm', 'mnT'], ['ps%d' % pb])
        cp('act', KmT[:, hd, :], PS[pb][:, 0:256], ['ps%d' % pb], ['KmT'])
    pr.barrier()
    AR.off = mM0

    class StopX(Exception):
        pass

    KCUT = int(os.environ.get('KCUT', '99'))
    try:
        AR.off = mSSM
        if KCUT == 0:
            raise StopX()
        Wout = AR.alloc([8, 1024], BF16)
        mixin2 = [AR.alloc([8, 512], BF16) for _ in range(2)]; f_sb2 = [AR.alloc([8, 512])] * 2
        NSB = norm_set()
        cast_load(Wout, w_out.rearrange('(c p) n -> p c n', p=128), 'Wout')
        SKIP = os.environ.get('KSKIP', '')

        def mixA(bi):
            off, n = BLOCKS[bi]
            NORM(attn_o[:, :, off:off + n], 4, n, 'attn', 512, mixin2[bi % 2][:, 0:4, 0:n], ['attn_o'], ['mixin%d' % (bi % 2)])
            NORM(ssm_o[:, :, off:off + n], 4, n, 'ssm', 512, mixin2[bi % 2][:, 4:8, 0:n], ['ssm_o'], ['mixin%d' % (bi % 2)])

        def mixB(bi):
            off, n = BLOCKS[bi]
            for oc in range(8):
                pb = nxt('pj', 4)
                for k in range(8):
                    mm(PS[pb][:, 0:n], Wout[:, k, oc * 128:(oc + 1) * 128], mixin2[bi % 2][:, k, 0:n], k == 0, k == 7,
                       ['Wout', 'mixin%d' % (bi % 2)], ['ps%d' % pb])
                cp('act' if oc % 2 else 'dve', f_sb2[bi % 2][:, oc, 0:n], PS[pb][:, 0:n], ['ps%d' % pb], ['f_sbm'])

        def mixC(bi):
            off, n = BLOCKS[bi]
            NORM(f_sb2[bi % 2][:, :, 0:n], 8, n, 'mix_post', 1024, None, ['f_sbm'], [t_xT[bi]],
                 resid=xT[:, :, off:off + n], ns=NSB)

        if 'x' not in SKIP:
            mixA(0)
            for bi in range(5):
                if bi + 1 < 5:
                    mixA(bi + 1)
                mixB(bi)
                mixC(bi)
        pr.barrier()
        AR.off = mUT

        if KCUT == 1:
            raise StopX()
        Wqm = AR.alloc([8, 512], BF16); Wom = AR.alloc([4, 1024], BF16)
        KmTs = AR.alloc([4, 4, 256], BF16); Vms = AR.alloc([4, 2, 512], BF16)
        mkr = AR.alloc([2, 2, 512])
        hT2 = AR.alloc([8, 512], BF16); qmT = AR.alloc([4, 512], BF16); omT = AR.alloc([4, 512], BF16)
        osm = AR.alloc([512]); rsm = AR.alloc([512]); f_sb = AR.alloc([8, 512])
        cast_load(Wqm, w_qm.rearrange('(c p) n -> p c n', p=128), 'Wqm')
        cast_load(Wom, w_om.rearrange('(c p) n -> p c n', p=128), 'Wom')
        for s_ in range(4):
            sl = s_ % 2
            dma(mkr[:, sl, :, :], memk[s_].rearrange('(t p) d -> p t d', p=128), [], ['mkr%d' % sl])
            cast_load(Vms[:, s_, :, :], memv[s_].rearrange('(t p) d -> p t d', p=128), 'Vms')
            for t in range(2):
                pb = nxt('pj', 4)
                for hd in range(4):
                    tr(PS[pb][:, hd * 128:(hd + 1) * 128], mkr[:, sl, t, hd * 128:(hd + 1) * 128], ident, ['mkr%d' % sl, 'ident'],
                       ['ps%d' % pb])
                cp('act' if t else 'dve', KmTs[:, s_, :, t * 128:(t + 1) * 128], PS[pb][:].rearrange('p (a b) -> p a b', a=4),
                   ['ps%d' % pb], ['KmTs'])
        if KCUT == 2:
            raise StopX()
        hT2b = [hT2, AR.alloc([8, 512], BF16)]; qmTb = [qmT, AR.alloc([4, 512], BF16)]
        PTm = [AR.alloc([2, 512], BF16) for _ in range(2)]
        NSB2 = norm_set()

        def memA(bi):
            off, n = BLOCKS[bi]
            h2, q2 = hT2b[bi % 2], qmTb[bi % 2]
            NORM(xT[:, :, off:off + n], 8, n, 'mem_pre', 1024, h2[:, :, 0:n], [t_xT[bi]], ['hT2%d' % (bi % 2)])
            for hd in range(4):
                pb = nxt('pj', 4)
                for k in range(8):
                    mm(PS[pb][:, 0:n], Wqm[:, k, hd * 128:(hd + 1) * 128], h2[:, k, 0:n], k == 0, k == 7,
                       ['Wqm', 'hT2%d' % (bi % 2)], ['ps%d' % pb])
                cp('act', q2[:, hd, 0:n], PS[pb][:, 0:n], ['ps%d' % pb], ['qmT%d' % (bi % 2)])

        def memS(bi, hd):
            off, n = BLOCKS[bi]
            q2, tq2 = qmTb[bi % 2], 'qmT%d' % (bi % 2)
            pi = hd % 2
            if bi < 4:
                for t in range(2):
                    sb = 6 + t
                    mm(PS[sb][:, 0:n], KmT[:, hd, t * 128:(t + 1) * 128], q2[:, hd, 0:n], True, True, ['KmT', tq2], ['ps%d' % sb])
                    act(PTm[pi][:, t, 0:n], PS[sb][:, 0:n], AF.Exp, ['ps%d' % sb], ['PTm%d' % pi], scale=MEM_SCALE)
            else:
                sb = 6 + hd % 2
                for s_ in range(4):
                    for t in range(2):
                        c_ = (s_ * 2 + t) * 8
                        mm(PS[sb][:, c_:c_ + 8], KmTs[:, s_, hd, t * 128:(t + 1) * 128], q2[:, hd, 8 * s_:8 * s_ + 8], True, True,
                           ['KmTs', tq2], ['ps%d' % sb])
                act(PTm[pi][:, 0, 0:64], PS[sb][:, 0:64], AF.Exp, ['ps%d' % sb], ['PTm%d' % pi], scale=MEM_SCALE)

        def memO(bi, hd):
            off, n = BLOCKS[bi]
            pi = hd % 2
            bo, bs = (4, 5)
            if bi < 4:
                for t in range(2):
                    mm(PS[bo][:, 0:n], Vm[:, t, hd * 128:(hd + 1) * 128], PTm[pi][:, t, 0:n], t == 0, t == 1, ['Vm', 'PTm%d' % pi],
                       ['ps%d' % bo])
                    mm(PS[bs][:, 0:n], ones_b, PTm[pi][:, t, 0:n], t == 0, t == 1, ['ones_b', 'PTm%d' % pi], ['ps%d' % bs])
            else:
                for s_ in range(4):
                    for t in range(2):
                        c_ = (s_ * 2 + t) * 8
                        mm(PS[bo][:, 8 * s_:8 * s_ + 8], Vms[:, s_, t, hd * 128:(hd + 1) * 128], PTm[pi][:, 0, c_:c_ + 8],
                           t == 0, t == 1, ['Vms', 'PTm%d' % pi], ['ps%d' % bo])
                        mm(PS[bs][:, 8 * s_:8 * s_ + 8], ones_b, PTm[pi][:, 0, c_:c_ + 8], t == 0, t == 1,
                           ['ones_b', 'PTm%d' % pi], ['ps%d' % bs])
            recip(rsm[:, 0:n], PS[bs][:, 0:n], ['ps%d' % bs], ['rsm'])
            cp('act', osm[:, 0:n], PS[bo][:, 0:n], ['ps%d' % bo], ['osm'])
            tt(omT[:, hd, 0:n], osm[:, 0:n], rsm[:, 0:n], ALU.mult, ['osm', 'rsm'], ['omT'])

        def memB(bi):
            off, n = BLOCKS[bi]
            memS(bi, 0)
            for hd in range(4):
                if hd + 1 < 4:
                    memS(bi, hd + 1)
                memO(bi, hd)
            for oc in range(8):
                pb = nxt('pj', 4)
                for k in range(4):
                    mm(PS[pb][:, 0:n], Wom[:, k, oc * 128:(oc + 1) * 128], omT[:, k, 0:n], k == 0, k == 3, ['Wom', 'omT'],
                       ['ps%d' % pb])
                cp('act' if oc % 2 else 'dve', f_sb[:, oc, 0:n], PS[pb][:, 0:n], ['ps%d' % pb], ['f_sb'])

        def memC(bi):
            off, n = BLOCKS[bi]
            NORM(f_sb[:, :, 0:n], 8, n, 'mem_post', 1024, None, ['f_sb'], [t_xT[bi]], resid=xT[:, :, off:off + n], ns=NSB2)

        if 'm' not in SKIP:
            memA(0)
            for bi in range(5):
                if bi + 1 < 5:
                    memA(bi + 1)
                memB(bi)
                memC(bi)
        pr.barrier()
        AR.off = mUT

        if KCUT == 3:
            raise StopX()
        gprev = AR.alloc([22, 2]); Gst = AR.alloc([22, 8]); GoutP = AR.alloc([2, 22]); GoutS = AR.alloc([4, 2, 22])
        cvo = AR.alloc([128])
        mF = AR.off
        stc = AR.alloc([2816])
        dma(stc[0:8, :], st_cv, [], ['stc'])
        pb = nxt('pj', 4)
        for j in range(22):
            tr(PS[pb][:, j * 8:(j + 1) * 8], stc[0:8, j * 128:(j + 1) * 128], ident[0:8, 0:8], ['stc', 'ident'], ['ps%d' % pb])
        cp('dve', Gst, PS[pb][:, 0:176].rearrange('p (j a) -> p j a', j=22), ['ps%d' % pb], ['Gst'])
        pr.barrier()
        AR.off = mF
        if KCUT == 4:
            raise StopX()
        hid = AR.alloc([22, 1056], BF16); f_all = AR.alloc([8, 1056])
        hT3 = f_all[:, 0:4, :].bitcast(BF16)
        hT3 = hT3.rearrange('p a b -> p (a b)')[:, 0:8 * 1056].rearrange('p (a b) -> p a b', a=8)
        Wg = [AR.alloc([8, 128], BF16) for _ in range(2)]; Wu = [AR.alloc([8, 128], BF16) for _ in range(2)]
        Wd = [AR.alloc([22, 128], BF16) for _ in range(2)]
        gsb = [AR.alloc([516]) for _ in range(2)]; a1 = [AR.alloc([512]) for _ in range(2)]
        memset('pool', gprev, 0.0, ['gprev'])
        for sbi, blks in enumerate(((0, 1), (2, 3, 4)) if 'f' not in SKIP else ()):
            sb_off = BLOCKS[blks[0]][0]
            for bi in blks:
                off, n = BLOCKS[bi]
                loc = off - sb_off
                NORM(xT[:, :, off:off + n], 8, n, 'ffn_pre', 1024, hT3[:, :, loc:loc + n], [t_xT[bi]], ['hT3'])
            for j in range(22):
                ws = j % 2
                cast_load(Wg[ws].rearrange('p k f -> p (k f)'), w_gate[j], 'Wg%d' % ws)
                cast_load(Wu[ws].rearrange('p k f -> p (k f)'), w_up[j], 'Wu%d' % ws)
                cw0 = G[:, GO['cw0'] + j:GO['cw0'] + j + 1]; cw1 = G[:, GO['cw1'] + j:GO['cw1'] + j + 1]
                cw2 = G[:, GO['cw2'] + j:GO['cw2'] + j + 1]; cb = G[:, GO['cb'] + j:GO['cb'] + j + 1]
                for bi in blks:
                    off, n = BLOCKS[bi]
                    loc = off - sb_off
                    pg = nxt('fg', 2); pu = 2 + nxt('fu', 2)
                    for k in range(8):
                        mm(PS[pg][:, 0:n], Wg[ws][:, k, :], hT3[:, k, loc:loc + n], k == 0, k == 7, ['Wg%d' % ws, 'hT3'], ['ps%d' % pg])
                    for k in range(8):
                        mm(PS[pu][:, 0:n], Wu[ws][:, k, :], hT3[:, k, loc:loc + n], k == 0, k == 7, ['Wu%d' % ws, 'hT3'], ['ps%d' % pu])
                    gs = nxt('gsb', 2)
                    tg, ta1 = 'gsb%d' % gs, 'a1%d' % gs
                    if bi < 4:
                        cp('act', gsb[gs][:, 2:2 + n], PS[pg][:, 0:n], ['ps%d' % pg], [tg])
                        cp('dve', gsb[gs][:, 0:2], gprev[:, j, :], ['gprev'], [tg])
                        cp('dve', gprev[:, j, :], gsb[gs][:, n:n + 2], [tg], ['gprev'])
                        if bi == 3:
                            cp('dve', GoutP[:, :, j], gsb[gs][:, n:n + 2], [tg], ['GoutP'])
                        v0, v1, v2 = gsb[gs][:, 0:n], gsb[gs][:, 1:n + 1], gsb[gs][:, 2:n + 2]
                        va = a1[gs][:, 0:n]; vu = PS[pu][:, 0:n]; vh = hid[:, j, loc:loc + n]
                    else:
                        g3 = gsb[gs][:, 0:40].rearrange('p (s t) -> p s t', s=4)
                        cp('act', g3[:, :, 2:10], PS[pg][:, 0:32].rearrange('p (s t) -> p s t', s=4), ['ps%d' % pg], [tg])
                        cp('dve', g3[:, :, 0:2], Gst[:, j, :].rearrange('p (s r) -> p s r', s=4), ['Gst'], [tg])
                        cp('dve', GoutS[:, :, :, j], g3[:, :, 8:10], [tg], ['GoutS'])
                        v0, v1, v2 = g3[:, :, 0:8], g3[:, :, 1:9], g3[:, :, 2:10]
                        va = a1[gs][:, 0:32].rearrange('p (s t) -> p s t', s=4)
                        vu = PS[pu][:, 0:32].rearrange('p (s t) -> p s t', s=4)
                        vh = hid[:, j, loc:loc + 32].rearrange('p (s t) -> p s t', s=4)
                    ts(va, v0, cw0, cb, ALU.mult, ALU.add, [tg, 'G'], [ta1])
                    stt(va, v1, cw1, va, ALU.mult, ALU.add, [tg, 'G', ta1], [ta1])
                    stt(va, v2, cw2, va, ALU.mult, ALU.add, [tg, 'G', ta1], [ta1])
                    act(va, va, AF.Silu, [ta1], [ta1])
                    tt(vh, va, vu, ALU.mult, [ta1, 'ps%d' % pu], ['hid'])
            pr.barrier()
            for c in range(8):
                ws = c % 2
                cast_load(Wd[ws].rearrange('p j d -> p (j d)'), w_down[c], 'Wd%d' % ws)
                for bi in blks:
                    off, n = BLOCKS[bi]
                    loc = off - sb_off
                    pb = nxt('pj', 4)
                    for j in range(22):
                        mm(PS[pb][:, 0:n], Wd[ws][:, j, :], hid[:, j, loc:loc + n], j == 0, j == 21, ['Wd%d' % ws, 'hid'], ['ps%d' % pb])
                    cp('act' if c % 2 else 'dve', f_all[:, c, loc:loc + n], PS[pb][:, 0:n], ['ps%d' % pb], ['f_all'])
            for bi in blks:
                off, n = BLOCKS[bi]
                loc = off - sb_off
                NORM(f_all[:, :, loc:loc + n], 8, n, 'ffn_post', 1024, None, ['f_all'], [t_xT[bi]], resid=xT[:, :, off:off + n])
            pr.barrier()
        if KCUT == 5:
            raise StopX()
        pb = nxt('pj', 4)
        tr(PS[pb][0:44, 0:128], GoutP[:].rearrange('p r j -> p (r j)'), ident, ['GoutP', 'ident'], ['ps%d' % pb])
        cp('act', cvo[0:44, :], PS[pb][0:44, 0:128], ['ps%d' % pb], ['cvo'])
        dma_out(o_cv_p.rearrange('r (j p) -> (r j) p', p=128), cvo[0:44, :], ['cvo'])
        gs2 = GoutS[:].rearrange('p s r j -> p (s r j)')
        for hf in range(2):
            pb = nxt('pj', 4)
            tr(PS[pb][0:88, 0:128], gs2[:, hf * 88:(hf + 1) * 88], ident, ['GoutS', 'ident'], ['ps%d' % pb])
            cp('act', cvo[0:88, :], PS[pb][0:88, 0:128], ['ps%d' % pb], ['cvo'])
            dma_out(o_cv_s.rearrange('a (j p) -> (a j) p', p=128)[hf * 88:(hf + 1) * 88, :], cvo[0:88, :], ['cvo'])
        pr.barrier()
        AR.off = mUT


    except StopX:
        pr.barrier()
        AR.off = mUT

    ytm = AR.alloc([2, 1024])
    for bi, (off, n) in enumerate(BLOCKS):
        for t0 in range(0, n, 128):
            nt = min(128, n - t0)
            sl = nxt('ytm', 2)
            for half in range(2):
                pb = nxt('pj', 4)
                for c4 in range(4):
                    c = half * 4 + c4
                    tr(PS[pb][0:nt, c4 * 128:(c4 + 1) * 128], xT[:, c, off + t0:off + t0 + nt], ident, [t_xT[bi], 'ident'],
                       ['ps%d' % pb])
                cp('act' if half else 'dve', ytm[0:nt, sl, half * 512:(half + 1) * 512], PS[pb][0:nt, :], ['ps%d' % pb],
                   ['ytm%d' % sl])
            dst = y_p[off + t0:off + t0 + nt, :] if bi < 4 else y_s
            dma_out(dst, ytm[0:nt, sl, :], ['ytm%d' % sl])
    for i_ in range(int(os.environ.get('KPAD', '0'))):
        pb = nxt('pj', 4)
        tr(PS[pb][:, 0:128], ident, ident, ['ident'], ['ps%d' % pb])
        cp('act', ytm[:, 0, 0:128], PS[pb][:, 0:128], ['ps%d' % pb], ['ytm0'])
    if os.environ.get('KFB', '1') == '1':
        pr.barrier()
    pr.add('sp', None, ['__out%d' % i for i in range(len(out_dmas))], [])
    pr.emit(nc, es)
    return nc, es


_GO = [('norm_mix_pre', 8), ('q_norm', 2), ('kv_norm', 2), ('norm_attn_out', 4), ('norm_ssm_out', 4),
       ('norm_mix_post', 8), ('norm_mem_pre', 8), ('mem_norm', 8), ('norm_mem_post', 8), ('norm_ffn_pre', 8),
       ('norm_ffn_post', 8), ('ssm_d', 4)]


def _consts():
    half = 16
    inv = (10000.0 ** (-np.arange(half, dtype=np.float32) * np.float32(2.0 / 32))).astype(np.float32)
    pos = np.concatenate([np.arange(2048), np.tile(16384 + np.arange(8), 4)]).astype(np.float32)
    ang = (pos[None, :] * inv[:, None]).astype(np.float32)
    rope = np.zeros((32, 2, NTOK), np.float32)
    rope[:, 0, :] = np.concatenate([np.cos(ang), np.cos(ang)], 0)
    rope[:, 1, :] = np.concatenate([np.sin(ang), np.sin(ang)], 0)
    tri = (np.arange(128)[None, :] >= np.arange(128)[:, None]).astype(np.float32)
    mskn = np.zeros((8, 64), np.float32)
    for i in range(8):
        for h in range(8):
            for t in range(8):
                mskn[i, h * 8 + t] = 1.0 if i <= t else 0.0
    smask = np.ones((1, 32), np.float32); smask[0, ::8] = 0.0
    return dict(k_ident=np.eye(128, dtype=np.float32), k_rope=rope, k_tri=tri, k_mskn=mskn,
                k_iota=np.arange(2048, dtype=np.float32)[None, :], k_smask=smask,
                k_pidx32=(np.arange(128) % 32).astype(np.float32)[:, None])


_CACHE = {}


def make_maps(inp, ncores=NCORES):
    f = lambda a: np.ascontiguousarray(np.asarray(a))
    g = [f(inp[k])[0].reshape(c, 128) for k, c in _GO]
    g.append(f(inp['ffn_conv_w'])[0].reshape(66, 128))
    g.append(f(inp['ffn_conv_b'])[0].reshape(22, 128))
    gains = np.concatenate(g, 0).astype(np.float32)
    assert gains.shape == (160, 128)
    shared = dict(
        c_lat=f(inp['cache_kv_latent'])[0], c_pe=f(inp['cache_k_rope'])[0], gains=gains,
        w_in=f(inp['w_in'])[0], w_uq=f(inp['w_uq'])[0].reshape(256, 768), w_uk=f(inp['w_uk'])[0].reshape(256, 512),
        w_uv=f(inp['w_uv'])[0].reshape(256, 512), a_re=f(inp['ssm_a_re'])[0].reshape(16, 128),
        a_im=f(inp['ssm_a_im'])[0].reshape(16, 128), log_dt=f(inp['ssm_log_dt'])[0].reshape(16, 2),
        b_re=f(inp['ssm_b_re'])[0].reshape(2048, 16), b_im=f(inp['ssm_b_im'])[0].reshape(2048, 16),
        cc_re=f(inp['ssm_c_re'])[0], cc_im=f(inp['ssm_c_im'])[0], w_glu=f(inp['ssm_w_glu'])[0],
        w_out=f(inp['w_out'])[0], w_qm=f(inp['w_q_mem'])[0], w_km=f(inp['w_k_mem'])[0], w_vm=f(inp['w_v_mem'])[0],
        w_om=f(inp['w_o_mem'])[0],
        w_gate=f(f(inp['w_gate'])[0].reshape(8, 128, 22, 128).transpose(2, 1, 0, 3)).reshape(22, 128, 1024),
        w_up=f(f(inp['w_up'])[0].reshape(8, 128, 22, 128).transpose(2, 1, 0, 3)).reshape(22, 128, 1024),
        w_down=f(f(inp['w_down'])[0].reshape(22, 128, 8, 128).transpose(2, 1, 0, 3)).reshape(8, 128, 2816))
    shared.update(_consts())
    in_maps = []
    for c in range(ncores):
        sl = slice(4 * c, 4 * c + 4)
        m = dict(shared)
        m.update(x_p=f(inp['x_prompt'])[c], x_s=f(inp['x_sample'])[sl].reshape(32, 1024), mem_p=f(inp['mem_prompt'])[c],
                 ptab=f(inp['page_table'])[sl].reshape(1, 512).astype(np.int32),
                 st_re=f(inp['state_ssm_re'])[0, sl].reshape(64, 128), st_im=f(inp['state_ssm_im'])[0, sl].reshape(64, 128),
                 st_cv=f(inp['state_ffn_conv'])[0, sl].reshape(8, 2816),
                 memk=f(inp['cache_mem_k'])[0, sl].reshape(4, 256, 512), memv=f(inp['cache_mem_v'])[0, sl].reshape(4, 256, 512))
        in_maps.append({k: np.ascontiguousarray(v) for k, v in m.items()})
    return in_maps


def kernel(**inp):
    if 'nc' not in _CACHE:
        _CACHE['nc'] = build()
    nc, es = _CACHE['nc']
    in_maps = make_maps(inp)
    res = run_bass_kernel_spmd(nc, in_maps, core_ids=list(range(NCORES)))
    R = res.results
    cat = lambda k: np.stack([np.asarray(R[c][k]) for c in range(NCORES)], 0)
    y_p = cat('y_p'); y_s = cat('y_s').reshape(32, 8, 1024)
    outs = (y_p, y_s,
            cat('o_kvl_p')[None], cat('o_kr_p')[None],
            cat('o_sre_p').reshape(1, 8, 32, 64), cat('o_sim_p').reshape(1, 8, 32, 64),
            cat('o_cv_p')[None],
            cat('o_mk_p').reshape(1, 8, 256, 4, 128), cat('o_mv_p').reshape(1, 8, 256, 4, 128),
            cat('o_kvl_s').reshape(1, 32, 8, 256), cat('o_kr_s').reshape(1, 32, 8, 32),
            cat('o_sre_s').reshape(1, 32, 32, 64), cat('o_sim_s').reshape(1, 32, 32, 64),
            cat('o_cv_s').reshape(1, 32, 2, 2816))
    return tuple(np.ascontiguousarray(o, dtype=np.float32) for o in outs)
```

```python
import math, os
import os
from contextlib import ExitStack
import numpy as np
import concourse.bass as bass
import concourse.mybir as mybir
from concourse.bass_utils import run_bass_kernel_spmd

F32 = mybir.dt.float32
BF16 = mybir.dt.bfloat16
I32 = mybir.dt.int32
AF = mybir.ActivationFunctionType
ALU = mybir.AluOpType

NCORES = 8
NP_, NSMP, NTOK = 2048, 32, 2080
BLOCKS = [(0, 512), (512, 512), (1024, 512), (1536, 512), (2048, 32)]
ATTN_SCALE = 96.0 ** -0.5
MEM_SCALE = 128.0 ** -0.5
EPS = 1e-6
DSIZE = {F32: 4, BF16: 2, I32: 4}
ENGS = ['pe', 'act', 'dve', 'pool', 'sp']
NSQ = {'sp': int(os.environ.get('KNSP', '24')), 'pool': int(os.environ.get('KNPL', '24'))}


class Prog:
    def __init__(self):
        self.ops = []
        self.per = {e: [] for e in ENGS}
        self.lastw = {}
        self.readers = {}
        self.dma_since = []

    def add(self, eng, fn, r=(), w=(), dma=False):
        oid = len(self.ops)
        deps = set()
        for t in list(r) + list(w):
            if t in self.lastw:
                deps.add(self.lastw[t])
        for t in w:
            deps.update(self.readers.get(t, {}).values())
            deps.update(self.readers.get((t, 'dma'), []))
        op = dict(id=oid, eng=eng, fn=fn, deps=deps, dma=dma)
        self.ops.append(op)
        self.per[eng].append(op)
        if dma:
            self.dma_since.append(oid)
        for t in w:
            self.lastw[t] = oid
            self.readers[t] = {}
            self.readers[(t, 'dma')] = []
        for t in r:
            if dma:
                self.readers.setdefault((t, 'dma'), []).append(oid)
            else:
                self.readers.setdefault(t, {})[eng] = oid
        return oid

    def barrier(self):
        last = set()
        for e in ENGS:
            comp = [op['id'] for op in self.per[e] if not op['dma'] and op['fn'] is not None]
            if comp:
                last.add(comp[-1])
        last.update(self.dma_since)
        self.dma_since = []
        for e in ENGS:
            oid = len(self.ops)
            op = dict(id=oid, eng=e, fn=None, deps=set(last), dma=False)
            self.ops.append(op)
            self.per[e].append(op)

    def emit(self, nc, es):
        ops = self.ops
        need = set()
        for op in ops:
            for d in op['deps']:
                dop = ops[d]
                if dop['dma']:
                    continue
                if dop['eng'] == 'pe' and op['eng'] == 'pe' and not op['dma']:
                    continue
                need.add(d)
        esem = {e: es.enter_context(nc.semaphore('se_' + e)) for e in ENGS}
        dsem = {q: [es.enter_context(nc.semaphore('sd_%s%d' % (q, i))) for i in range(n)]
                for q, n in NSQ.items()}
        for e in ENGS:
            c = 0
            k = 0
            for op in self.per[e]:
                if op['dma']:
                    op['sem'] = dsem[e][k % NSQ[e]]
                    op['val'] = 16 * (k // NSQ[e] + 1)
                    k += 1
                elif op['id'] in need:
                    c += 1
                    op['sig'] = c
        block = es.enter_context(nc.Block())

        def run(e, eng):
            waited = {}
            last_tsz = [None]
            for op in self.per[e]:
                waits = {}
                for d in op['deps']:
                    dop = ops[d]
                    if dop['dma']:
                        s, v = dop['sem'], dop['val']
                    else:
                        if dop['eng'] == 'pe' and e == 'pe' and not op['dma']:
                            continue
                        s, v = esem[dop['eng']], dop['sig']
                    if waits.get(s, (None, 0))[1] < v:
                        waits[s] = (s, v)
                if op['dma'] and op['val'] > 16:
                    s, v = op['sem'], op['val'] - 16
                    if waits.get(s, (None, 0))[1] < v:
                        waits[s] = (s, v)
                wl = []
                for s, v in waits.values():
                    if waited.get(s, 0) >= v:
                        continue
                    waited[s] = v
                    wl.append((s, v))
                attach = None
                if wl and e in ('act', 'dve') and not op['dma'] and op['fn'] is not None \
                        and os.environ.get('KATTACH', '1') == '1':
                    attach = wl.pop()
                for s, v in wl:
                    eng.wait_ge(s, v)
                if op['fn'] is None:
                    continue
                if e == 'pe' and 'tsz' in op and os.environ.get('KDRAIN', '0') == '1':
                    if last_tsz[0] is not None and last_tsz[0] != op['tsz']:
                        eng.drain()
                    last_tsz[0] = op['tsz']
                ins = op['fn'](eng)
                if attach is not None:
                    ins._wait_ge(attach[0], attach[1])
                if op['dma']:
                    ins.then_inc(op['sem'], 16)
                elif op['id'] in need:
                    ins.then_inc(esem[e], 1)

        @block.tensor
        def _(eng):
            run('pe', eng)

        @block.scalar
        def _(eng):
            run('act', eng)

        @block.vector
        def _(eng):
            run('dve', eng)

        @block.gpsimd
        def _(eng):
            run('pool', eng)

        @block.sync
        def _(eng):
            run('sp', eng)


class Arena:
    def __init__(self, ap, nwords):
        self.ap, self.n, self.off = ap, nwords, 0
        self.hi = 0

    def alloc(self, free, dtype=F32):
        n = 1
        for s in free:
            n *= s
        words = (n * DSIZE[dtype] + 3) // 4
        words = (words + 15) // 16 * 16
        a = self.off
        self.off += words
        self.hi = max(self.hi, self.off)
        assert self.off <= self.n, ('arena overflow', self.off, self.n)
        v = self.ap[:, a:a + words]
        if dtype != F32:
            v = v.bitcast(dtype)
        v = v[:, 0:n]
        if len(free) == 2:
            v = v.rearrange('p (a b) -> p a b', a=free[0])
        elif len(free) == 3:
            v = v.rearrange('p (a b c) -> p a b c', a=free[0], b=free[1])
        elif len(free) == 4:
            v = v.rearrange('p (a b c d) -> p a b c d', a=free[0], b=free[1], c=free[2])
        return v


def build(npool=5120, debug=False):
    nc = bass.Bass('TRN2', target_bir_lowering=False)
    pr = Prog()

    def din(name, shape, dt=F32):
        return nc.dram_tensor(name, list(shape), dt, kind='ExternalInput').ap()

    def dout(name, shape, dt=F32):
        return nc.dram_tensor(name, list(shape), dt, kind='ExternalOutput').ap()

    x_p = din('x_p', [NP_, 1024]); x_s = din('x_s', [NSMP, 1024]); mem_p = din('mem_p', [256, 1024])
    c_lat = din('c_lat', [npool, 128, 256]); c_pe = din('c_pe', [npool, 128, 32])
    ptab = din('ptab', [1, 512], I32)
    st_re = din('st_re', [64, 128]); st_im = din('st_im', [64, 128])
    st_cv = din('st_cv', [8, 2816])
    memk = din('memk', [4, 256, 512]); memv = din('memv', [4, 256, 512])
    gains = din('gains', [160, 128])
    w_in = din('w_in', [1024, 1056]); w_uq = din('w_uq', [256, 768]); w_uk = din('w_uk', [256, 512])
    w_uv = din('w_uv', [256, 512])
    a_re = din('a_re', [16, 128]); a_im = din('a_im', [16, 128]); log_dt = din('log_dt', [16, 2])
    b_re = din('b_re', [2048, 16]); b_im = din('b_im', [2048, 16])
    cc_re = din('cc_re', [32, 16, 64]); cc_im = din('cc_im', [32, 16, 64])
    w_glu = din('w_glu', [512, 512]); w_out = din('w_out', [1024, 1024])
    w_qm = din('w_qm', [1024, 512]); w_km = din('w_km', [1024, 512]); w_vm = din('w_vm', [1024, 512])
    w_om = din('w_om', [512, 1024])
    w_gate = din('w_gate', [22, 128, 1024]); w_up = din('w_up', [22, 128, 1024]); w_down = din('w_down', [8, 128, 2816])
    k_ident = din('k_ident', [128, 128]); k_rope = din('k_rope', [32, 2, NTOK])
    k_tri = din('k_tri', [128, 128]); k_mskn = din('k_mskn', [8, 64]); k_iota = din('k_iota', [1, 2048])
    k_smask = din('k_smask', [1, 32]); k_pidx32 = din('k_pidx32', [128, 1])

    y_p = dout('y_p', [NP_, 1024]); y_s = dout('y_s', [NSMP, 1024])
    o_kvl_p = dout('o_kvl_p', [NP_, 256]); o_kr_p = dout('o_kr_p', [NP_, 32])
    o_sre_p = dout('o_sre_p', [16, 128]); o_sim_p = dout('o_sim_p', [16, 128])
    o_cv_p = dout('o_cv_p', [2, 2816])
    o_mk_p = dout('o_mk_p', [256, 512]); o_mv_p = dout('o_mv_p', [256, 512])
    o_kvl_s = dout('o_kvl_s', [NSMP, 256]); o_kr_s = dout('o_kr_s', [NSMP, 32])
    o_sre_s = dout('o_sre_s', [64, 128]); o_sim_s = dout('o_sim_s', [64, 128])
    o_cv_s = dout('o_cv_s', [8, 2816])

    es = ExitStack()
    NW = 53000
    arena_t = es.enter_context(nc.sbuf_tensor('arena', [128, NW], F32))
    AR = Arena(arena_t[:], NW)
    PS = [es.enter_context(nc.psum_tensor('ps%d' % i, [128, 512], F32)) for i in range(8)]
    rg = es.enter_context(nc.sync.register('rg'))
    out_dmas = []

    def _rnd(x):
        return 32 if x <= 32 else (64 if x <= 64 else 128)

    def _tsz(ap):
        sh = ap.shape
        fsz = 1
        for d_ in sh[1:]:
            fsz *= d_
        return (_rnd(sh[0]), _rnd(fsz))

    def mm(out, lhsT, rhs, start, stop, r, w):
        oid = pr.add('pe', lambda e: e.matmul(out, lhsT=lhsT, rhs=rhs, start=start, stop=stop), r, w)
        pr.ops[oid]['tsz'] = _tsz(lhsT)

    def tr(out, in_, ident, r, w):
        oid = pr.add('pe', lambda e: e.transpose(out, in_, ident), r, w)
        pr.ops[oid]['tsz'] = _tsz(in_)

    def act(out, in_, func, r, w, scale=1.0, bias=None):
        if bias is None:
            pr.add('act', lambda e: e.activation(out=out, in_=in_, func=func, scale=scale), r, w)
        else:
            pr.add('act', lambda e: e.activation(out=out, in_=in_, func=func, scale=scale, bias=bias), r, w)

    def cp(eng, out, in_, r, w):
        if eng == 'pool' and os.environ.get('KNOPOOL', '0') == '1':
            eng = 'dve'
        if eng == 'act':
            pr.add('act', lambda e: e.activation(out=out, in_=in_, func=AF.Copy), r, w)
        else:
            pr.add(eng, lambda e: e.tensor_copy(out=out, in_=in_), r, w)

    def tt(out, in0, in1, op, r, w, eng='dve'):
        pr.add(eng, lambda e: e.tensor_tensor(out=out, in0=in0, in1=in1, op=op), r, w)

    def ts(out, in0, s1, s2, op0, op1, r, w, eng='dve'):
        if s2 is None:
            pr.add(eng, lambda e: e.tensor_scalar(out=out, in0=in0, scalar1=s1, scalar2=None, op0=op0), r, w)
        else:
            pr.add(eng, lambda e: e.tensor_scalar(out=out, in0=in0, scalar1=s1, scalar2=s2, op0=op0, op1=op1), r, w)

    def stt(out, in0, scalar, in1, op0, op1, r, w):
        pr.add('dve', lambda e: e.scalar_tensor_tensor(out=out, in0=in0, scalar=scalar, in1=in1, op0=op0, op1=op1), r, w)

    def scan(out, d0, d1, init, r, w):
        pr.add('dve', lambda e: e.tensor_tensor_scan(out=out, data0=d0, data1=d1, initial=init,
                                                     op0=ALU.mult, op1=ALU.add), r, w)

    def recip(out, in_, r, w):
        pr.add('dve', lambda e: e.reciprocal(out=out, in_=in_), r, w)

    def memset(eng, ap, val, w):
        if eng == 'pool' and os.environ.get('KNOPOOL', '0') == '1':
            eng = 'dve'
        pr.add(eng, lambda e: e.memset(ap, val), (), w)

    def dma(out, in_, r, w, q='sp', slow=False):
        if slow:
            return pr.add(q, lambda e: e.dma_start(out=out, in_=in_, allow_slow_non_contiguous=True), r, w, dma=True)
        return pr.add(q, lambda e: e.dma_start(out=out, in_=in_), r, w, dma=True)

    def dma_out(out, in_, r):
        out_dmas.append(dma(out, in_, r, ['__out%d' % len(out_dmas)]))

    rot = {}

    def nxt(key, n):
        rot[key] = (rot.get(key, -1) + 1) % n
        return rot[key]

    xT = AR.alloc([8, NTOK]); t_xT = ['xT%d' % b for b in range(5)]
    ident = AR.alloc([128]); identb = AR.alloc([128], BF16)
    ones_b = AR.alloc([128], BF16); ones_f = AR.alloc([128])
    G = AR.alloc([160])
    epsT = AR.alloc([1]); hpiT = AR.alloc([1])
    tri = AR.alloc([128], BF16); mskn = AR.alloc([64], BF16)
    GO = dict(mix_pre=0, q=8, kv=10, attn=12, ssm=16, mix_post=20, mem_pre=28, mem=36, mem_post=44,
              ffn_pre=52, ffn_post=60, d=68, cw0=72, cw1=94, cw2=116, cb=138)

    dma(ident, k_ident, [], ['ident'])
    cp('act', identb, ident, ['ident'], ['identb'])
    memset('pool', ones_b, 1.0, ['ones_b']); memset('pool', ones_f, 1.0, ['ones_f'])
    memset('pool', epsT, EPS, ['epsT']); memset('pool', hpiT, math.pi / 2, ['hpiT'])
    KOLD = os.environ.get('KOLD', '0')
    if '1' in KOLD:
        trf = AR.alloc([128]); mnf = AR.alloc([64]); graw = AR.alloc([2, 128])
    if '2' not in KOLD:
        KmT = AR.alloc([4, 256], BF16); Vm = AR.alloc([2, 512], BF16)
    mark_persist = AR.off

    def cast_load(dst, src, tok, q='pool'):
        dma(dst, src, [], [tok], q=q)

    def norm_fm(src, C, n, gcol, D, dst, r, w, sq, ssb, rstd, psb, resid=None, tmp=None, sfx=''):
        tq_, ts_, tr_, tt_ = 'nsq' + sfx, 'nss' + sfx, 'nrstd' + sfx, 'ntmp' + sfx
        act(sq[:, 0:C, 0:n], src, AF.Square, r, [tq_])
        for c in range(C):
            mm(PS[psb][:, 0:n], ones_b, sq[:, c, 0:n], c == 0, c == C - 1, [tq_, 'ones_b'], ['ps%d' % psb])
        act(ssb[:, 0:n], PS[psb][:, 0:n], AF.Ln, ['ps%d' % psb, 'epsT'], [ts_], scale=1.0 / D, bias=epsT)
        act(rstd[:, 0:n], ssb[:, 0:n], AF.Exp, [ts_], [tr_], scale=-0.5)
        for c in range(C):
            if resid is None:
                stt(dst[:, c, :], src[:, c, :], G[:, gcol + c:gcol + c + 1], rstd[:, 0:n], ALU.mult, ALU.mult,
                    list(r) + [tr_, 'G'], w)
            else:
                stt(tmp[:, 0:n], src[:, c, :], G[:, gcol + c:gcol + c + 1], rstd[:, 0:n], ALU.mult, ALU.mult,
                    list(r) + [tr_, 'G'], [tt_])
                tt(resid[:, c, :], resid[:, c, :], tmp[:, 0:n], ALU.add, [tt_] + list(w), w)

    n_sq = AR.alloc([8, 512], BF16); n_ss = AR.alloc([512]); n_rstd = AR.alloc([512]); n_tmp = AR.alloc([512])

    def NORM(src, C, n, gname, D, dst, r, w, psb=7, resid=None, ns=None):
        if ns is None:
            norm_fm(src, C, n, GO[gname], D, dst, r, w, n_sq, n_ss, n_rstd, psb, resid=resid, tmp=n_tmp)
        else:
            norm_fm(src, C, n, GO[gname], D, dst, r, w, ns[0], ns[1], ns[2], 6, resid=resid, tmp=ns[3], sfx='B')

    def norm_set():
        return (AR.alloc([8, 512], BF16), AR.alloc([512]), AR.alloc([512]), AR.alloc([512]))

    m0 = AR.off
    if '1' not in KOLD:
        trf = AR.alloc([128]); mnf = AR.alloc([64])
    dma(trf, k_tri, [], ['trf']); dma(mnf[0:8, :], k_mskn, [], ['mnf'])
    cp('dve', tri, trf, ['trf'], ['tri']); cp('dve', mskn[0:8, :], mnf[0:8, :], ['mnf'], ['mskn'])
    if '1' not in KOLD:
        graw = AR.alloc([2, 128])
    dma(graw[:, 0, :], gains[0:128, :], [], ['graw0']); dma(graw[0:32, 1, :], gains[128:160, :], [], ['graw1'])
    tr(PS[0][:, 0:128], graw[:, 0, :], ident, ['graw0', 'ident'], ['ps0'])
    tr(PS[0][:, 128:160], graw[0:32, 1, :], ident[0:32, 0:32], ['graw1', 'ident'], ['ps0'])
    cp('dve', G, PS[0][:, 0:160], ['ps0'], ['G'])
    xtm = AR.alloc([2, 1024])
    for t in range(17):
        sl = t % 2
        nt = 128 if t < 16 else 32
        src = x_p[t * 128:(t + 1) * 128, :] if t < 16 else x_s[:, :]
        dma(xtm[0:nt, sl, :], src, [], ['xtm%d' % sl])
        b = min(t // 4, 4)
        col0 = t * 128
        for half in range(2):
            pb = nxt('x0', 2)
            for c4 in range(4):
                c = half * 4 + c4
                tr(PS[pb][:, c4 * 128:c4 * 128 + nt], xtm[0:nt, sl, c * 128:(c + 1) * 128], ident[0:nt, 0:nt],
                   ['xtm%d' % sl, 'ident'], ['ps%d' % pb])
            cp('act' if half == 0 else 'dve', xT[:, half * 4:half * 4 + 4, col0:col0 + nt],
               PS[pb][:].rearrange('p (a b) -> p a b', a=4)[:, :, 0:nt], ['ps%d' % pb], [t_xT[b]])
    pr.barrier()
    AR.off = m0

    def out_rows(src_fm, C, n, dst_rows, width, r, base=0):
        for t0 in range(0, n, 128):
            nt = min(128, n - t0)
            pb = nxt('orow', 2)
            sl = nxt('otm', 2)
            for c in range(C):
                if base == 0:
                    tr(PS[pb][0:nt, c * 128:(c + 1) * 128], src_fm[:, c, t0:t0 + nt], ident, list(r) + ['ident'], ['ps%d' % pb])
                else:
                    tr(PS[pb][0:nt, 0:32], src_fm[base:base + 32, c, t0:t0 + nt], ident[base:base + 32, base:base + 32],
                       list(r) + ['ident'], ['ps%d' % pb])
            cp('act', otm[0:nt, sl, 0:width], PS[pb][0:nt, 0:width], ['ps%d' % pb], ['otm%d' % sl])
            dma_out(dst_rows[t0:t0 + nt, :], otm[0:nt, sl, 0:width], ['otm%d' % sl])

    otm = AR.alloc([2, 512])

    mUT = AR.off
    uT = AR.alloc([4, NTOK], BF16); attn_o = AR.alloc([4, NTOK], BF16)
    mATT = AR.off
    ropeT = AR.alloc([2, NTOK])
    dma(ropeT[64:96, :, :], k_rope, [], ['ropeT'])
    cqn = AR.alloc([2, NTOK], BF16); ckvn = AR.alloc([2, NTOK], BF16)
    KpeT = AR.alloc([NTOK], BF16)
    mA = AR.off
    Win = AR.alloc([8, 1056], BF16); WinSw = AR.alloc([8, 32], BF16)
    w_in_v = w_in.rearrange('(c p) n -> p c n', p=128)
    for c in range(8):
        cast_load(Win[:, c, :], w_in_v[:, c, :], 'Win')
    cast_load(WinSw[:, :, 0:16], w_in_v[:, :, 528:544], 'WinSw')
    cast_load(WinSw[:, :, 16:32], w_in_v[:, :, 512:528], 'WinSw')
    ts(WinSw[:, :, 0:16], WinSw[:, :, 0:16], -1.0, None, ALU.mult, None, ['WinSw'], ['WinSw'])
    hT = AR.alloc([8, 512], BF16)
    zq = AR.alloc([2, 512]); zkv = zq; zkvn = AR.alloc([2, 512])
    krf = AR.alloc([1, 512]); rt1 = AR.alloc([512]); rt2 = AR.alloc([512])
    for bi, (off, n) in enumerate(BLOCKS):
        NORM(xT[:, :, off:off + n], 8, n, 'mix_pre', 1024, hT[:, :, 0:n], [t_xT[bi]], ['hT'])

        def proj(col0, M, outp, pb, wt=Win, wtok='Win'):
            for k in range(8):
                mm(outp, wt[:, k, col0:col0 + M], hT[:, k, 0:n], k == 0, k == 7, [wtok, 'hT'], ['ps%d' % pb])
        for c in range(2):
            pb = nxt('pj', 4)
            proj(c * 128, 128, PS[pb][:, 0:n], pb)
            cp('act', zq[:, c, 0:n], PS[pb][:, 0:n], ['ps%d' % pb], ['zq'])
        NORM(zq[:, :, 0:n], 2, n, 'q', 256, cqn[:, :, off:off + n], ['zq'], ['cqn'])
        for c in range(2):
            pb = nxt('pj', 4)
            proj(256 + c * 128, 128, PS[pb][:, 0:n], pb)
            cp('act', zkv[:, c, 0:n], PS[pb][:, 0:n], ['ps%d' % pb], ['zq'])
        NORM(zkv[:, :, 0:n], 2, n, 'kv', 256, zkvn[:, :, 0:n], ['zq'], ['zkvn'])
        cp('act', ckvn[:, :, off:off + n], zkvn[:, :, 0:n], ['zkvn'], ['ckvn'])
        out_rows(zkvn, 2, n, (o_kvl_p[off:off + n, :] if bi < 4 else o_kvl_s), 256, ['zkvn'])
        pb = nxt('pj', 4); pb2 = nxt('pj', 4)
        proj(512, 32, PS[pb][64:96, 0:n], pb)
        proj(0, 32, PS[pb2][64:96, 0:n], pb2, wt=WinSw, wtok='WinSw')
        tt(rt1[64:96, 0:n], PS[pb][64:96, 0:n], ropeT[64:96, 0, off:off + n], ALU.mult, ['ps%d' % pb, 'ropeT'], ['rt1'])
        tt(rt2[64:96, 0:n], PS[pb2][64:96, 0:n], ropeT[64:96, 1, off:off + n], ALU.mult, ['ps%d' % pb2, 'ropeT'], ['rt2'])
        tt(krf[64:96, 0, 0:n], rt1[64:96, 0:n], rt2[64:96, 0:n], ALU.add, ['rt1', 'rt2'], ['krf'])
        cp('act', KpeT[64:96, off:off + n], krf[64:96, 0, 0:n], ['krf'], ['KpeT'])
        out_rows(krf, 1, n, (o_kr_p[off:off + n, :] if bi < 4 else o_kr_s), 32, ['krf'], base=64)
        for c in range(4):
            pb = nxt('pj', 4)
            proj(544 + c * 128, 128, PS[pb][:, 0:n], pb)
            cp('act' if c % 2 else 'dve', uT[:, c, off:off + n], PS[pb][:, 0:n], ['ps%d' % pb], ['uT'])
    pr.barrier()
    AR.off = mA

    Wuv = AR.alloc([2, 512], BF16); QA = AR.alloc([4, 2, 64], BF16); QAp = AR.alloc([4, 64], BF16)
    mSA = AR.off
    Wuq = AR.alloc([2, 768], BF16); WuqSw = AR.alloc([2, 8, 32], BF16)
    Wuk = AR.alloc([2, 512], BF16); WukT = AR.alloc([8, 256], BF16)
    w_uq_v = w_uq.rearrange('(c p) n -> p c n', p=128)
    cast_load(Wuq, w_uq_v, 'Wuq')
    w_uq_4 = w_uq.rearrange('(c p) (h e) -> p c h e', p=128, e=96)
    for c in range(2):
        cast_load(WuqSw[:, c, :, 0:16], w_uq_4[:, c, :, 80:96], 'WuqSw')
        cast_load(WuqSw[:, c, :, 16:32], w_uq_4[:, c, :, 64:80], 'WuqSw')
    ts(WuqSw[:, :, :, 0:16], WuqSw[:, :, :, 0:16], -1.0, None, ALU.mult, None, ['WuqSw'], ['WuqSw'])
    cast_load(Wuk, w_uk.rearrange('(c p) n -> p c n', p=128), 'Wuk')
    cast_load(Wuv, w_uv.rearrange('(c p) n -> p c n', p=128), 'Wuv')
    for h in range(8):
        pb = nxt('pj', 4)
        psb = PS[pb][:].bitcast(BF16)
        for c in range(2):
            tr(psb[0:64, c * 128:(c + 1) * 128], Wuk[:, c, h * 64:(h + 1) * 64], identb, ['Wuk', 'identb'], ['ps%d' % pb])
        cp('dve', WukT[0:64, h, :], psb[0:64, 0:256], ['ps%d' % pb], ['WukT'])
    QT = [AR.alloc([NTOK], BF16) for _ in range(2)]
    KT = [AR.alloc([NTOK], BF16) for _ in range(2)]
    Vh = [AR.alloc([17, 65], BF16) for _ in range(2)]
    PT = [AR.alloc([512], BF16) for _ in range(3)]
    qr1 = AR.alloc([512]); qr2 = AR.alloc([512])
    osb = AR.alloc([512]); rsb = AR.alloc([512]); otmpb = AR.alloc([512], BF16)
    for i in range(2):
        memset('pool', Vh[i][:, :, 64:65], 1.0, ['Vh%d' % i])
    for h in range(8):
        hs = h % 2
        tq, tk, tv = 'QT%d' % hs, 'KT%d' % hs, 'Vh%d' % hs
        for bi, (off, n) in enumerate(BLOCKS):
            pb = nxt('pj', 4); pb2 = nxt('pj', 4); pb3 = nxt('pj', 4)
            for c in range(2):
                mm(PS[pb][0:96, 0:n], Wuq[:, c, h * 96:(h + 1) * 96], cqn[:, c, off:off + n], c == 0, c == 1,
                   ['Wuq', 'cqn'], ['ps%d' % pb])
            for c in range(2):
                mm(PS[pb2][64:96, 0:n], WuqSw[:, c, h, :], cqn[:, c, off:off + n], c == 0, c == 1,
                   ['WuqSw', 'cqn'], ['ps%d' % pb2])
            cp('act', QT[hs][0:64, off:off + n], PS[pb][0:64, 0:n], ['ps%d' % pb], [tq])
            tt(qr1[64:96, 0:n], PS[pb][64:96, 0:n], ropeT[64:96, 0, off:off + n], ALU.mult, ['ps%d' % pb, 'ropeT'], ['qr1'])
            tt(qr2[64:96, 0:n], PS[pb2][64:96, 0:n], ropeT[64:96, 1, off:off + n], ALU.mult, ['ps%d' % pb2, 'ropeT'], ['qr2'])
            tt(QT[hs][64:96, off:off + n], qr1[64:96, 0:n], qr2[64:96, 0:n], ALU.add, ['qr1', 'qr2'], [tq])
            for c in range(2):
                mm(PS[pb3][0:64, 0:n], Wuk[:, c, h * 64:(h + 1) * 64], ckvn[:, c, off:off + n], c == 0, c == 1,
                   ['Wuk', 'ckvn'], ['ps%d' % pb3])
            cp('act', KT[hs][0:64, off:off + n], PS[pb3][0:64, 0:n], ['ps%d' % pb3], [tk])
        cp('pool', KT[hs][64:96, :], KpeT[64:96, :], ['KpeT'], [tk])
        for g4 in range(5):
            pb = nxt('pj', 4)
            ntl = 4 if g4 < 4 else 1
            for j in range(ntl):
                t = g4 * 4 + j
                nt = 128 if t < 16 else 32
                for c in range(2):
                    mm(PS[pb][0:nt, j * 64:(j + 1) * 64], ckvn[:, c, t * 128:t * 128 + nt], Wuv[:, c, h * 64:(h + 1) * 64],
                       c == 0, c == 1, ['ckvn', 'Wuv'], ['ps%d' % pb])
            npart = 128 if g4 < 4 else 32
            cp('dve', Vh[hs][0:npart, g4 * 4:g4 * 4 + ntl, 0:64],
               PS[pb][0:npart, 0:ntl * 64].rearrange('p (a b) -> p a b', a=ntl), ['ps%d' % pb], [tv])
        for c in range(2):
            pb = nxt('pj', 4)
            mm(PS[pb][:, 0:32], WukT[0:64, h, c * 128:(c + 1) * 128], QT[hs][0:64, 2048:2080], True, True,
               ['WukT', tq], ['ps%d' % pb])
            cp('dve', QA[:, :, c, h * 8:(h + 1) * 8], PS[pb][:, 0:32].rearrange('p (s t) -> p s t', s=4),
               ['ps%d' % pb], ['QA'])
        cp('dve', QAp[64:96, :, h * 8:(h + 1) * 8], QT[hs][64:96, 2048:2080].rearrange('p (s t) -> p s t', s=4), [tq], ['QA'])
        for qt in range(4):
            ob = 4 + nxt('ob', 2)
            nk = 4 * qt + 4
            pend = []
            for kt in range(nk):
                d = kt - 4 * qt
                c0 = 128 * d if d > 0 else 0
                sb = 6 + nxt('sb', 2)
                pi = nxt('PT', 3)
                qs = qt * 512
                mm(PS[sb][:, c0:512], KT[hs][0:96, kt * 128:(kt + 1) * 128], QT[hs][0:96, qs + c0:qs + 512], True, True,
                   [tk, tq], ['ps%d' % sb])
                act(PT[pi][:, c0:512], PS[sb][:, c0:512], AF.Exp, ['ps%d' % sb], ['PT%d' % pi], scale=ATTN_SCALE)
                if d >= 0:
                    tt(PT[pi][:, c0:c0 + 128], PT[pi][:, c0:c0 + 128], tri, ALU.mult, ['PT%d' % pi, 'tri'], ['PT%d' % pi])

                def pv(kt=kt, c0=c0, pi=pi):
                    mm(PS[ob][0:65, c0:512], Vh[hs][:, kt, :], PT[pi][:, c0:512], kt == 0, kt == nk - 1,
                       [tv, 'PT%d' % pi], ['ps%d' % ob])
                pend.append(pv)
                if len(pend) > 1:
                    pend.pop(0)()
            while pend:
                pend.pop(0)()
            recip(rsb[64:65, :], PS[ob][64:65, :], ['ps%d' % ob], ['rsb'])
            cp('act', osb[0:64, :], PS[ob][0:64, :], ['ps%d' % ob], ['osb'])
            pb = nxt('pj', 4)
            mm(PS[pb][0:64, :], ones_f[64:65, 0:64], rsb[64:65, :], True, True, ['ones_f', 'rsb'], ['ps%d' % pb])
            if h % 2 == 0:
                tt(attn_o[0:64, h // 2, qs:qs + 512], osb[0:64, :], PS[pb][0:64, :], ALU.mult, ['osb', 'ps%d' % pb], ['attn_o'])
            else:
                tt(otmpb[0:64, :], osb[0:64, :], PS[pb][0:64, :], ALU.mult, ['osb', 'ps%d' % pb], ['otmpb'])
                cp('dve', attn_o[64:128, h // 2, qs:qs + 512], otmpb[0:64, :], ['otmpb'], ['attn_o'])
    pr.barrier()
    AR.off = mSA
    NSL = 4
    ptS = AR.alloc([512], I32); ptF = AR.alloc([512]); pidx = AR.alloc([1]); idxG = AR.alloc([128], I32); idxF = AR.alloc([128])
    dma(ptS, ptab.partition_broadcast(128), [], ['ptS'])
    dma(pidx, k_pidx32, [], ['pidx'])
    cp('dve', ptF, ptS, ['ptS'], ['ptF'])
    for a in range(4):
        rows = slice(32 * a, 32 * a + 32)
        ts(idxF[rows, :], ptF[rows, :].rearrange('p (j a) -> p j a', a=4)[:, :, a], 32.0, pidx[rows, 0:1], ALU.mult, ALU.add,
           ['ptF', 'pidx'], ['idxF'])
    cp('dve', idxG, idxF, ['idxF'], ['idxG'])
    Kl = [AR.alloc([4, 256], BF16) for _ in range(NSL)]; Kp = [AR.alloc([4, 32], BF16) for _ in range(NSL)]
    KTl = [AR.alloc([4, 2, 128], BF16) for _ in range(NSL)]
    KTp = [AR.alloc([4, 128], BF16) for _ in range(NSL)]
    PTs = [AR.alloc([4, 64], BF16) for _ in range(NSL)]
    KnN = AR.alloc([257], BF16); PTn = AR.alloc([64], BF16)
    onb = AR.alloc([256], BF16); olT = AR.alloc([2, 64], BF16); rs1 = AR.alloc([1])
    memset('pool', KnN[0:8, 256:257], 1.0, ['KnN'])
    c_lat_r = c_lat.rearrange('a (r t) d -> (a r) (t d)', t=4); c_pe_r = c_pe.rearrange('a (r t) d -> (a r) (t d)', t=4)

    def page_dma(dst, src_rows, col, w):
        pr.add('pool', lambda e: e.indirect_dma_start(out=dst, out_offset=None, in_=src_rows,
                                                      in_offset=bass.IndirectOffsetOnAxis(ap=idxG[:, col:col + 1], axis=0)),
               ['idxG'], w, dma=True)

    memset('pool', attn_o[:, :, 2048:2080], 0.0, ['attn_o'])
    NSEQ = int(os.environ.get('KSEQ', '4'))

    def st0(s_, g, sl):
        col = s_ * 32 + g
        page_dma(Kl[sl].rearrange('p t d -> p (t d)'), c_lat_r, col, ['Kb%d' % sl])
        page_dma(Kp[sl].rearrange('p t d -> p (t d)'), c_pe_r, col, ['Kb%d' % sl])
        pa = nxt('pj', 4); pbb = nxt('pj', 4)
        psa = PS[pa][:].bitcast(BF16); psp = PS[pbb][:].bitcast(BF16)
        for p in range(4):
            for c in range(2):
                tr(psa[:, (p * 2 + c) * 128:(p * 2 + c + 1) * 128], Kl[sl][:, p, c * 128:(c + 1) * 128], identb,
                   ['Kb%d' % sl, 'identb'], ['ps%d' % pa])
            tr(psp[64:96, p * 128:(p + 1) * 128], Kp[sl][:, p, :], identb, ['Kb%d' % sl, 'identb'], ['ps%d' % pbb])
        cp('act', KTl[sl], psa[:, 0:1024].rearrange('p (a c k) -> p a c k', a=4, c=2), ['ps%d' % pa], ['KTl%d' % sl])
        cp('dve', KTp[sl][64:96, :, :], psp[64:96, 0:512].rearrange('p (a k) -> p a k', a=4), ['ps%d' % pbb], ['KTp%d' % sl])

    def st1(s_, g, sl):
        sb = 6 + nxt('sb', 2)
        for p in range(4):
            o = PS[sb][:, p * 64:(p + 1) * 64]
            mm(o, KTl[sl][:, p, 0, :], QA[:, s_, 0, :], True, False, ['KTl%d' % sl, 'QA'], ['ps%d' % sb])
            mm(o, KTl[sl][:, p, 1, :], QA[:, s_, 1, :], False, False, ['KTl%d' % sl, 'QA'], ['ps%d' % sb])
            mm(o, KTp[sl][64:96, p, :], QAp[64:96, s_, :], False, True, ['KTp%d' % sl, 'QA'], ['ps%d' % sb])
        act(PTs[sl], PS[sb][:, 0:256].rearrange('p (a q) -> p a q', a=4), AF.Exp, ['ps%d' % sb], ['PTs%d' % sl],
            scale=ATTN_SCALE)

    def st2(s_, g, sl):
        for p in range(4):
            mm(PS[OBS[s_ % 2]][0:64, 0:256], PTs[sl][:, p, :], Kl[sl][:, p, :], g == 0 and p == 0, False,
               ['PTs%d' % sl, 'Kb%d' % sl], ['ps%d' % OBS[s_ % 2]])
            mm(PS[OBS[s_ % 2]][0:64, 256:257], PTs[sl][:, p, :], ones_b[:, 0:1], False, False,
               ['PTs%d' % sl, 'ones_b'], ['ps%d' % OBS[s_ % 2]])
        if g == 31:
            fin(s_)

    OBS = [4, 5]
    groups = [(s_, g, i % NSL) for i, (s_, g) in enumerate((s_, g) for s_ in range(NSEQ) for g in range(32))]

    def fin(s_):
        OB = OBS[s_ % 2]
        c0 = 2048 + 8 * s_
        pb = nxt('pj', 4)
        psb = PS[pb][:].bitcast(BF16)
        for c in range(2):
            tr(psb[0:8, c * 128:(c + 1) * 128], ckvn[:, c, c0:c0 + 8], identb, ['ckvn', 'identb'], ['ps%d' % pb])
        cp('dve', KnN[0:8, 0:256], psb[0:8, 0:256], ['ps%d' % pb], ['KnN'])
        sb = 6 + nxt('sb', 2)
        mm(PS[sb][0:8, 0:64], ckvn[:, 0, c0:c0 + 8], QA[:, s_, 0, :], True, False, ['ckvn', 'QA'], ['ps%d' % sb])
        mm(PS[sb][0:8, 0:64], ckvn[:, 1, c0:c0 + 8], QA[:, s_, 1, :], False, False, ['ckvn', 'QA'], ['ps%d' % sb])
        mm(PS[sb][0:8, 0:64], KpeT[64:96, c0:c0 + 8], QAp[64:96, s_, :], False, True, ['KpeT', 'QA'], ['ps%d' % sb])
        act(PTn[0:8, :], PS[sb][0:8, 0:64], AF.Exp, ['ps%d' % sb], ['PTn'], scale=ATTN_SCALE)
        tt(PTn[0:8, :], PTn[0:8, :], mskn[0:8, :], ALU.mult, ['PTn', 'mskn'], ['PTn'])
        mm(PS[OB][0:64, 0:257], PTn[0:8, :], KnN[0:8, 0:257], False, True, ['PTn', 'KnN'], ['ps%d' % OB])
        recip(rs1[0:64, :], PS[OB][0:64, 256:257], ['ps%d' % OB], ['rs1'])
        ts(onb[0:64, :], PS[OB][0:64, 0:256], rs1[0:64, 0:1], None, ALU.mult, None, ['ps%d' % OB, 'rs1'], ['onb'])
        pb = nxt('pj', 4)
        psb = PS[pb][:].bitcast(BF16)
        for c in range(2):
            tr(psb[:, c * 64:(c + 1) * 64], onb[0:64, c * 128:(c + 1) * 128], identb[0:64, 0:64], ['onb', 'identb'], ['ps%d' % pb])
        cp('dve', olT, psb[:, 0:128].rearrange('p (c q) -> p c q', c=2), ['ps%d' % pb], ['olT'])
        pb = nxt('pj', 4)
        for h in range(8):
            for c in range(2):
                mm(PS[pb][(h % 2) * 64:(h % 2) * 64 + 64, (h // 2) * 8:(h // 2) * 8 + 8], Wuv[:, c, h * 64:(h + 1) * 64],
                   olT[:, c, h * 8:(h + 1) * 8], c == 0, c == 1, ['Wuv', 'olT'], ['ps%d' % pb])
        cp('dve', attn_o[:, :, c0:c0 + 8], PS[pb][:, 0:32].rearrange('p (a t) -> p a t', a=4), ['ps%d' % pb], ['attn_o'])

    for i in range(len(groups) + 2):
        if i < len(groups):
            st0(*groups[i])
        if 0 <= i - 1 < len(groups):
            st1(*groups[i - 1])
        if 0 <= i - 2 < len(groups):
            st2(*groups[i - 2])
    if debug:
        dbg_attn = dout('dbg_attn', [128, 4, NTOK], BF16)
        dma_out(dbg_attn, attn_o, ['attn_o'])
    pr.barrier()
    AR.off = mATT

    class StopS5(Exception):
        pass

    try:
        LVL = int(os.environ.get('KLVL', '99'))
        ssm_o = AR.alloc([4, NTOK], BF16)
        mSSM = AR.off
        BreT = AR.alloc([16, 128], BF16); BimT = AR.alloc([16, 128], BF16)
        CreT = AR.alloc([16, 128], BF16); NCreT = AR.alloc([16, 128], BF16); NCimT = AR.alloc([16, 128], BF16)
        prm = AR.alloc([24, 16])
        H0re = AR.alloc([4, 16]); H0im = AR.alloc([4, 16]); HSre = AR.alloc([16]); HSim = AR.alloc([16])
        HSsre = AR.alloc([4, 16]); HSsim = AR.alloc([4, 16]); smT = AR.alloc([32]); rhoS = AR.alloc([32])
        cry = AR.alloc([4]); t4a = AR.alloc([4]); t4b = AR.alloc([4]); ah_re = AR.alloc([4]); ah_im = AR.alloc([4]); c1t = AR.alloc([4])
        PN = ['are', 'aim', 'ldt', 'dt', 'lam', 'th', 'rho', 'sth', 'cth', 'abr', 'abi', 'den', 'rden', 'nr', 'fr', 'fi',
              'u1', 'u2', 'u3', 'u4', 'a512', 's512', 'c512']
        P_ = {nm: prm[:, i, :] for i, nm in enumerate(PN)}
        mS5 = AR.off
        INV2PI = 1.0 / (2 * math.pi); MAGIC = 12582912.0; C1 = 6.28125; C2 = 2 * math.pi - 6.28125

        def sincos(s_out, c_out, ang, ta, tb, r, w_s, w_c, tok):
            ts(ta, ang, INV2PI, None, ALU.mult, None, r, [tok + 'a'])
            ts(ta, ta, MAGIC, None, ALU.add, None, [tok + 'a'], [tok + 'a'])
            ts(ta, ta, -MAGIC, None, ALU.add, None, [tok + 'a'], [tok + 'a'])
            stt(tb, ta, -C1, ang, ALU.mult, ALU.add, list(r) + [tok + 'a'], [tok + 'b'])
            stt(tb, ta, -C2, tb, ALU.mult, ALU.add, [tok + 'a', tok + 'b'], [tok + 'b'])
            ts(tb, tb, math.pi, -math.pi, ALU.min, ALU.max, [tok + 'b'], [tok + 'b'])
            act(s_out, tb, AF.Sin, [tok + 'b'], w_s)
            stt(ta, tb, -1.0, tb, ALU.mult, ALU.max, [tok + 'b'], [tok + 'a'])
            act(c_out, ta, AF.Sin, [tok + 'a', 'hpiT'], w_c, scale=-1.0, bias=hpiT)

        araw = AR.alloc([3, 128]); ldr = AR.alloc([2])
        dma(araw[0:16, 0, :], a_re, [], ['araw']); dma(araw[0:16, 1, :], a_im, [], ['araw'])
        dma(ldr[0:16, :], log_dt, [], ['ldr'])
        cp('dve', araw[0:16, 2, :].rearrange('p (g n) -> p g n', g=2), ldr[0:16, :].unsqueeze(2).to_broadcast([16, 2, 64]),
           ['ldr', 'araw'], ['araw'])
        pb = nxt('pj', 4)
        for i in range(3):
            tr(PS[pb][:, i * 16:(i + 1) * 16], araw[0:16, i, :], ident[0:16, 0:16], ['araw', 'ident'], ['ps%d' % pb])
        cp('dve', prm[:, 0:3, :], PS[pb][:, 0:48].rearrange('p (a b) -> p a b', a=3), ['ps%d' % pb], ['prm'])
        TP = ['prm']
        act(P_['dt'], P_['ldt'], AF.Exp, TP, TP)
        tt(P_['lam'], P_['dt'], P_['are'], ALU.mult, TP, TP)
        tt(P_['th'], P_['dt'], P_['aim'], ALU.mult, TP, TP)
        act(P_['rho'], P_['lam'], AF.Exp, TP, TP)
        sincos(P_['sth'], P_['cth'], P_['th'], P_['u1'], P_['u2'], TP, TP, TP, 'prm')
        ts(P_['a512'], P_['th'], 512.0, None, ALU.mult, None, TP, TP)
        sincos(P_['s512'], P_['c512'], P_['a512'], P_['u3'], P_['u4'], TP, TP, TP, 'prm')
        tt(P_['abr'], P_['rho'], P_['cth'], ALU.mult, TP, TP)
        tt(P_['abi'], P_['rho'], P_['sth'], ALU.mult, TP, TP)
        tt(P_['u1'], P_['are'], P_['are'], ALU.mult, TP, TP)
        tt(P_['u2'], P_['aim'], P_['aim'], ALU.mult, TP, TP)
        tt(P_['den'], P_['u1'], P_['u2'], ALU.add, TP, TP)
        recip(P_['rden'], P_['den'], TP, TP)
        ts(P_['nr'], P_['abr'], -1.0, None, ALU.add, None, TP, TP)
        tt(P_['u1'], P_['nr'], P_['are'], ALU.mult, TP, TP)
        tt(P_['u2'], P_['abi'], P_['aim'], ALU.mult, TP, TP)
        tt(P_['u1'], P_['u1'], P_['u2'], ALU.add, TP, TP)
        tt(P_['fr'], P_['u1'], P_['rden'], ALU.mult, TP, TP)
        tt(P_['u1'], P_['abi'], P_['are'], ALU.mult, TP, TP)
        tt(P_['u2'], P_['nr'], P_['aim'], ALU.mult, TP, TP)
        tt(P_['u1'], P_['u1'], P_['u2'], ALU.subtract, TP, TP)
        tt(P_['fi'], P_['u1'], P_['rden'], ALU.mult, TP, TP)
        if LVL == 0:
            raise StopS5()
        sraw = AR.alloc([2, 128])
        dma(sraw[0:64, 0, :], st_re, [], ['sraw']); dma(sraw[0:64, 1, :], st_im, [], ['sraw'])
        pb = nxt('pj', 4)
        tr(PS[pb][:, 0:64], sraw[0:64, 0, :], ident[0:64, 0:64], ['sraw', 'ident'], ['ps%d' % pb])
        tr(PS[pb][:, 64:128], sraw[0:64, 1, :], ident[0:64, 0:64], ['sraw', 'ident'], ['ps%d' % pb])
        cp('dve', H0re, PS[pb][:, 0:64].rearrange('p (s r) -> p s r', s=4), ['ps%d' % pb], ['H0'])
        cp('dve', H0im, PS[pb][:, 64:128].rearrange('p (s r) -> p s r', s=4), ['ps%d' % pb], ['H0'])
        dma(smT, k_smask.partition_broadcast(128), [], ['smT'])
        if LVL == -1:
            raise StopS5()
        Braw_re = AR.alloc([16, 16]); Braw_im = AR.alloc([16, 16]); t16a = AR.alloc([16]); t16b = AR.alloc([16])
        Bexp_re = AR.alloc([16, 128]); Bexp_im = AR.alloc([16, 128])
        dma(Braw_re, b_re.rearrange('(r q) c -> q r c', q=128), [], ['Braw'])
        dma(Braw_im, b_im.rearrange('(r q) c -> q r c', q=128), [], ['Braw'])
        memset('pool', Bexp_re, 0.0, ['Bexp']); memset('pool', Bexp_im, 0.0, ['Bexp'])
        for pr_ in range(16 if LVL >= 2 else 0):
            for gi in range(2):
                rows = slice(64 * gi, 64 * gi + 64)
                col0 = (pr_ % 4) * 32 + gi * 16
                fr_, fi_ = P_['fr'][rows, pr_:pr_ + 1], P_['fi'][rows, pr_:pr_ + 1]
                ts(t16a[rows, :], Braw_im[rows, pr_, :], fi_, None, ALU.mult, None, ['Braw', 'prm'], ['t16a'])
                stt(Bexp_re[rows, pr_, col0:col0 + 16], Braw_re[rows, pr_, :], fr_, t16a[rows, :], ALU.mult, ALU.subtract,
                    ['Braw', 'prm', 't16a'], ['Bexp'])
                ts(t16b[rows, :], Braw_re[rows, pr_, :], fi_, None, ALU.mult, None, ['Braw', 'prm'], ['t16b'])
                stt(Bexp_im[rows, pr_, col0:col0 + 16], Braw_im[rows, pr_, :], fr_, t16b[rows, :], ALU.mult, ALU.add,
                    ['Braw', 'prm', 't16b'], ['Bexp'])
        for src_, dst_, tok in ((Bexp_re, BreT, 'BreT'), (Bexp_im, BimT, 'BimT')):
            for q4 in range(4):
                pb = nxt('pj', 4)
                for j in range(4):
                    tr(PS[pb][:, j * 128:(j + 1) * 128], src_[:, q4 * 4 + j, :], ident, ['Bexp', 'ident'], ['ps%d' % pb])
                cp('act', dst_[:, q4 * 4:q4 * 4 + 4, :], PS[pb][:].rearrange('p (a b) -> p a b', a=4), ['ps%d' % pb], [tok])
        pr.barrier()
        AR.off = mS5
        if LVL == -2:
            raise StopS5()
        X_re = AR.alloc([4, 512]); X_im = AR.alloc([4, 512])
        memset('pool', X_re, 0.0, ['X_re']); memset('pool', X_im, 0.0, ['X_im'])
        for g_ in range(32 if LVL >= 3 else 0):
            r0 = 16 * (g_ % 8)
            cc0 = ((g_ % 8) // 2) * 128 + (g_ % 2) * 64
            dma(X_re[r0:r0 + 16, g_ // 8, cc0:cc0 + 64], cc_re[g_], [], ['X_re'])
            dma(X_im[r0:r0 + 16, g_ // 8, cc0:cc0 + 64], cc_im[g_], [], ['X_im'])
        for q4 in range(4):
            pb = nxt('pj', 4)
            for j in range(4):
                tr(PS[pb][:, j * 128:(j + 1) * 128], X_re[:, q4, j * 128:(j + 1) * 128], ident, ['X_re', 'ident'], ['ps%d' % pb])
            v = PS[pb][:].rearrange('p (a b) -> p a b', a=4)
            cp('act', CreT[:, q4 * 4:q4 * 4 + 4, :], v, ['ps%d' % pb], ['CreT'])
            act(NCreT[:, q4 * 4:q4 * 4 + 4, :], v, AF.Copy, ['ps%d' % pb], ['NCreT'], scale=-1.0)
            pb = nxt('pj', 4)
            for j in range(4):
                tr(PS[pb][:, j * 128:(j + 1) * 128], X_im[:, q4, j * 128:(j + 1) * 128], ident, ['X_im', 'ident'], ['ps%d' % pb])
            act(NCimT[:, q4 * 4:q4 * 4 + 4, :], PS[pb][:].rearrange('p (a b) -> p a b', a=4), AF.Copy, ['ps%d' % pb], ['NCimT'],
                scale=-1.0)
        pr.barrier()
        AR.off = mS5
        if LVL == -3:
            raise StopS5()
        cosT2 = [AR.alloc([512]) for _ in range(2)]; sinT2 = [AR.alloc([512]) for _ in range(2)]; iotaT = AR.alloc([512])
        tg_ang = AR.alloc([512]); tg_a = AR.alloc([512]); tg_b = AR.alloc([512])
        t_ang = AR.alloc([1024]); t_a = AR.alloc([1024]); t_b = AR.alloc([1024])
        Sre = [AR.alloc([512]) for _ in range(2)]; Sim = [AR.alloc([512]) for _ in range(2)]
        Zb = AR.alloc([4, 512], BF16)
        dma(iotaT, k_iota[0:1, 0:512].partition_broadcast(128), [], ['iotaT'])
        m1, m2, g_re, g_im = t_a[:, 0:512], t_a[:, 512:1024], t_b[:, 0:512], t_b[:, 512:1024]
        m3, m4 = t_ang[:, 0:512], t_ang[:, 512:1024]
        G2 = [t_b, AR.alloc([1024])]
        YB = [0, 1, 2, 3, 4]

        def gen_tables(p_):
            ts(tg_ang, iotaT, P_['th'][:, p_:p_ + 1], None, ALU.mult, None, ['iotaT', 'prm'], ['tg_ang'])
            sincos(sinT2[p_ % 2], cosT2[p_ % 2], tg_ang, tg_a, tg_b, ['tg_ang'], ['sinT%d' % (p_ % 2)], ['cosT%d' % (p_ % 2)], 'tg_')
        for pr_ in range({4: 1, 5: 4}.get(LVL, 16) if LVL >= 4 else 0):
            qc = pr_ // 4
            thp = P_['th'][:, pr_:pr_ + 1]
            rho_p = P_['rho'][:, pr_:pr_ + 1]
            if pr_ == 0:
                gen_tables(0)
            if pr_ + 1 < 16:
                gen_tables(pr_ + 1)
            cosT, sinT = cosT2[pr_ % 2], sinT2[pr_ % 2]
            tcs, tsn = 'cosT%d' % (pr_ % 2), 'sinT%d' % (pr_ % 2)
            ts(rhoS, smT, rho_p, None, ALU.mult, None, ['smT', 'prm'], ['rhoS'])
            state = {'prev': None}

            def s5pre(bi):
                off, n = BLOCKS[bi]
                g_re, g_im = G2[bi % 2][:, 0:512], G2[bi % 2][:, 512:1024]
                tgb = 't_b%d' % (bi % 2)
                mm(PS[5][:, 0:n], BreT[:, pr_, :], uT[:, qc, off:off + n], True, True, ['BreT', 'uT'], ['ps5'])
                mm(PS[6][:, 0:n], BimT[:, pr_, :], uT[:, qc, off:off + n], True, True, ['BimT', 'uT'], ['ps6'])
                if bi < 4:
                    cs, sn = cosT[:, 0:n], sinT[:, 0:n]
                    vw = lambda a: a
                else:
                    cs = cosT[:, 0:8].unsqueeze(1).to_broadcast([128, 4, 8])
                    sn = sinT[:, 0:8].unsqueeze(1).to_broadcast([128, 4, 8])
                    vw = lambda a: a.rearrange('p (s t) -> p s t', s=4)
                TB = [tcs, tsn]
                PE_ = 'pool' if (bi < 4 and os.environ.get('KS5POOL', '0') == '1') else 'dve'
                tt(vw(m1[:, 0:n]), vw(PS[5][:, 0:n]), cs, ALU.mult, ['ps5'] + TB, ['t_a'])
                tt(vw(m2[:, 0:n]), vw(PS[6][:, 0:n]), sn, ALU.mult, ['ps6'] + TB, ['t_a'])
                tt(g_re[:, 0:n], m1[:, 0:n], m2[:, 0:n], ALU.add, ['t_a'], [tgb], eng=PE_)
                tt(vw(m3[:, 0:n]), vw(PS[6][:, 0:n]), cs, ALU.mult, ['ps6'] + TB, ['t_ang'])
                tt(vw(m4[:, 0:n]), vw(PS[5][:, 0:n]), sn, ALU.mult, ['ps5'] + TB, ['t_ang'])
                tt(g_im[:, 0:n], m3[:, 0:n], m4[:, 0:n], ALU.subtract, ['t_ang'], [tgb], eng=PE_)

            def s5post(bi):
                off, n = BLOCKS[bi]
                g_re, g_im = G2[bi % 2][:, 0:512], G2[bi % 2][:, 512:1024]
                tgb = 't_b%d' % (bi % 2)
                if bi < 4:
                    cs, sn = cosT[:, 0:n], sinT[:, 0:n]
                    vw = lambda a: a
                else:
                    cs = cosT[:, 0:8].unsqueeze(1).to_broadcast([128, 4, 8])
                    sn = sinT[:, 0:8].unsqueeze(1).to_broadcast([128, 4, 8])
                    vw = lambda a: a.rearrange('p (s t) -> p s t', s=4)
                TB = [tcs, tsn]
                PE_ = 'dve'
                prev = state['prev']
                sl = bi % 2
                if bi < 4:
                    d0 = rho_p.to_broadcast([128, n])
                    if prev is None:
                        i_re = i_im = 0.0
                        rr = [tgb, 'prm']
                    else:
                        c5, s5 = P_['c512'][:, pr_:pr_ + 1], P_['s512'][:, pr_:pr_ + 1]
                        pr_l, pi_l = Sre[prev][:, 511:512], Sim[prev][:, 511:512]
                        ts(cry[:, 2:3], pi_l, s5, None, ALU.mult, None, ['S%d' % prev, 'prm'], ['cry'])
                        stt(cry[:, 0:1], pr_l, c5, cry[:, 2:3], ALU.mult, ALU.subtract, ['S%d' % prev, 'prm', 'cry'], ['cry'])
                        ts(cry[:, 3:4], pr_l, s5, None, ALU.mult, None, ['S%d' % prev, 'prm'], ['cry'])
                        stt(cry[:, 1:2], pi_l, c5, cry[:, 3:4], ALU.mult, ALU.add, ['S%d' % prev, 'prm', 'cry'], ['cry'])
                        i_re, i_im = cry[:, 0:1], cry[:, 1:2]
                        rr = [tgb, 'prm', 'cry']
                else:
                    ts(t4a, H0im[:, :, pr_], P_['abi'][:, pr_:pr_ + 1], None, ALU.mult, None, ['H0', 'prm'], ['t4a'])
                    stt(ah_re, H0re[:, :, pr_], P_['abr'][:, pr_:pr_ + 1], t4a, ALU.mult, ALU.subtract, ['H0', 'prm', 't4a'], ['ah'])
                    ts(t4b, H0re[:, :, pr_], P_['abi'][:, pr_:pr_ + 1], None, ALU.mult, None, ['H0', 'prm'], ['t4b'])
                    stt(ah_im, H0im[:, :, pr_], P_['abr'][:, pr_:pr_ + 1], t4b, ALU.mult, ALU.add, ['H0', 'prm', 't4b'], ['ah'])
                    gv_re = g_re[:, 0:32].rearrange('p (s t) -> p s t', s=4)[:, :, 0]
                    gv_im = g_im[:, 0:32].rearrange('p (s t) -> p s t', s=4)[:, :, 0]
                    tt(gv_re, gv_re, ah_re, ALU.add, [tgb, 'ah'], [tgb])
                    tt(gv_im, gv_im, ah_im, ALU.add, [tgb, 'ah'], [tgb])
                    d0 = rhoS[:, 0:32]
                    i_re = i_im = 0.0
                    rr = [tgb, 'rhoS']
                scan(Sre[sl][:, 0:n], d0, g_re[:, 0:n], i_re, rr, ['S%d' % sl])
                scan(Sim[sl][:, 0:n], d0, g_im[:, 0:n], i_im, rr, ['S%d' % sl])
                TS = ['S%d' % sl] + TB
                tt(vw(Zb[:, 0, 0:n]), vw(Sre[sl][:, 0:n]), cs, ALU.mult, TS, ['Zb'], eng=PE_)
                tt(vw(Zb[:, 1, 0:n]), vw(Sim[sl][:, 0:n]), sn, ALU.mult, TS, ['Zb'], eng=PE_)
                tt(vw(Zb[:, 2, 0:n]), vw(Sim[sl][:, 0:n]), cs, ALU.mult, TS, ['Zb'], eng=PE_)
                tt(vw(Zb[:, 3, 0:n]), vw(Sre[sl][:, 0:n]), sn, ALU.mult, TS, ['Zb'], eng=PE_)
                for k, (W, wt_) in enumerate(((CreT, 'CreT'), (NCreT, 'NCreT'), (NCimT, 'NCimT'), (NCimT, 'NCimT'))):
                    mm(PS[YB[bi]][:, 0:n], W[:, pr_, :], Zb[:, k, 0:n], pr_ % 4 == 0 and k == 0, pr_ % 4 == 3 and k == 3,
                       [wt_, 'Zb'], ['ps%d' % YB[bi]])
                if bi == 3:
                    cl, sl_ = cosT[:, 511:512], sinT[:, 511:512]
                    sr, si = Sre[sl][:, 511:512], Sim[sl][:, 511:512]
                    tt(c1t[:, 0:1], cl, sr, ALU.mult, TS, ['c1t']); tt(c1t[:, 1:2], sl_, si, ALU.mult, TS, ['c1t'])
                    tt(HSre[:, pr_:pr_ + 1], c1t[:, 0:1], c1t[:, 1:2], ALU.subtract, ['c1t'], ['HS'])
                    tt(c1t[:, 2:3], cl, si, ALU.mult, TS, ['c1t']); tt(c1t[:, 3:4], sl_, sr, ALU.mult, TS, ['c1t'])
                    tt(HSim[:, pr_:pr_ + 1], c1t[:, 2:3], c1t[:, 3:4], ALU.add, ['c1t'], ['HS'])
                if bi == 4:
                    sr = Sre[sl][:, 0:32].rearrange('p (s t) -> p s t', s=4)[:, :, 7]
                    si = Sim[sl][:, 0:32].rearrange('p (s t) -> p s t', s=4)[:, :, 7]
                    c7, s7 = cosT[:, 7:8], sinT[:, 7:8]
                    ts(t4a, si, s7, None, ALU.mult, None, TS, ['t4a'])
                    stt(HSsre[:, :, pr_], sr, c7, t4a, ALU.mult, ALU.subtract, TS + ['t4a'], ['HSs'])
                    ts(t4b, sr, s7, None, ALU.mult, None, TS, ['t4b'])
                    stt(HSsim[:, :, pr_], si, c7, t4b, ALU.mult, ALU.add, TS + ['t4b'], ['HSs'])
                state['prev'] = sl if bi < 4 else None

            s5pre(0)
            for bi in range(5):
                if bi + 1 < 5:
                    s5pre(bi + 1)
                s5post(bi)
            if pr_ % 4 == 3:
                for bi, (off, n) in enumerate(BLOCKS):
                    yf, x2 = t_ang[:, 0:n], t_ang[:, 512:512 + n]
                    stt(yf, uT[:, qc, off:off + n], G[:, GO['d'] + qc:GO['d'] + qc + 1], PS[YB[bi]][:, 0:n], ALU.mult, ALU.add,
                        ['uT', 'G', 'ps%d' % YB[bi]], ['t_ang'])
                    act(x2, yf, AF.Square, ['t_ang'], ['t_ang'])
                    ts(x2, x2, 0.044715, 1.0, ALU.mult, ALU.add, ['t_ang'], ['t_ang'])
                    tt(x2, x2, yf, ALU.mult, ['t_ang'], ['t_ang'])
                    act(x2, x2, AF.Sigmoid, ['t_ang'], ['t_ang'], scale=2.0 * math.sqrt(2.0 / math.pi))
                    tt(ssm_o[:, qc, off:off + n], yf, x2, ALU.mult, ['t_ang'], ['ssm_o'])
        if LVL == -4:
            raise StopS5()
        hso = t_ang[:, 0:512].rearrange('p (a b) -> p a b', a=4)
        pb = nxt('pj', 4)
        tr(PS[pb][0:16, 0:128], HSre, ident, ['HS', 'ident'], ['ps%d' % pb])
        tr(PS[pb][0:16, 128:256], HSim, ident, ['HS', 'ident'], ['ps%d' % pb])
        tr(PS[pb][0:64, 256:384], HSsre[:].rearrange('p s r -> p (s r)'), ident, ['HSs', 'ident'], ['ps%d' % pb])
        tr(PS[pb][0:64, 384:512], HSsim[:].rearrange('p s r -> p (s r)'), ident, ['HSs', 'ident'], ['ps%d' % pb])
        cp('act', hso[0:64, :, :], PS[pb][0:64, :].rearrange('p (a b) -> p a b', a=4), ['ps%d' % pb], ['t_ang'])
        dma_out(o_sre_p, hso[0:16, 0, :], ['t_ang']); dma_out(o_sim_p, hso[0:16, 1, :], ['t_ang'])
        dma_out(o_sre_s, hso[0:64, 2, :], ['t_ang']); dma_out(o_sim_s, hso[0:64, 3, :], ['t_ang'])
        pr.barrier()
        AR.off = mS5
        if LVL == -5:
            raise StopS5()
        Wglu = AR.alloc([4, 512], BF16); gate = AR.alloc([4, 512], BF16)
        cast_load(Wglu, w_glu.rearrange('(c p) n -> p c n', p=128), 'Wglu')
        for bi, (off, n) in enumerate(BLOCKS):
            for oc in range(4):
                pb = nxt('pj', 4)
                for c in range(4):
                    mm(PS[pb][:, 0:n], Wglu[:, c, oc * 128:(oc + 1) * 128], ssm_o[:, c, off:off + n], c == 0, c == 3,
                       ['Wglu', 'ssm_o'], ['ps%d' % pb])
                act(gate[:, oc, 0:n], PS[pb][:, 0:n], AF.Sigmoid, ['ps%d' % pb], ['gate'])
            for oc in range(4):
                tt(ssm_o[:, oc, off:off + n], ssm_o[:, oc, off:off + n], gate[:, oc, 0:n], ALU.mult, ['ssm_o', 'gate'], ['ssm_o'])
        if debug:
            dbg_ssm = dout('dbg_ssm', [128, 4, NTOK], BF16)
            dma_out(dbg_ssm, ssm_o, ['ssm_o'])
        pr.barrier()
        AR.off = mS5


    except StopS5:
        pr.barrier()
        AR.off = mS5

    mM0 = AR.off
    if '2' in KOLD:
        KmT = AR.alloc([4, 256], BF16); Vm = AR.alloc([2, 512], BF16)
    memtm = AR.alloc([2, 1024]); memT = AR.alloc([8, 256]); mnT = AR.alloc([8, 256], BF16)
    Wkm = AR.alloc([8, 512], BF16); Wvm = AR.alloc([8, 512], BF16); mko = AR.alloc([2, 512])
    cast_load(Wkm, w_km.rearrange('(c p) n -> p c n', p=128), 'Wkm')
    cast_load(Wvm, w_vm.rearrange('(c p) n -> p c n', p=128), 'Wvm')
    for t in range(2):
        dma(memtm[:, t, :], mem_p[t * 128:(t + 1) * 128, :], [], ['memtm%d' % t])
        for half in range(2):
            pb = nxt('pj', 4)
            for c4 in range(4):
                c = half * 4 + c4
                tr(PS[pb][:, c4 * 128:(c4 + 1) * 128], memtm[:, t, c * 128:(c + 1) * 128], ident, ['memtm%d' % t, 'ident'],
                   ['ps%d' % pb])
            cp('act' if half else 'dve', memT[:, half * 4:half * 4 + 4, t * 128:(t + 1) * 128],
               PS[pb][:].rearrange('p (a b) -> p a b', a=4), ['ps%d' % pb], ['memT'])
    NORM(memT, 8, 256, 'mem', 1024, mnT, ['memT'], ['mnT'])
    for t in range(2):
        for wi, (W, wtok, dst) in enumerate(((Wkm, 'Wkm', o_mk_p), (Wvm, 'Wvm', o_mv_p))):
            pb = nxt('pj', 4)
            sl = nxt('mko', 2)
            for k in range(8):
                mm(PS[pb][:, :], mnT[:, k, t * 128:(t + 1) * 128], W[:, k, :], k == 0, k == 7, ['mnT', wtok], ['ps%d' % pb])
            cp('act', mko[:, sl, :], PS[pb][:, :], ['ps%d' % pb], ['mko%d' % sl])
            if wi == 1 and os.environ.get('KM0', '1') == '1':
                cp('dve', Vm[:, t, :], mko[:, sl, :], ['mko%d' % sl], ['Vm'])
            dma_out(dst[t * 128:(t + 1) * 128, :], mko[:, sl, :], ['mko%d' % sl])
    for hd in range(4 if os.environ.get('KM0', '1') == '1' else 0):
        pb = nxt('pj', 4)
        for k in range(8):
            mm(PS[pb][:, 0:256], Wkm[:, k, hd * 128:(hd + 1) * 128], mnT[:, k, :], k == 0, k == 7, ['Wkm', 'mnT'], ['ps%d' % pb])
        cp('act', KmT[:, hd, :], PS[pb][:, 0:256], ['ps%d' % pb], ['KmT'])
    pr.barrier()
    AR.off = mM0

    class StopX(Exception):
        pass

    KCUT = int(os.environ.get('KCUT', '99'))
    try:
        AR.off = mSSM
        if KCUT == 0:
            raise StopX()
        Wout = AR.alloc([8, 1024], BF16)
        mixin2 = [AR.alloc([8, 512], BF16) for _ in range(2)]; f_sb2 = [AR.alloc([8, 512])] * 2
        NSB = norm_set()
        cast_load(Wout, w_out.rearrange('(c p) n -> p c n', p=128), 'Wout')
        SKIP = os.environ.get('KSKIP', '')

        def mixA(bi):
            off, n = BLOCKS[bi]
            NORM(attn_o[:, :, off:off + n], 4, n, 'attn', 512, mixin2[bi % 2][:, 0:4, 0:n], ['attn_o'], ['mixin%d' % (bi % 2)])
            NORM(ssm_o[:, :, off:off + n], 4, n, 'ssm', 512, mixin2[bi % 2][:, 4:8, 0:n], ['ssm_o'], ['mixin%d' % (bi % 2)])

        def mixB(bi):
            off, n = BLOCKS[bi]
            for oc in range(8):
                pb = nxt('pj', 4)
                for k in range(8):
                    mm(PS[pb][:, 0:n], Wout[:, k, oc * 128:(oc + 1) * 128], mixin2[bi % 2][:, k, 0:n], k == 0, k == 7,
                       ['Wout', 'mixin%d' % (bi % 2)], ['ps%d' % pb])
                cp('act' if oc % 2 else 'dve', f_sb2[bi % 2][:, oc, 0:n], PS[pb][:, 0:n], ['ps%d' % pb], ['f_sbm'])

        def mixC(bi):
            off, n = BLOCKS[bi]
            NORM(f_sb2[bi % 2][:, :, 0:n], 8, n, 'mix_post', 1024, None, ['f_sbm'], [t_xT[bi]],
                 resid=xT[:, :, off:off + n], ns=NSB)

        if 'x' not in SKIP:
            mixA(0)
            for bi in range(5):
                if bi + 1 < 5:
                    mixA(bi + 1)
                mixB(bi)
                mixC(bi)
        pr.barrier()
        AR.off = mUT

        if KCUT == 1:
            raise StopX()
        Wqm = AR.alloc([8, 512], BF16); Wom = AR.alloc([4, 1024], BF16)
        KmTs = AR.alloc([4, 4, 256], BF16); Vms = AR.alloc([4, 2, 512], BF16)
        mkr = AR.alloc([2, 2, 512])
        hT2 = AR.alloc([8, 512], BF16); qmT = AR.alloc([4, 512], BF16); omT = AR.alloc([4, 512], BF16)
        osm = AR.alloc([512]); rsm = AR.alloc([512]); f_sb = AR.alloc([8, 512])
        cast_load(Wqm, w_qm.rearrange('(c p) n -> p c n', p=128), 'Wqm')
        cast_load(Wom, w_om.rearrange('(c p) n -> p c n', p=128), 'Wom')
        for s_ in range(4):
            sl = s_ % 2
            dma(mkr[:, sl, :, :], memk[s_].rearrange('(t p) d -> p t d', p=128), [], ['mkr%d' % sl])
            cast_load(Vms[:, s_, :, :], memv[s_].rearrange('(t p) d -> p t d', p=128), 'Vms')
            for t in range(2):
                pb = nxt('pj', 4)
                for hd in range(4):
                    tr(PS[pb][:, hd * 128:(hd + 1) * 128], mkr[:, sl, t, hd * 128:(hd + 1) * 128], ident, ['mkr%d' % sl, 'ident'],
                       ['ps%d' % pb])
                cp('act' if t else 'dve', KmTs[:, s_, :, t * 128:(t + 1) * 128], PS[pb][:].rearrange('p (a b) -> p a b', a=4),
                   ['ps%d' % pb], ['KmTs'])
        if KCUT == 2:
            raise StopX()
        hT2b = [hT2, AR.alloc([8, 512], BF16)]; qmTb = [qmT, AR.alloc([4, 512], BF16)]
        PTm = [AR.alloc([2, 512], BF16) for _ in range(2)]
        NSB2 = norm_set()

        def memA(bi):
            off, n = BLOCKS[bi]
            h2, q2 = hT2b[bi % 2], qmTb[bi % 2]
            NORM(xT[:, :, off:off + n], 8, n, 'mem_pre', 1024, h2[:, :, 0:n], [t_xT[bi]], ['hT2%d' % (bi % 2)])
            for hd in range(4):
                pb = nxt('pj', 4)
                for k in range(8):
                    mm(PS[pb][:, 0:n], Wqm[:, k, hd * 128:(hd + 1) * 128], h2[:, k, 0:n], k == 0, k == 7,
                       ['Wqm', 'hT2%d' % (bi % 2)], ['ps%d' % pb])
                cp('act', q2[:, hd, 0:n], PS[pb][:, 0:n], ['ps%d' % pb], ['qmT%d' % (bi % 2)])

        def memS(bi, hd):
            off, n = BLOCKS[bi]
            q2, tq2 = qmTb[bi % 2], 'qmT%d' % (bi % 2)
            pi = hd % 2
            if bi < 4:
                for t in range(2):
                    sb = 6 + t
                    mm(PS[sb][:, 0:n], KmT[:, hd, t * 128:(t + 1) * 128], q2[:, hd, 0:n], True, True, ['KmT', tq2], ['ps%d' % sb])
                    act(PTm[pi][:, t, 0:n], PS[sb][:, 0:n], AF.Exp, ['ps%d' % sb], ['PTm%d' % pi], scale=MEM_SCALE)
            else:
                sb = 6 + hd % 2
                for s_ in range(4):
                    for t in range(2):
                        c_ = (s_ * 2 + t) * 8
                        mm(PS[sb][:, c_:c_ + 8], KmTs[:, s_, hd, t * 128:(t + 1) * 128], q2[:, hd, 8 * s_:8 * s_ + 8], True, True,
                           ['KmTs', tq2], ['ps%d' % sb])
                act(PTm[pi][:, 0, 0:64], PS[sb][:, 0:64], AF.Exp, ['ps%d' % sb], ['PTm%d' % pi], scale=MEM_SCALE)

        def memO(bi, hd):
            off, n = BLOCKS[bi]
            pi = hd % 2
            bo, bs = (4, 5)
            if bi < 4:
                for t in range(2):
                    mm(PS[bo][:, 0:n], Vm[:, t, hd * 128:(hd + 1) * 128], PTm[pi][:, t, 0:n], t == 0, t == 1, ['Vm', 'PTm%d' % pi],
                       ['ps%d' % bo])
                    mm(PS[bs][:, 0:n], ones_b, PTm[pi][:, t, 0:n], t == 0, t == 1, ['ones_b', 'PTm%d' % pi], ['ps%d' % bs])
            else:
                for s_ in range(4):
                    for t in range(2):
                        c_ = (s_ * 2 + t) * 8
                        mm(PS[bo][:, 8 * s_:8 * s_ + 8], Vms[:, s_, t, hd * 128:(hd + 1) * 128], PTm[pi][:, 0, c_:c_ + 8],
                           t == 0, t == 1, ['Vms', 'PTm%d' % pi], ['ps%d' % bo])
                        mm(PS[bs][:, 8 * s_:8 * s_ + 8], ones_b, PTm[pi][:, 0, c_:c_ + 8], t == 0, t == 1,
                           ['ones_b', 'PTm%d' % pi], ['ps%d' % bs])
            recip(rsm[:, 0:n], PS[bs][:, 0:n], ['ps%d' % bs], ['rsm'])
            cp('act', osm[:, 0:n], PS[bo][:, 0:n], ['ps%d' % bo], ['osm'])
            tt(omT[:, hd, 0:n], osm[:, 0:n], rsm[:, 0:n], ALU.mult, ['osm', 'rsm'], ['omT'])

        def memB(bi):
            off, n = BLOCKS[bi]
            memS(bi, 0)
            for hd in range(4):
                if hd + 1 < 4:
                    memS(bi, hd + 1)
                memO(bi, hd)
            for oc in range(8):
                pb = nxt('pj', 4)
                for k in range(4):
                    mm(PS[pb][:, 0:n], Wom[:, k, oc * 128:(oc + 1) * 128], omT[:, k, 0:n], k == 0, k == 3, ['Wom', 'omT'],
                       ['ps%d' % pb])
                cp('act' if oc % 2 else 'dve', f_sb[:, oc, 0:n], PS[pb][:, 0:n], ['ps%d' % pb], ['f_sb'])

        def memC(bi):
            off, n = BLOCKS[bi]
            NORM(f_sb[:, :, 0:n], 8, n, 'mem_post', 1024, None, ['f_sb'], [t_xT[bi]], resid=xT[:, :, off:off + n], ns=NSB2)

        if 'm' not in SKIP:
            memA(0)
            for bi in range(5):
                if bi + 1 < 5:
                    memA(bi + 1)
                memB(bi)
                memC(bi)
        pr.barrier()
        AR.off = mUT

        if KCUT == 3:
            raise StopX()
        gprev = AR.alloc([22, 2]); Gst = AR.alloc([22, 8]); GoutP = AR.alloc([2, 22]); GoutS = AR.alloc([4, 2, 22])
        cvo = AR.alloc([128])
        mF = AR.off
        stc = AR.alloc([2816])
        dma(stc[0:8, :], st_cv, [], ['stc'])
        pb = nxt('pj', 4)
        for j in range(22):
            tr(PS[pb][:, j * 8:(j + 1) * 8], stc[0:8, j * 128:(j + 1) * 128], ident[0:8, 0:8], ['stc', 'ident'], ['ps%d' % pb])
        cp('dve', Gst, PS[pb][:, 0:176].rearrange('p (j a) -> p j a', j=22), ['ps%d' % pb], ['Gst'])
        pr.barrier()
        AR.off = mF
        if KCUT == 4:
            raise StopX()
        hid = AR.alloc([22, 1056], BF16); f_all = AR.alloc([8, 1056])
        hT3 = f_all[:, 0:4, :].bitcast(BF16)
        hT3 = hT3.rearrange('p a b -> p (a b)')[:, 0:8 * 1056].rearrange('p (a b) -> p a b', a=8)
        Wg = [AR.alloc([8, 128], BF16) for _ in range(2)]; Wu = [AR.alloc([8, 128], BF16) for _ in range(2)]
        Wd = [AR.alloc([22, 128], BF16) for _ in range(2)]
        gsb = [AR.alloc([516]) for _ in range(2)]; a1 = [AR.alloc([512]) for _ in range(2)]
        memset('pool', gprev, 0.0, ['gprev'])
        for sbi, blks in enumerate(((0, 1), (2, 3, 4)) if 'f' not in SKIP else ()):
            sb_off = BLOCKS[blks[0]][0]
            for bi in blks:
                off, n = BLOCKS[bi]
                loc = off - sb_off
                NORM(xT[:, :, off:off + n], 8, n, 'ffn_pre', 1024, hT3[:, :, loc:loc + n], [t_xT[bi]], ['hT3'])
            for j in range(22):
                ws = j % 2
                cast_load(Wg[ws].rearrange('p k f -> p (k f)'), w_gate[j], 'Wg%d' % ws)
                cast_load(Wu[ws].rearrange('p k f -> p (k f)'), w_up[j], 'Wu%d' % ws)
                cw0 = G[:, GO['cw0'] + j:GO['cw0'] + j + 1]; cw1 = G[:, GO['cw1'] + j:GO['cw1'] + j + 1]
                cw2 = G[:, GO['cw2'] + j:GO['cw2'] + j + 1]; cb = G[:, GO['cb'] + j:GO['cb'] + j + 1]
                for bi in blks:
                    off, n = BLOCKS[bi]
                    loc = off - sb_off
                    pg = nxt('fg', 2); pu = 2 + nxt('fu', 2)
                    for k in range(8):
                        mm(PS[pg][:, 0:n], Wg[ws][:, k, :], hT3[:, k, loc:loc + n], k == 0, k == 7, ['Wg%d' % ws, 'hT3'], ['ps%d' % pg])
                    for k in range(8):
                        mm(PS[pu][:, 0:n], Wu[ws][:, k, :], hT3[:, k, loc:loc + n], k == 0, k == 7, ['Wu%d' % ws, 'hT3'], ['ps%d' % pu])
                    gs = nxt('gsb', 2)
                    tg, ta1 = 'gsb%d' % gs, 'a1%d' % gs
                    if bi < 4:
                        cp('act', gsb[gs][:, 2:2 + n], PS[pg][:, 0:n], ['ps%d' % pg], [tg])
                        cp('dve', gsb[gs][:, 0:2], gprev[:, j, :], ['gprev'], [tg])
                        cp('dve', gprev[:, j, :], gsb[gs][:, n:n + 2], [tg], ['gprev'])
                        if bi == 3:
                            cp('dve', GoutP[:, :, j], gsb[gs][:, n:n + 2], [tg], ['GoutP'])
                        v0, v1, v2 = gsb[gs][:, 0:n], gsb[gs][:, 1:n + 1], gsb[gs][:, 2:n + 2]
                        va = a1[gs][:, 0:n]; vu = PS[pu][:, 0:n]; vh = hid[:, j, loc:loc + n]
                    else:
                        g3 = gsb[gs][:, 0:40].rearrange('p (s t) -> p s t', s=4)
                        cp('act', g3[:, :, 2:10], PS[pg][:, 0:32].rearrange('p (s t) -> p s t', s=4), ['ps%d' % pg], [tg])
                        cp('dve', g3[:, :, 0:2], Gst[:, j, :].rearrange('p (s r) -> p s r', s=4), ['Gst'], [tg])
                        cp('dve', GoutS[:, :, :, j], g3[:, :, 8:10], [tg], ['GoutS'])
                        v0, v1, v2 = g3[:, :, 0:8], g3[:, :, 1:9], g3[:, :, 2:10]
                        va = a1[gs][:, 0:32].rearrange('p (s t) -> p s t', s=4)
                        vu = PS[pu][:, 0:32].rearrange('p (s t) -> p s t', s=4)
                        vh = hid[:, j, loc:loc + 32].rearrange('p (s t) -> p s t', s=4)
                    ts(va, v0, cw0, cb, ALU.mult, ALU.add, [tg, 'G'], [ta1])
                    stt(va, v1, cw1, va, ALU.mult, ALU.add, [tg, 'G', ta1], [ta1])
                    stt(va, v2, cw2, va, ALU.mult, ALU.add, [tg, 'G', ta1], [ta1])
                    act(va, va, AF.Silu, [ta1], [ta1])
                    tt(vh, va, vu, ALU.mult, [ta1, 'ps%d' % pu], ['hid'])
            pr.barrier()
            for c in range(8):
                ws = c % 2
                cast_load(Wd[ws].rearrange('p j d -> p (j d)'), w_down[c], 'Wd%d' % ws)
                for bi in blks:
                    off, n = BLOCKS[bi]
                    loc = off - sb_off
                    pb = nxt('pj', 4)
                    for j in range(22):
                        mm(PS[pb][:, 0:n], Wd[ws][:, j, :], hid[:, j, loc:loc + n], j == 0, j == 21, ['Wd%d' % ws, 'hid'], ['ps%d' % pb])
                    cp('act' if c % 2 else 'dve', f_all[:, c, loc:loc + n], PS[pb][:, 0:n], ['ps%d' % pb], ['f_all'])
            for bi in blks:
                off, n = BLOCKS[bi]
                loc = off - sb_off
                NORM(f_all[:, :, loc:loc + n], 8, n, 'ffn_post', 1024, None, ['f_all'], [t_xT[bi]], resid=xT[:, :, off:off + n])
            pr.barrier()
        if KCUT == 5:
            raise StopX()
        pb = nxt('pj', 4)
        tr(PS[pb][0:44, 0:128], GoutP[:].rearrange('p r j -> p (r j)'), ident, ['GoutP', 'ident'], ['ps%d' % pb])
        cp('act', cvo[0:44, :], PS[pb][0:44, 0:128], ['ps%d' % pb], ['cvo'])
        dma_out(o_cv_p.rearrange('r (j p) -> (r j) p', p=128), cvo[0:44, :], ['cvo'])
        gs2 = GoutS[:].rearrange('p s r j -> p (s r j)')
        for hf in range(2):
            pb = nxt('pj', 4)
            tr(PS[pb][0:88, 0:128], gs2[:, hf * 88:(hf + 1) * 88], ident, ['GoutS', 'ident'], ['ps%d' % pb])
            cp('act', cvo[0:88, :], PS[pb][0:88, 0:128], ['ps%d' % pb], ['cvo'])
            dma_out(o_cv_s.rearrange('a (j p) -> (a j) p', p=128)[hf * 88:(hf + 1) * 88, :], cvo[0:88, :], ['cvo'])
        pr.barrier()
        AR.off = mUT


    except StopX:
        pr.barrier()
        AR.off = mUT

    ytm = AR.alloc([2, 1024])
    for bi, (off, n) in enumerate(BLOCKS):
        for t0 in range(0, n, 128):
            nt = min(128, n - t0)
            sl = nxt('ytm', 2)
            for half in range(2):
                pb = nxt('pj', 4)
                for c4 in range(4):
                    c = half * 4 + c4
                    tr(PS[pb][0:nt, c4 * 128:(c4 + 1) * 128], xT[:, c, off + t0:off + t0 + nt], ident, [t_xT[bi], 'ident'],
                       ['ps%d' % pb])
                cp('act' if half else 'dve', ytm[0:nt, sl, half * 512:(half + 1) * 512], PS[pb][0:nt, :], ['ps%d' % pb],
                   ['ytm%d' % sl])
            dst = y_p[off + t0:off + t0 + nt, :] if bi < 4 else y_s
            dma_out(dst, ytm[0:nt, sl, :], ['ytm%d' % sl])
    for i_ in range(int(os.environ.get('KPAD', '0'))):
        pb = nxt('pj', 4)
        tr(PS[pb][:, 0:128], ident, ident, ['ident'], ['ps%d' % pb])
        cp('act', ytm[:, 0, 0:128], PS[pb][:, 0:128], ['ps%d' % pb], ['ytm0'])
    if os.environ.get('KFB', '1') == '1':
        pr.barrier()
    pr.add('sp', None, ['__out%d' % i for i in range(len(out_dmas))], [])
    pr.emit(nc, es)
    return nc, es


_GO = [('norm_mix_pre', 8), ('q_norm', 2), ('kv_norm', 2), ('norm_attn_out', 4), ('norm_ssm_out', 4),
       ('norm_mix_post', 8), ('norm_mem_pre', 8), ('mem_norm', 8), ('norm_mem_post', 8), ('norm_ffn_pre', 8),
       ('norm_ffn_post', 8), ('ssm_d', 4)]


def _consts():
    half = 16
    inv = (10000.0 ** (-np.arange(half, dtype=np.float32) * np.float32(2.0 / 32))).astype(np.float32)
    pos = np.concatenate([np.arange(2048), np.tile(16384 + np.arange(8), 4)]).astype(np.float32)
    ang = (pos[None, :] * inv[:, None]).astype(np.float32)
    rope = np.zeros((32, 2, NTOK), np.float32)
    rope[:, 0, :] = np.concatenate([np.cos(ang), np.cos(ang)], 0)
    rope[:, 1, :] = np.concatenate([np.sin(ang), np.sin(ang)], 0)
    tri = (np.arange(128)[None, :] >= np.arange(128)[:, None]).astype(np.float32)
    mskn = np.zeros((8, 64), np.float32)
    for i in range(8):
        for h in range(8):
            for t in range(8):
                mskn[i, h * 8 + t] = 1.0 if i <= t else 0.0
    smask = np.ones((1, 32), np.float32); smask[0, ::8] = 0.0
    return dict(k_ident=np.eye(128, dtype=np.float32), k_rope=rope, k_tri=tri, k_mskn=mskn,
                k_iota=np.arange(2048, dtype=np.float32)[None, :], k_smask=smask,
                k_pidx32=(np.arange(128) % 32).astype(np.float32)[:, None])


_CACHE = {}


def make_maps(inp, ncores=NCORES):
    f = lambda a: np.ascontiguousarray(np.asarray(a))
    g = [f(inp[k])[0].reshape(c, 128) for k, c in _GO]
    g.append(f(inp['ffn_conv_w'])[0].reshape(66, 128))
    g.append(f(inp['ffn_conv_b'])[0].reshape(22, 128))
    gains = np.concatenate(g, 0).astype(np.float32)
    assert gains.shape == (160, 128)
    shared = dict(
        c_lat=f(inp['cache_kv_latent'])[0], c_pe=f(inp['cache_k_rope'])[0], gains=gains,
        w_in=f(inp['w_in'])[0], w_uq=f(inp['w_uq'])[0].reshape(256, 768), w_uk=f(inp['w_uk'])[0].reshape(256, 512),
        w_uv=f(inp['w_uv'])[0].reshape(256, 512), a_re=f(inp['ssm_a_re'])[0].reshape(16, 128),
        a_im=f(inp['ssm_a_im'])[0].reshape(16, 128), log_dt=f(inp['ssm_log_dt'])[0].reshape(16, 2),
        b_re=f(inp['ssm_b_re'])[0].reshape(2048, 16), b_im=f(inp['ssm_b_im'])[0].reshape(2048, 16),
        cc_re=f(inp['ssm_c_re'])[0], cc_im=f(inp['ssm_c_im'])[0], w_glu=f(inp['ssm_w_glu'])[0],
        w_out=f(inp['w_out'])[0], w_qm=f(inp['w_q_mem'])[0], w_km=f(inp['w_k_mem'])[0], w_vm=f(inp['w_v_mem'])[0],
        w_om=f(inp['w_o_mem'])[0],
        w_gate=f(f(inp['w_gate'])[0].reshape(8, 128, 22, 128).transpose(2, 1, 0, 3)).reshape(22, 128, 1024),
        w_up=f(f(inp['w_up'])[0].reshape(8, 128, 22, 128).transpose(2, 1, 0, 3)).reshape(22, 128, 1024),
        w_down=f(f(inp['w_down'])[0].reshape(22, 128, 8, 128).transpose(2, 1, 0, 3)).reshape(8, 128, 2816))
    shared.update(_consts())
    in_maps = []
    for c in range(ncores):
        sl = slice(4 * c, 4 * c + 4)
        m = dict(shared)
        m.update(x_p=f(inp['x_prompt'])[c], x_s=f(inp['x_sample'])[sl].reshape(32, 1024), mem_p=f(inp['mem_prompt'])[c],
                 ptab=f(inp['page_table'])[sl].reshape(1, 512).astype(np.int32),
                 st_re=f(inp['state_ssm_re'])[0, sl].reshape(64, 128), st_im=f(inp['state_ssm_im'])[0, sl].reshape(64, 128),
                 st_cv=f(inp['state_ffn_conv'])[0, sl].reshape(8, 2816),
                 memk=f(inp['cache_mem_k'])[0, sl].reshape(4, 256, 512), memv=f(inp['cache_mem_v'])[0, sl].reshape(4, 256, 512))
        in_maps.append({k: np.ascontiguousarray(v) for k, v in m.items()})
    return in_maps


def kernel(**inp):
    if 'nc' not in _CACHE:
        _CACHE['nc'] = build()
    nc, es = _CACHE['nc']
    in_maps = make_maps(inp)
    res = run_bass_kernel_spmd(nc, in_maps, core_ids=list(range(NCORES)))
    R = res.results
    cat = lambda k: np.stack([np.asarray(R[c][k]) for c in range(NCORES)], 0)
    y_p = cat('y_p'); y_s = cat('y_s').reshape(32, 8, 1024)
    outs = (y_p, y_s,
            cat('o_kvl_p')[None], cat('o_kr_p')[None],
            cat('o_sre_p').reshape(1, 8, 32, 64), cat('o_sim_p').reshape(1, 8, 32, 64),
            cat('o_cv_p')[None],
            cat('o_mk_p').reshape(1, 8, 256, 4, 128), cat('o_mv_p').reshape(1, 8, 256, 4, 128),
            cat('o_kvl_s').reshape(1, 32, 8, 256), cat('o_kr_s').reshape(1, 32, 8, 32),
            cat('o_sre_s').reshape(1, 32, 32, 64), cat('o_sim_s').reshape(1, 32, 32, 64),
            cat('o_cv_s').reshape(1, 32, 2, 2816))
    return tuple(np.ascontiguousarray(o, dtype=np.float32) for o in outs)
```

```python
import math, os
import os
from contextlib import ExitStack
import numpy as np
import concourse.bass as bass
import concourse.mybir as mybir
from concourse.bass_utils import run_bass_kernel_spmd

F32 = mybir.dt.float32
BF16 = mybir.dt.bfloat16
I32 = mybir.dt.int32
AF = mybir.ActivationFunctionType
ALU = mybir.AluOpType

NCORES = 8
NP_, NSMP, NTOK = 2048, 32, 2080
BLOCKS = [(0, 512), (512, 512), (1024, 512), (1536, 512), (2048, 32)]
ATTN_SCALE = 96.0 ** -0.5
MEM_SCALE = 128.0 ** -0.5
EPS = 1e-6
DSIZE = {F32: 4, BF16: 2, I32: 4}
ENGS = ['pe', 'act', 'dve', 'pool', 'sp']
NSQ = {'sp': int(os.environ.get('KNSP', '24')), 'pool': int(os.environ.get('KNPL', '24'))}


class Prog:
    def __init__(self):
        self.ops = []
        self.per = {e: [] for e in ENGS}
        self.lastw = {}
        self.readers = {}
        self.dma_since = []

    def add(self, eng, fn, r=(), w=(), dma=False):
        oid = len(self.ops)
        deps = set()
        for t in list(r) + list(w):
            if t in self.lastw:
                deps.add(self.lastw[t])
        for t in w:
            deps.update(self.readers.get(t, {}).values())
            deps.update(self.readers.get((t, 'dma'), []))
        op = dict(id=oid, eng=eng, fn=fn, deps=deps, dma=dma)
        self.ops.append(op)
        self.per[eng].append(op)
        if dma:
            self.dma_since.append(oid)
        for t in w:
            self.lastw[t] = oid
            self.readers[t] = {}
            self.readers[(t, 'dma')] = []
        for t in r:
            if dma:
                self.readers.setdefault((t, 'dma'), []).append(oid)
            else:
                self.readers.setdefault(t, {})[eng] = oid
        return oid

    def barrier(self):
        last = set()
        for e in ENGS:
            comp = [op['id'] for op in self.per[e] if not op['dma'] and op['fn'] is not None]
            if comp:
                last.add(comp[-1])
        last.update(self.dma_since)
        self.dma_since = []
        for e in ENGS:
            oid = len(self.ops)
            op = dict(id=oid, eng=e, fn=None, deps=set(last), dma=False)
            self.ops.append(op)
            self.per[e].append(op)

    def emit(self, nc, es):
        ops = self.ops
        need = set()
        for op in ops:
            for d in op['deps']:
                dop = ops[d]
                if dop['dma']:
                    continue
                if dop['eng'] == 'pe' and op['eng'] == 'pe' and not op['dma']:
                    continue
                need.add(d)
        esem = {e: es.enter_context(nc.semaphore('se_' + e)) for e in ENGS}
        dsem = {q: [es.enter_context(nc.semaphore('sd_%s%d' % (q, i))) for i in range(n)]
                for q, n in NSQ.items()}
        for e in ENGS:
            c = 0
            k = 0
            for op in self.per[e]:
                if op['dma']:
                    op['sem'] = dsem[e][k % NSQ[e]]
                    op['val'] = 16 * (k // NSQ[e] + 1)
                    k += 1
                elif op['id'] in need:
                    c += 1
                    op['sig'] = c
        block = es.enter_context(nc.Block())

        def run(e, eng):
            waited = {}
            last_tsz = [None]
            for op in self.per[e]:
                waits = {}
                for d in op['deps']:
                    dop = ops[d]
                    if dop['dma']:
                        s, v = dop['sem'], dop['val']
                    else:
                        if dop['eng'] == 'pe' and e == 'pe' and not op['dma']:
                            continue
                        s, v = esem[dop['eng']], dop['sig']
                    if waits.get(s, (None, 0))[1] < v:
                        waits[s] = (s, v)
                if op['dma'] and op['val'] > 16:
                    s, v = op['sem'], op['val'] - 16
                    if waits.get(s, (None, 0))[1] < v:
                        waits[s] = (s, v)
                wl = []
                for s, v in waits.values():
                    if waited.get(s, 0) >= v:
                        continue
                    waited[s] = v
                    wl.append((s, v))
                attach = None
                if wl and e in os.environ.get('KATTENG', 'act,dve,pe,pool').split(',') and not op['dma'] and op['fn'] is not None \
                        and os.environ.get('KATTACH', '1') == '1':
                    attach = wl.pop()
                for s, v in wl:
                    eng.wait_ge(s, v)
                if op['fn'] is None:
                    continue
                if e == 'pe' and 'tsz' in op and os.environ.get('KDRAIN', '0') == '1':
                    if last_tsz[0] is not None and last_tsz[0] != op['tsz']:
                        eng.drain()
                    last_tsz[0] = op['tsz']
                ins = op['fn'](eng)
                if attach is not None:
                    ins._wait_ge(attach[0], attach[1])
                if op['dma']:
                    ins.then_inc(op['sem'], 16)
                elif op['id'] in need:
                    ins.then_inc(esem[e], 1)

        @block.tensor
        def _(eng):
            run('pe', eng)

        @block.scalar
        def _(eng):
            run('act', eng)

        @block.vector
        def _(eng):
            run('dve', eng)

        @block.gpsimd
        def _(eng):
            run('pool', eng)

        @block.sync
        def _(eng):
            run('sp', eng)


class Arena:
    def __init__(self, ap, nwords):
        self.ap, self.n, self.off = ap, nwords, 0
        self.hi = 0

    def alloc(self, free, dtype=F32):
        n = 1
        for s in free:
            n *= s
        words = (n * DSIZE[dtype] + 3) // 4
        words = (words + 15) // 16 * 16
        a = self.off
        self.off += words
        self.hi = max(self.hi, self.off)
        assert self.off <= self.n, ('arena overflow', self.off, self.n)
        v = self.ap[:, a:a + words]
        if dtype != F32:
            v = v.bitcast(dtype)
        v = v[:, 0:n]
        if len(free) == 2:
            v = v.rearrange('p (a b) -> p a b', a=free[0])
        elif len(free) == 3:
            v = v.rearrange('p (a b c) -> p a b c', a=free[0], b=free[1])
        elif len(free) == 4:
            v = v.rearrange('p (a b c d) -> p a b c d', a=free[0], b=free[1], c=free[2])
        return v


def build(npool=5120, debug=False):
    nc = bass.Bass('TRN2', target_bir_lowering=False)
    pr = Prog()

    def din(name, shape, dt=F32):
        return nc.dram_tensor(name, list(shape), dt, kind='ExternalInput').ap()

    def dout(name, shape, dt=F32):
        return nc.dram_tensor(name, list(shape), dt, kind='ExternalOutput').ap()

    x_p = din('x_p', [NP_, 1024]); x_s = din('x_s', [NSMP, 1024]); mem_p = din('mem_p', [256, 1024])
    c_lat = din('c_lat', [npool, 128, 256]); c_pe = din('c_pe', [npool, 128, 32])
    ptab = din('ptab', [1, 512], I32)
    st_re = din('st_re', [64, 128]); st_im = din('st_im', [64, 128])
    st_cv = din('st_cv', [8, 2816])
    memk = din('memk', [4, 256, 512]); memv = din('memv', [4, 256, 512])
    gains = din('gains', [160, 128])
    w_in = din('w_in', [1024, 1056]); w_uq = din('w_uq', [256, 768]); w_uk = din('w_uk', [256, 512])
    w_uv = din('w_uv', [256, 512])
    a_re = din('a_re', [16, 128]); a_im = din('a_im', [16, 128]); log_dt = din('log_dt', [16, 2])
    b_re = din('b_re', [2048, 16]); b_im = din('b_im', [2048, 16])
    cc_re = din('cc_re', [32, 16, 64]); cc_im = din('cc_im', [32, 16, 64])
    w_glu = din('w_glu', [512, 512]); w_out = din('w_out', [1024, 1024])
    w_qm = din('w_qm', [1024, 512]); w_km = din('w_km', [1024, 512]); w_vm = din('w_vm', [1024, 512])
    w_om = din('w_om', [512, 1024])
    w_gate = din('w_gate', [22, 128, 1024]); w_up = din('w_up', [22, 128, 1024]); w_down = din('w_down', [8, 128, 2816])
    k_ident = din('k_ident', [128, 128]); k_rope = din('k_rope', [32, 2, NTOK])
    k_tri = din('k_tri', [128, 128]); k_mskn = din('k_mskn', [8, 64]); k_iota = din('k_iota', [1, 2048])
    k_smask = din('k_smask', [1, 32]); k_pidx32 = din('k_pidx32', [128, 1])

    y_p = dout('y_p', [NP_, 1024]); y_s = dout('y_s', [NSMP, 1024])
    o_kvl_p = dout('o_kvl_p', [NP_, 256]); o_kr_p = dout('o_kr_p', [NP_, 32])
    o_sre_p = dout('o_sre_p', [16, 128]); o_sim_p = dout('o_sim_p', [16, 128])
    o_cv_p = dout('o_cv_p', [2, 2816])
    o_mk_p = dout('o_mk_p', [256, 512]); o_mv_p = dout('o_mv_p', [256, 512])
    o_kvl_s = dout('o_kvl_s', [NSMP, 256]); o_kr_s = dout('o_kr_s', [NSMP, 32])
    o_sre_s = dout('o_sre_s', [64, 128]); o_sim_s = dout('o_sim_s', [64, 128])
    o_cv_s = dout('o_cv_s', [8, 2816])

    es = ExitStack()
    NW = 53000
    arena_t = es.enter_context(nc.sbuf_tensor('arena', [128, NW], F32))
    AR = Arena(arena_t[:], NW)
    PS = [es.enter_context(nc.psum_tensor('ps%d' % i, [128, 512], F32)) for i in range(8)]
    rg = es.enter_context(nc.sync.register('rg'))
    out_dmas = []

    def _rnd(x):
        return 32 if x <= 32 else (64 if x <= 64 else 128)

    def _tsz(ap):
        sh = ap.shape
        fsz = 1
        for d_ in sh[1:]:
            fsz *= d_
        return (_rnd(sh[0]), _rnd(fsz))

    def mm(out, lhsT, rhs, start, stop, r, w):
        oid = pr.add('pe', lambda e: e.matmul(out, lhsT=lhsT, rhs=rhs, start=start, stop=stop), r, w)
        pr.ops[oid]['tsz'] = _tsz(lhsT)

    def tr(out, in_, ident, r, w):
        oid = pr.add('pe', lambda e: e.transpose(out, in_, ident), r, w)
        pr.ops[oid]['tsz'] = _tsz(in_)

    def act(out, in_, func, r, w, scale=1.0, bias=None):
        if bias is None:
            pr.add('act', lambda e: e.activation(out=out, in_=in_, func=func, scale=scale), r, w)
        else:
            pr.add('act', lambda e: e.activation(out=out, in_=in_, func=func, scale=scale, bias=bias), r, w)

    def cp(eng, out, in_, r, w):
        if eng == 'pool' and os.environ.get('KNOPOOL', '0') == '1':
            eng = 'dve'
        if eng == 'act':
            pr.add('act', lambda e: e.activation(out=out, in_=in_, func=AF.Copy), r, w)
        else:
            pr.add(eng, lambda e: e.tensor_copy(out=out, in_=in_), r, w)

    def tt(out, in0, in1, op, r, w, eng='dve'):
        pr.add(eng, lambda e: e.tensor_tensor(out=out, in0=in0, in1=in1, op=op), r, w)

    def ts(out, in0, s1, s2, op0, op1, r, w, eng='dve'):
        if s2 is None:
            pr.add(eng, lambda e: e.tensor_scalar(out=out, in0=in0, scalar1=s1, scalar2=None, op0=op0), r, w)
        else:
            pr.add(eng, lambda e: e.tensor_scalar(out=out, in0=in0, scalar1=s1, scalar2=s2, op0=op0, op1=op1), r, w)

    def stt(out, in0, scalar, in1, op0, op1, r, w):
        pr.add('dve', lambda e: e.scalar_tensor_tensor(out=out, in0=in0, scalar=scalar, in1=in1, op0=op0, op1=op1), r, w)

    def scan(out, d0, d1, init, r, w):
        pr.add('dve', lambda e: e.tensor_tensor_scan(out=out, data0=d0, data1=d1, initial=init,
                                                     op0=ALU.mult, op1=ALU.add), r, w)

    def recip(out, in_, r, w):
        pr.add('dve', lambda e: e.reciprocal(out=out, in_=in_), r, w)

    def memset(eng, ap, val, w):
        if eng == 'pool' and os.environ.get('KNOPOOL', '0') == '1':
            eng = 'dve'
        pr.add(eng, lambda e: e.memset(ap, val), (), w)

    def dma(out, in_, r, w, q='sp', slow=False):
        if slow:
            return pr.add(q, lambda e: e.dma_start(out=out, in_=in_, allow_slow_non_contiguous=True), r, w, dma=True)
        return pr.add(q, lambda e: e.dma_start(out=out, in_=in_), r, w, dma=True)

    def dma_out(out, in_, r):
        out_dmas.append(dma(out, in_, r, ['__out%d' % len(out_dmas)]))

    rot = {}

    def nxt(key, n):
        rot[key] = (rot.get(key, -1) + 1) % n
        return rot[key]

    xT = AR.alloc([8, NTOK]); t_xT = ['xT%d' % b for b in range(5)]
    ident = AR.alloc([128]); identb = AR.alloc([128], BF16)
    ones_b = AR.alloc([128], BF16); ones_f = AR.alloc([128])
    G = AR.alloc([160])
    epsT = AR.alloc([1]); hpiT = AR.alloc([1])
    tri = AR.alloc([128], BF16); mskn = AR.alloc([64], BF16)
    GO = dict(mix_pre=0, q=8, kv=10, attn=12, ssm=16, mix_post=20, mem_pre=28, mem=36, mem_post=44,
              ffn_pre=52, ffn_post=60, d=68, cw0=72, cw1=94, cw2=116, cb=138)

    dma(ident, k_ident, [], ['ident'])
    cp('act', identb, ident, ['ident'], ['identb'])
    memset('pool', ones_b, 1.0, ['ones_b']); memset('pool', ones_f, 1.0, ['ones_f'])
    memset('pool', epsT, EPS, ['epsT']); memset('pool', hpiT, math.pi / 2, ['hpiT'])
    KOLD = os.environ.get('KOLD', '0')
    if '1' in KOLD:
        trf = AR.alloc([128]); mnf = AR.alloc([64]); graw = AR.alloc([2, 128])
    if '2' not in KOLD:
        KmT = AR.alloc([4, 256], BF16); Vm = AR.alloc([2, 512], BF16)
    mark_persist = AR.off

    def cast_load(dst, src, tok, q='pool'):
        dma(dst, src, [], [tok], q=q)

    def norm_fm(src, C, n, gcol, D, dst, r, w, sq, ssb, rstd, psb, resid=None, tmp=None, sfx=''):
        tq_, ts_, tr_, tt_ = 'nsq' + sfx, 'nss' + sfx, 'nrstd' + sfx, 'ntmp' + sfx
        act(sq[:, 0:C, 0:n], src, AF.Square, r, [tq_])
        for c in range(C):
            mm(PS[psb][:, 0:n], ones_b, sq[:, c, 0:n], c == 0, c == C - 1, [tq_, 'ones_b'], ['ps%d' % psb])
        act(ssb[:, 0:n], PS[psb][:, 0:n], AF.Ln, ['ps%d' % psb, 'epsT'], [ts_], scale=1.0 / D, bias=epsT)
        act(rstd[:, 0:n], ssb[:, 0:n], AF.Exp, [ts_], [tr_], scale=-0.5)
        for c in range(C):
            if resid is None:
                stt(dst[:, c, :], src[:, c, :], G[:, gcol + c:gcol + c + 1], rstd[:, 0:n], ALU.mult, ALU.mult,
                    list(r) + [tr_, 'G'], w)
            else:
                stt(tmp[:, 0:n], src[:, c, :], G[:, gcol + c:gcol + c + 1], rstd[:, 0:n], ALU.mult, ALU.mult,
                    list(r) + [tr_, 'G'], [tt_])
                tt(resid[:, c, :], resid[:, c, :], tmp[:, 0:n], ALU.add, [tt_] + list(w), w)

    n_sq = AR.alloc([8, 512], BF16); n_ss = AR.alloc([512]); n_rstd = AR.alloc([512]); n_tmp = AR.alloc([512])

    def NORM(src, C, n, gname, D, dst, r, w, psb=7, resid=None, ns=None):
        if ns is None:
            norm_fm(src, C, n, GO[gname], D, dst, r, w, n_sq, n_ss, n_rstd, psb, resid=resid, tmp=n_tmp)
        else:
            norm_fm(src, C, n, GO[gname], D, dst, r, w, ns[0], ns[1], ns[2], 6, resid=resid, tmp=ns[3], sfx='B')

    def norm_set():
        return (AR.alloc([8, 512], BF16), AR.alloc([512]), AR.alloc([512]), AR.alloc([512]))

    m0 = AR.off
    if '1' not in KOLD:
        trf = AR.alloc([128]); mnf = AR.alloc([64])
    dma(trf, k_tri, [], ['trf']); dma(mnf[0:8, :], k_mskn, [], ['mnf'])
    cp('dve', tri, trf, ['trf'], ['tri']); cp('dve', mskn[0:8, :], mnf[0:8, :], ['mnf'], ['mskn'])
    if '1' not in KOLD:
        graw = AR.alloc([2, 128])
    dma(graw[:, 0, :], gains[0:128, :], [], ['graw0']); dma(graw[0:32, 1, :], gains[128:160, :], [], ['graw1'])
    tr(PS[0][:, 0:128], graw[:, 0, :], ident, ['graw0', 'ident'], ['ps0'])
    tr(PS[0][:, 128:160], graw[0:32, 1, :], ident[0:32, 0:32], ['graw1', 'ident'], ['ps0'])
    cp('dve', G, PS[0][:, 0:160], ['ps0'], ['G'])
    xtm = AR.alloc([2, 1024])
    for t in range(17):
        sl = t % 2
        nt = 128 if t < 16 else 32
        src = x_p[t * 128:(t + 1) * 128, :] if t < 16 else x_s[:, :]
        dma(xtm[0:nt, sl, :], src, [], ['xtm%d' % sl])
        b = min(t // 4, 4)
        col0 = t * 128
        for half in range(2):
            pb = nxt('x0', 2)
            for c4 in range(4):
                c = half * 4 + c4
                tr(PS[pb][:, c4 * 128:c4 * 128 + nt], xtm[0:nt, sl, c * 128:(c + 1) * 128], ident[0:nt, 0:nt],
                   ['xtm%d' % sl, 'ident'], ['ps%d' % pb])
            cp('act' if half == 0 else 'dve', xT[:, half * 4:half * 4 + 4, col0:col0 + nt],
               PS[pb][:].rearrange('p (a b) -> p a b', a=4)[:, :, 0:nt], ['ps%d' % pb], [t_xT[b]])
    pr.barrier()
    AR.off = m0

    def out_rows(src_fm, C, n, dst_rows, width, r, base=0):
        for t0 in range(0, n, 128):
            nt = min(128, n - t0)
            pb = nxt('orow', 2)
            sl = nxt('otm', 2)
            for c in range(C):
                if base == 0:
                    tr(PS[pb][0:nt, c * 128:(c + 1) * 128], src_fm[:, c, t0:t0 + nt], ident, list(r) + ['ident'], ['ps%d' % pb])
                else:
                    tr(PS[pb][0:nt, 0:32], src_fm[base:base + 32, c, t0:t0 + nt], ident[base:base + 32, base:base + 32],
                       list(r) + ['ident'], ['ps%d' % pb])
            cp('act', otm[0:nt, sl, 0:width], PS[pb][0:nt, 0:width], ['ps%d' % pb], ['otm%d' % sl])
            dma_out(dst_rows[t0:t0 + nt, :], otm[0:nt, sl, 0:width], ['otm%d' % sl])

    otm = AR.alloc([2, 512])

    mUT = AR.off
    uT = AR.alloc([4, NTOK], BF16); attn_o = AR.alloc([4, NTOK], BF16)
    mATT = AR.off
    ropeT = AR.alloc([2, NTOK])
    dma(ropeT[64:96, :, :], k_rope, [], ['ropeT'])
    cqn = AR.alloc([2, NTOK], BF16); ckvn = AR.alloc([2, NTOK], BF16)
    KpeT = AR.alloc([NTOK], BF16)
    mA = AR.off
    Win = AR.alloc([8, 1056], BF16); WinSw = AR.alloc([8, 32], BF16)
    w_in_v = w_in.rearrange('(c p) n -> p c n', p=128)
    for c in range(8):
        cast_load(Win[:, c, :], w_in_v[:, c, :], 'Win')
    cast_load(WinSw[:, :, 0:16], w_in_v[:, :, 528:544], 'WinSw')
    cast_load(WinSw[:, :, 16:32], w_in_v[:, :, 512:528], 'WinSw')
    ts(WinSw[:, :, 0:16], WinSw[:, :, 0:16], -1.0, None, ALU.mult, None, ['WinSw'], ['WinSw'])
    hT = AR.alloc([8, 512], BF16)
    zq = AR.alloc([2, 512]); zkv = zq; zkvn = AR.alloc([2, 512])
    krf = AR.alloc([1, 512]); rt1 = AR.alloc([512]); rt2 = AR.alloc([512])
    for bi, (off, n) in enumerate(BLOCKS):
        NORM(xT[:, :, off:off + n], 8, n, 'mix_pre', 1024, hT[:, :, 0:n], [t_xT[bi]], ['hT'])

        def proj(col0, M, outp, pb, wt=Win, wtok='Win'):
            for k in range(8):
                mm(outp, wt[:, k, col0:col0 + M], hT[:, k, 0:n], k == 0, k == 7, [wtok, 'hT'], ['ps%d' % pb])
        for c in range(2):
            pb = nxt('pj', 4)
            proj(c * 128, 128, PS[pb][:, 0:n], pb)
            cp('act', zq[:, c, 0:n], PS[pb][:, 0:n], ['ps%d' % pb], ['zq'])
        NORM(zq[:, :, 0:n], 2, n, 'q', 256, cqn[:, :, off:off + n], ['zq'], ['cqn'])
        for c in range(2):
            pb = nxt('pj', 4)
            proj(256 + c * 128, 128, PS[pb][:, 0:n], pb)
            cp('act', zkv[:, c, 0:n], PS[pb][:, 0:n], ['ps%d' % pb], ['zq'])
        NORM(zkv[:, :, 0:n], 2, n, 'kv', 256, zkvn[:, :, 0:n], ['zq'], ['zkvn'])
        cp('act', ckvn[:, :, off:off + n], zkvn[:, :, 0:n], ['zkvn'], ['ckvn'])
        out_rows(zkvn, 2, n, (o_kvl_p[off:off + n, :] if bi < 4 else o_kvl_s), 256, ['zkvn'])
        pb = nxt('pj', 4); pb2 = nxt('pj', 4)
        proj(512, 32, PS[pb][64:96, 0:n], pb)
        proj(0, 32, PS[pb2][64:96, 0:n], pb2, wt=WinSw, wtok='WinSw')
        tt(rt1[64:96, 0:n], PS[pb][64:96, 0:n], ropeT[64:96, 0, off:off + n], ALU.mult, ['ps%d' % pb, 'ropeT'], ['rt1'])
        tt(rt2[64:96, 0:n], PS[pb2][64:96, 0:n], ropeT[64:96, 1, off:off + n], ALU.mult, ['ps%d' % pb2, 'ropeT'], ['rt2'])
        tt(krf[64:96, 0, 0:n], rt1[64:96, 0:n], rt2[64:96, 0:n], ALU.add, ['rt1', 'rt2'], ['krf'])
        cp('act', KpeT[64:96, off:off + n], krf[64:96, 0, 0:n], ['krf'], ['KpeT'])
        out_rows(krf, 1, n, (o_kr_p[off:off + n, :] if bi < 4 else o_kr_s), 32, ['krf'], base=64)
        for c in range(4):
            pb = nxt('pj', 4)
            proj(544 + c * 128, 128, PS[pb][:, 0:n], pb)
            cp('act' if c % 2 else 'dve', uT[:, c, off:off + n], PS[pb][:, 0:n], ['ps%d' % pb], ['uT'])
    pr.barrier()
    AR.off = mA

    Wuv = AR.alloc([2, 512], BF16); QA = AR.alloc([4, 2, 64], BF16); QAp = AR.alloc([4, 64], BF16)
    mSA = AR.off
    Wuq = AR.alloc([2, 768], BF16); WuqSw = AR.alloc([2, 8, 32], BF16)
    Wuk = AR.alloc([2, 512], BF16); WukT = AR.alloc([8, 256], BF16)
    w_uq_v = w_uq.rearrange('(c p) n -> p c n', p=128)
    cast_load(Wuq, w_uq_v, 'Wuq')
    w_uq_4 = w_uq.rearrange('(c p) (h e) -> p c h e', p=128, e=96)
    for c in range(2):
        cast_load(WuqSw[:, c, :, 0:16], w_uq_4[:, c, :, 80:96], 'WuqSw')
        cast_load(WuqSw[:, c, :, 16:32], w_uq_4[:, c, :, 64:80], 'WuqSw')
    ts(WuqSw[:, :, :, 0:16], WuqSw[:, :, :, 0:16], -1.0, None, ALU.mult, None, ['WuqSw'], ['WuqSw'])
    cast_load(Wuk, w_uk.rearrange('(c p) n -> p c n', p=128), 'Wuk')
    cast_load(Wuv, w_uv.rearrange('(c p) n -> p c n', p=128), 'Wuv')
    for h in range(8):
        pb = nxt('pj', 4)
        psb = PS[pb][:].bitcast(BF16)
        for c in range(2):
            tr(psb[0:64, c * 128:(c + 1) * 128], Wuk[:, c, h * 64:(h + 1) * 64], identb, ['Wuk', 'identb'], ['ps%d' % pb])
        cp('dve', WukT[0:64, h, :], psb[0:64, 0:256], ['ps%d' % pb], ['WukT'])
    QT = [AR.alloc([NTOK], BF16) for _ in range(2)]
    KT = [AR.alloc([NTOK], BF16) for _ in range(2)]
    Vh = [AR.alloc([17, 65], BF16) for _ in range(2)]
    PT = [AR.alloc([512], BF16) for _ in range(3)]
    qr1 = AR.alloc([512]); qr2 = AR.alloc([512])
    osb = AR.alloc([512]); rsb = AR.alloc([512]); otmpb = AR.alloc([512], BF16)
    for i in range(2):
        memset('pool', Vh[i][:, :, 64:65], 1.0, ['Vh%d' % i])
    for h in range(8):
        hs = h % 2
        tq, tk, tv = 'QT%d' % hs, 'KT%d' % hs, 'Vh%d' % hs
        for bi, (off, n) in enumerate(BLOCKS):
            pb = nxt('pj', 4); pb2 = nxt('pj', 4); pb3 = nxt('pj', 4)
            for c in range(2):
                mm(PS[pb][0:96, 0:n], Wuq[:, c, h * 96:(h + 1) * 96], cqn[:, c, off:off + n], c == 0, c == 1,
                   ['Wuq', 'cqn'], ['ps%d' % pb])
            for c in range(2):
                mm(PS[pb2][64:96, 0:n], WuqSw[:, c, h, :], cqn[:, c, off:off + n], c == 0, c == 1,
                   ['WuqSw', 'cqn'], ['ps%d' % pb2])
            cp('act', QT[hs][0:64, off:off + n], PS[pb][0:64, 0:n], ['ps%d' % pb], [tq])
            tt(qr1[64:96, 0:n], PS[pb][64:96, 0:n], ropeT[64:96, 0, off:off + n], ALU.mult, ['ps%d' % pb, 'ropeT'], ['qr1'])
            tt(qr2[64:96, 0:n], PS[pb2][64:96, 0:n], ropeT[64:96, 1, off:off + n], ALU.mult, ['ps%d' % pb2, 'ropeT'], ['qr2'])
            tt(QT[hs][64:96, off:off + n], qr1[64:96, 0:n], qr2[64:96, 0:n], ALU.add, ['qr1', 'qr2'], [tq])
            for c in range(2):
                mm(PS[pb3][0:64, 0:n], Wuk[:, c, h * 64:(h + 1) * 64], ckvn[:, c, off:off + n], c == 0, c == 1,
                   ['Wuk', 'ckvn'], ['ps%d' % pb3])
            cp('act', KT[hs][0:64, off:off + n], PS[pb3][0:64, 0:n], ['ps%d' % pb3], [tk])
        cp('pool', KT[hs][64:96, :], KpeT[64:96, :], ['KpeT'], [tk])
        for g4 in range(5):
            pb = nxt('pj', 4)
            ntl = 4 if g4 < 4 else 1
            for j in range(ntl):
                t = g4 * 4 + j
                nt = 128 if t < 16 else 32
                for c in range(2):
                    mm(PS[pb][0:nt, j * 64:(j + 1) * 64], ckvn[:, c, t * 128:t * 128 + nt], Wuv[:, c, h * 64:(h + 1) * 64],
                       c == 0, c == 1, ['ckvn', 'Wuv'], ['ps%d' % pb])
            npart = 128 if g4 < 4 else 32
            cp('dve', Vh[hs][0:npart, g4 * 4:g4 * 4 + ntl, 0:64],
               PS[pb][0:npart, 0:ntl * 64].rearrange('p (a b) -> p a b', a=ntl), ['ps%d' % pb], [tv])
        for c in range(2):
            pb = nxt('pj', 4)
            mm(PS[pb][:, 0:32], WukT[0:64, h, c * 128:(c + 1) * 128], QT[hs][0:64, 2048:2080], True, True,
               ['WukT', tq], ['ps%d' % pb])
            cp('dve', QA[:, :, c, h * 8:(h + 1) * 8], PS[pb][:, 0:32].rearrange('p (s t) -> p s t', s=4),
               ['ps%d' % pb], ['QA'])
        cp('dve', QAp[64:96, :, h * 8:(h + 1) * 8], QT[hs][64:96, 2048:2080].rearrange('p (s t) -> p s t', s=4), [tq], ['QA'])
        for qt in range(4):
            ob = 4 + nxt('ob', 2)
            nk = 4 * qt + 4
            pend = []
            for kt in range(nk):
                d = kt - 4 * qt
                c0 = 128 * d if d > 0 else 0
                sb = 6 + nxt('sb', 2)
                pi = nxt('PT', 3)
                qs = qt * 512
                mm(PS[sb][:, c0:512], KT[hs][0:96, kt * 128:(kt + 1) * 128], QT[hs][0:96, qs + c0:qs + 512], True, True,
                   [tk, tq], ['ps%d' % sb])
                act(PT[pi][:, c0:512], PS[sb][:, c0:512], AF.Exp, ['ps%d' % sb], ['PT%d' % pi], scale=ATTN_SCALE)
                if d >= 0:
                    tt(PT[pi][:, c0:c0 + 128], PT[pi][:, c0:c0 + 128], tri, ALU.mult, ['PT%d' % pi, 'tri'], ['PT%d' % pi])

                def pv(kt=kt, c0=c0, pi=pi):
                    mm(PS[ob][0:65, c0:512], Vh[hs][:, kt, :], PT[pi][:, c0:512], kt == 0, kt == nk - 1,
                       [tv, 'PT%d' % pi], ['ps%d' % ob])
                pend.append(pv)
                if len(pend) > 1:
                    pend.pop(0)()
            while pend:
                pend.pop(0)()
            recip(rsb[64:65, :], PS[ob][64:65, :], ['ps%d' % ob], ['rsb'])
            cp('act', osb[0:64, :], PS[ob][0:64, :], ['ps%d' % ob], ['osb'])
            pb = nxt('pj', 4)
            mm(PS[pb][0:64, :], ones_f[64:65, 0:64], rsb[64:65, :], True, True, ['ones_f', 'rsb'], ['ps%d' % pb])
            if h % 2 == 0:
                tt(attn_o[0:64, h // 2, qs:qs + 512], osb[0:64, :], PS[pb][0:64, :], ALU.mult, ['osb', 'ps%d' % pb], ['attn_o'])
            else:
                tt(otmpb[0:64, :], osb[0:64, :], PS[pb][0:64, :], ALU.mult, ['osb', 'ps%d' % pb], ['otmpb'])
                cp('dve', attn_o[64:128, h // 2, qs:qs + 512], otmpb[0:64, :], ['otmpb'], ['attn_o'])
    pr.barrier()
    AR.off = mSA
    NSL = 4
    ptS = AR.alloc([512], I32); ptF = AR.alloc([512]); pidx = AR.alloc([1]); idxG = AR.alloc([128], I32); idxF = AR.alloc([128])
    dma(ptS, ptab.partition_broadcast(128), [], ['ptS'])
    dma(pidx, k_pidx32, [], ['pidx'])
    cp('dve', ptF, ptS, ['ptS'], ['ptF'])
    for a in range(4):
        rows = slice(32 * a, 32 * a + 32)
        ts(idxF[rows, :], ptF[rows, :].rearrange('p (j a) -> p j a', a=4)[:, :, a], 32.0, pidx[rows, 0:1], ALU.mult, ALU.add,
           ['ptF', 'pidx'], ['idxF'])
    cp('dve', idxG, idxF, ['idxF'], ['idxG'])
    Kl = [AR.alloc([4, 256], BF16) for _ in range(NSL)]; Kp = [AR.alloc([4, 32], BF16) for _ in range(NSL)]
    KTl = [AR.alloc([4, 2, 128], BF16) for _ in range(NSL)]
    KTp = [AR.alloc([4, 128], BF16) for _ in range(NSL)]
    PTs = [AR.alloc([4, 64], BF16) for _ in range(NSL)]
    KnN = AR.alloc([257], BF16); PTn = AR.alloc([64], BF16)
    onb = AR.alloc([256], BF16); olT = AR.alloc([2, 64], BF16); rs1 = AR.alloc([1])
    memset('pool', KnN[0:8, 256:257], 1.0, ['KnN'])
    c_lat_r = c_lat.rearrange('a (r t) d -> (a r) (t d)', t=4); c_pe_r = c_pe.rearrange('a (r t) d -> (a r) (t d)', t=4)

    def page_dma(dst, src_rows, col, w):
        pr.add('pool', lambda e: e.indirect_dma_start(out=dst, out_offset=None, in_=src_rows,
                                                      in_offset=bass.IndirectOffsetOnAxis(ap=idxG[:, col:col + 1], axis=0)),
               ['idxG'], w, dma=True)

    memset('pool', attn_o[:, :, 2048:2080], 0.0, ['attn_o'])
    NSEQ = int(os.environ.get('KSEQ', '4'))

    def st0(s_, g, sl):
        col = s_ * 32 + g
        page_dma(Kl[sl].rearrange('p t d -> p (t d)'), c_lat_r, col, ['Kb%d' % sl])
        page_dma(Kp[sl].rearrange('p t d -> p (t d)'), c_pe_r, col, ['Kb%d' % sl])
        pa = nxt('pj', 4); pbb = nxt('pj', 4)
        psa = PS[pa][:].bitcast(BF16); psp = PS[pbb][:].bitcast(BF16)
        for p in range(4):
            for c in range(2):
                tr(psa[:, (p * 2 + c) * 128:(p * 2 + c + 1) * 128], Kl[sl][:, p, c * 128:(c + 1) * 128], identb,
                   ['Kb%d' % sl, 'identb'], ['ps%d' % pa])
            tr(psp[64:96, p * 128:(p + 1) * 128], Kp[sl][:, p, :], identb, ['Kb%d' % sl, 'identb'], ['ps%d' % pbb])
        cp('act', KTl[sl], psa[:, 0:1024].rearrange('p (a c k) -> p a c k', a=4, c=2), ['ps%d' % pa], ['KTl%d' % sl])
        cp('dve', KTp[sl][64:96, :, :], psp[64:96, 0:512].rearrange('p (a k) -> p a k', a=4), ['ps%d' % pbb], ['KTp%d' % sl])

    def st1(s_, g, sl):
        sb = 6 + nxt('sb', 2)
        for p in range(4):
            o = PS[sb][:, p * 64:(p + 1) * 64]
            mm(o, KTl[sl][:, p, 0, :], QA[:, s_, 0, :], True, False, ['KTl%d' % sl, 'QA'], ['ps%d' % sb])
            mm(o, KTl[sl][:, p, 1, :], QA[:, s_, 1, :], False, False, ['KTl%d' % sl, 'QA'], ['ps%d' % sb])
            mm(o, KTp[sl][64:96, p, :], QAp[64:96, s_, :], False, True, ['KTp%d' % sl, 'QA'], ['ps%d' % sb])
        act(PTs[sl], PS[sb][:, 0:256].rearrange('p (a q) -> p a q', a=4), AF.Exp, ['ps%d' % sb], ['PTs%d' % sl],
            scale=ATTN_SCALE)

    def st2(s_, g, sl):
        for p in range(4):
            mm(PS[OBS[s_ % 2]][0:64, 0:256], PTs[sl][:, p, :], Kl[sl][:, p, :], g == 0 and p == 0, False,
               ['PTs%d' % sl, 'Kb%d' % sl], ['ps%d' % OBS[s_ % 2]])
            mm(PS[OBS[s_ % 2]][0:64, 256:257], PTs[sl][:, p, :], ones_b[:, 0:1], False, False,
               ['PTs%d' % sl, 'ones_b'], ['ps%d' % OBS[s_ % 2]])
        if g == 31:
            fin(s_)

    OBS = [4, 5]
    groups = [(s_, g, i % NSL) for i, (s_, g) in enumerate((s_, g) for s_ in range(NSEQ) for g in range(32))]

    def fin(s_):
        OB = OBS[s_ % 2]
        c0 = 2048 + 8 * s_
        pb = nxt('pj', 4)
        psb = PS[pb][:].bitcast(BF16)
        for c in range(2):
            tr(psb[0:8, c * 128:(c + 1) * 128], ckvn[:, c, c0:c0 + 8], identb, ['ckvn', 'identb'], ['ps%d' % pb])
        cp('dve', KnN[0:8, 0:256], psb[0:8, 0:256], ['ps%d' % pb], ['KnN'])
        sb = 6 + nxt('sb', 2)
        mm(PS[sb][0:8, 0:64], ckvn[:, 0, c0:c0 + 8], QA[:, s_, 0, :], True, False, ['ckvn', 'QA'], ['ps%d' % sb])
        mm(PS[sb][0:8, 0:64], ckvn[:, 1, c0:c0 + 8], QA[:, s_, 1, :], False, False, ['ckvn', 'QA'], ['ps%d' % sb])
        mm(PS[sb][0:8, 0:64], KpeT[64:96, c0:c0 + 8], QAp[64:96, s_, :], False, True, ['KpeT', 'QA'], ['ps%d' % sb])
        act(PTn[0:8, :], PS[sb][0:8, 0:64], AF.Exp, ['ps%d' % sb], ['PTn'], scale=ATTN_SCALE)
        tt(PTn[0:8, :], PTn[0:8, :], mskn[0:8, :], ALU.mult, ['PTn', 'mskn'], ['PTn'])
        mm(PS[OB][0:64, 0:257], PTn[0:8, :], KnN[0:8, 0:257], False, True, ['PTn', 'KnN'], ['ps%d' % OB])
        recip(rs1[0:64, :], PS[OB][0:64, 256:257], ['ps%d' % OB], ['rs1'])
        ts(onb[0:64, :], PS[OB][0:64, 0:256], rs1[0:64, 0:1], None, ALU.mult, None, ['ps%d' % OB, 'rs1'], ['onb'])
        pb = nxt('pj', 4)
        psb = PS[pb][:].bitcast(BF16)
        for c in range(2):
            tr(psb[:, c * 64:(c + 1) * 64], onb[0:64, c * 128:(c + 1) * 128], identb[0:64, 0:64], ['onb', 'identb'], ['ps%d' % pb])
        cp('dve', olT, psb[:, 0:128].rearrange('p (c q) -> p c q', c=2), ['ps%d' % pb], ['olT'])
        pb = nxt('pj', 4)
        for h in range(8):
            for c in range(2):
                mm(PS[pb][(h % 2) * 64:(h % 2) * 64 + 64, (h // 2) * 8:(h // 2) * 8 + 8], Wuv[:, c, h * 64:(h + 1) * 64],
                   olT[:, c, h * 8:(h + 1) * 8], c == 0, c == 1, ['Wuv', 'olT'], ['ps%d' % pb])
        cp('dve', attn_o[:, :, c0:c0 + 8], PS[pb][:, 0:32].rearrange('p (a t) -> p a t', a=4), ['ps%d' % pb], ['attn_o'])

    for i in range(len(groups) + 2):
        if i < len(groups):
            st0(*groups[i])
        if 0 <= i - 1 < len(groups):
            st1(*groups[i - 1])
        if 0 <= i - 2 < len(groups):
            st2(*groups[i - 2])
    if debug:
        dbg_attn = dout('dbg_attn', [128, 4, NTOK], BF16)
        dma_out(dbg_attn, attn_o, ['attn_o'])
    pr.barrier()
    AR.off = mATT

    class StopS5(Exception):
        pass

    try:
        LVL = int(os.environ.get('KLVL', '99'))
        ssm_o = AR.alloc([4, NTOK], BF16)
        mSSM = AR.off
        BreT = AR.alloc([16, 128], BF16); BimT = AR.alloc([16, 128], BF16)
        CreT = AR.alloc([16, 128], BF16); NCreT = AR.alloc([16, 128], BF16); NCimT = AR.alloc([16, 128], BF16)
        prm = AR.alloc([24, 16])
        H0re = AR.alloc([4, 16]); H0im = AR.alloc([4, 16]); HSre = AR.alloc([16]); HSim = AR.alloc([16])
        HSsre = AR.alloc([4, 16]); HSsim = AR.alloc([4, 16]); smT = AR.alloc([32]); rhoS = AR.alloc([32])
        cry = AR.alloc([4]); t4a = AR.alloc([4]); t4b = AR.alloc([4]); ah_re = AR.alloc([4]); ah_im = AR.alloc([4]); c1t = AR.alloc([4])
        PN = ['are', 'aim', 'ldt', 'dt', 'lam', 'th', 'rho', 'sth', 'cth', 'abr', 'abi', 'den', 'rden', 'nr', 'fr', 'fi',
              'u1', 'u2', 'u3', 'u4', 'a512', 's512', 'c512']
        P_ = {nm: prm[:, i, :] for i, nm in enumerate(PN)}
        mS5 = AR.off
        INV2PI = 1.0 / (2 * math.pi); MAGIC = 12582912.0; C1 = 6.28125; C2 = 2 * math.pi - 6.28125

        def sincos(s_out, c_out, ang, ta, tb, r, w_s, w_c, tok):
            ts(ta, ang, INV2PI, None, ALU.mult, None, r, [tok + 'a'])
            ts(ta, ta, MAGIC, None, ALU.add, None, [tok + 'a'], [tok + 'a'])
            ts(ta, ta, -MAGIC, None, ALU.add, None, [tok + 'a'], [tok + 'a'])
            stt(tb, ta, -C1, ang, ALU.mult, ALU.add, list(r) + [tok + 'a'], [tok + 'b'])
            stt(tb, ta, -C2, tb, ALU.mult, ALU.add, [tok + 'a', tok + 'b'], [tok + 'b'])
            ts(tb, tb, math.pi, -math.pi, ALU.min, ALU.max, [tok + 'b'], [tok + 'b'])
            act(s_out, tb, AF.Sin, [tok + 'b'], w_s)
            stt(ta, tb, -1.0, tb, ALU.mult, ALU.max, [tok + 'b'], [tok + 'a'])
            act(c_out, ta, AF.Sin, [tok + 'a', 'hpiT'], w_c, scale=-1.0, bias=hpiT)

        araw = AR.alloc([3, 128]); ldr = AR.alloc([2])
        dma(araw[0:16, 0, :], a_re, [], ['araw']); dma(araw[0:16, 1, :], a_im, [], ['araw'])
        dma(ldr[0:16, :], log_dt, [], ['ldr'])
        cp('dve', araw[0:16, 2, :].rearrange('p (g n) -> p g n', g=2), ldr[0:16, :].unsqueeze(2).to_broadcast([16, 2, 64]),
           ['ldr', 'araw'], ['araw'])
        pb = nxt('pj', 4)
        for i in range(3):
            tr(PS[pb][:, i * 16:(i + 1) * 16], araw[0:16, i, :], ident[0:16, 0:16], ['araw', 'ident'], ['ps%d' % pb])
        cp('dve', prm[:, 0:3, :], PS[pb][:, 0:48].rearrange('p (a b) -> p a b', a=3), ['ps%d' % pb], ['prm'])
        TP = ['prm']
        act(P_['dt'], P_['ldt'], AF.Exp, TP, TP)
        tt(P_['lam'], P_['dt'], P_['are'], ALU.mult, TP, TP)
        tt(P_['th'], P_['dt'], P_['aim'], ALU.mult, TP, TP)
        act(P_['rho'], P_['lam'], AF.Exp, TP, TP)
        sincos(P_['sth'], P_['cth'], P_['th'], P_['u1'], P_['u2'], TP, TP, TP, 'prm')
        ts(P_['a512'], P_['th'], 512.0, None, ALU.mult, None, TP, TP)
        sincos(P_['s512'], P_['c512'], P_['a512'], P_['u3'], P_['u4'], TP, TP, TP, 'prm')
        tt(P_['abr'], P_['rho'], P_['cth'], ALU.mult, TP, TP)
        tt(P_['abi'], P_['rho'], P_['sth'], ALU.mult, TP, TP)
        tt(P_['u1'], P_['are'], P_['are'], ALU.mult, TP, TP)
        tt(P_['u2'], P_['aim'], P_['aim'], ALU.mult, TP, TP)
        tt(P_['den'], P_['u1'], P_['u2'], ALU.add, TP, TP)
        recip(P_['rden'], P_['den'], TP, TP)
        ts(P_['nr'], P_['abr'], -1.0, None, ALU.add, None, TP, TP)
        tt(P_['u1'], P_['nr'], P_['are'], ALU.mult, TP, TP)
        tt(P_['u2'], P_['abi'], P_['aim'], ALU.mult, TP, TP)
        tt(P_['u1'], P_['u1'], P_['u2'], ALU.add, TP, TP)
        tt(P_['fr'], P_['u1'], P_['rden'], ALU.mult, TP, TP)
        tt(P_['u1'], P_['abi'], P_['are'], ALU.mult, TP, TP)
        tt(P_['u2'], P_['nr'], P_['aim'], ALU.mult, TP, TP)
        tt(P_['u1'], P_['u1'], P_['u2'], ALU.subtract, TP, TP)
        tt(P_['fi'], P_['u1'], P_['rden'], ALU.mult, TP, TP)
        if LVL == 0:
            raise StopS5()
        sraw = AR.alloc([2, 128])
        dma(sraw[0:64, 0, :], st_re, [], ['sraw']); dma(sraw[0:64, 1, :], st_im, [], ['sraw'])
        pb = nxt('pj', 4)
        tr(PS[pb][:, 0:64], sraw[0:64, 0, :], ident[0:64, 0:64], ['sraw', 'ident'], ['ps%d' % pb])
        tr(PS[pb][:, 64:128], sraw[0:64, 1, :], ident[0:64, 0:64], ['sraw', 'ident'], ['ps%d' % pb])
        cp('dve', H0re, PS[pb][:, 0:64].rearrange('p (s r) -> p s r', s=4), ['ps%d' % pb], ['H0'])
        cp('dve', H0im, PS[pb][:, 64:128].rearrange('p (s r) -> p s r', s=4), ['ps%d' % pb], ['H0'])
        dma(smT, k_smask.partition_broadcast(128), [], ['smT'])
        if LVL == -1:
            raise StopS5()
        Braw_re = AR.alloc([16, 16]); Braw_im = AR.alloc([16, 16]); t16a = AR.alloc([16]); t16b = AR.alloc([16])
        Bexp_re = AR.alloc([16, 128]); Bexp_im = AR.alloc([16, 128])
        dma(Braw_re, b_re.rearrange('(r q) c -> q r c', q=128), [], ['Braw'])
        dma(Braw_im, b_im.rearrange('(r q) c -> q r c', q=128), [], ['Braw'])
        memset('pool', Bexp_re, 0.0, ['Bexp']); memset('pool', Bexp_im, 0.0, ['Bexp'])
        for pr_ in range(16 if LVL >= 2 else 0):
            for gi in range(2):
                rows = slice(64 * gi, 64 * gi + 64)
                col0 = (pr_ % 4) * 32 + gi * 16
                fr_, fi_ = P_['fr'][rows, pr_:pr_ + 1], P_['fi'][rows, pr_:pr_ + 1]
                ts(t16a[rows, :], Braw_im[rows, pr_, :], fi_, None, ALU.mult, None, ['Braw', 'prm'], ['t16a'])
                stt(Bexp_re[rows, pr_, col0:col0 + 16], Braw_re[rows, pr_, :], fr_, t16a[rows, :], ALU.mult, ALU.subtract,
                    ['Braw', 'prm', 't16a'], ['Bexp'])
                ts(t16b[rows, :], Braw_re[rows, pr_, :], fi_, None, ALU.mult, None, ['Braw', 'prm'], ['t16b'])
                stt(Bexp_im[rows, pr_, col0:col0 + 16], Braw_im[rows, pr_, :], fr_, t16b[rows, :], ALU.mult, ALU.add,
                    ['Braw', 'prm', 't16b'], ['Bexp'])
        for src_, dst_, tok in ((Bexp_re, BreT, 'BreT'), (Bexp_im, BimT, 'BimT')):
            for q4 in range(4):
                pb = nxt('pj', 4)
                for j in range(4):
                    tr(PS[pb][:, j * 128:(j + 1) * 128], src_[:, q4 * 4 + j, :], ident, ['Bexp', 'ident'], ['ps%d' % pb])
                cp('act', dst_[:, q4 * 4:q4 * 4 + 4, :], PS[pb][:].rearrange('p (a b) -> p a b', a=4), ['ps%d' % pb], [tok])
        pr.barrier()
        AR.off = mS5
        if LVL == -2:
            raise StopS5()
        X_re = AR.alloc([4, 512]); X_im = AR.alloc([4, 512])
        memset('pool', X_re, 0.0, ['X_re']); memset('pool', X_im, 0.0, ['X_im'])
        for g_ in range(32 if LVL >= 3 else 0):
            r0 = 16 * (g_ % 8)
            cc0 = ((g_ % 8) // 2) * 128 + (g_ % 2) * 64
            dma(X_re[r0:r0 + 16, g_ // 8, cc0:cc0 + 64], cc_re[g_], [], ['X_re'])
            dma(X_im[r0:r0 + 16, g_ // 8, cc0:cc0 + 64], cc_im[g_], [], ['X_im'])
        for q4 in range(4):
            pb = nxt('pj', 4)
            for j in range(4):
                tr(PS[pb][:, j * 128:(j + 1) * 128], X_re[:, q4, j * 128:(j + 1) * 128], ident, ['X_re', 'ident'], ['ps%d' % pb])
            v = PS[pb][:].rearrange('p (a b) -> p a b', a=4)
            cp('act', CreT[:, q4 * 4:q4 * 4 + 4, :], v, ['ps%d' % pb], ['CreT'])
            act(NCreT[:, q4 * 4:q4 * 4 + 4, :], v, AF.Copy, ['ps%d' % pb], ['NCreT'], scale=-1.0)
            pb = nxt('pj', 4)
            for j in range(4):
                tr(PS[pb][:, j * 128:(j + 1) * 128], X_im[:, q4, j * 128:(j + 1) * 128], ident, ['X_im', 'ident'], ['ps%d' % pb])
            act(NCimT[:, q4 * 4:q4 * 4 + 4, :], PS[pb][:].rearrange('p (a b) -> p a b', a=4), AF.Copy, ['ps%d' % pb], ['NCimT'],
                scale=-1.0)
        pr.barrier()
        AR.off = mS5
        if LVL == -3:
            raise StopS5()
        cosT2 = [AR.alloc([512]) for _ in range(2)]; sinT2 = [AR.alloc([512]) for _ in range(2)]; iotaT = AR.alloc([512])
        tg_ang = AR.alloc([512]); tg_a = AR.alloc([512]); tg_b = AR.alloc([512])
        t_ang = AR.alloc([1024]); t_a = AR.alloc([1024]); t_b = AR.alloc([1024])
        Sre = [AR.alloc([512]) for _ in range(2)]; Sim = [AR.alloc([512]) for _ in range(2)]
        Zb = AR.alloc([4, 512], BF16)
        dma(iotaT, k_iota[0:1, 0:512].partition_broadcast(128), [], ['iotaT'])
        m1, m2, g_re, g_im = t_a[:, 0:512], t_a[:, 512:1024], t_b[:, 0:512], t_b[:, 512:1024]
        m3, m4 = t_ang[:, 0:512], t_ang[:, 512:1024]
        G2 = [t_b, AR.alloc([1024])]
        YB = [0, 1, 2, 3, 4]

        def gen_tables(p_):
            ts(tg_ang, iotaT, P_['th'][:, p_:p_ + 1], None, ALU.mult, None, ['iotaT', 'prm'], ['tg_ang'])
            sincos(sinT2[p_ % 2], cosT2[p_ % 2], tg_ang, tg_a, tg_b, ['tg_ang'], ['sinT%d' % (p_ % 2)], ['cosT%d' % (p_ % 2)], 'tg_')
        for pr_ in range({4: 1, 5: 4}.get(LVL, 16) if LVL >= 4 else 0):
            qc = pr_ // 4
            thp = P_['th'][:, pr_:pr_ + 1]
            rho_p = P_['rho'][:, pr_:pr_ + 1]
            if pr_ == 0:
                gen_tables(0)
            if pr_ + 1 < 16:
                gen_tables(pr_ + 1)
            cosT, sinT = cosT2[pr_ % 2], sinT2[pr_ % 2]
            tcs, tsn = 'cosT%d' % (pr_ % 2), 'sinT%d' % (pr_ % 2)
            ts(rhoS, smT, rho_p, None, ALU.mult, None, ['smT', 'prm'], ['rhoS'])
            state = {'prev': None}

            def s5pre(bi):
                off, n = BLOCKS[bi]
                g_re, g_im = G2[bi % 2][:, 0:512], G2[bi % 2][:, 512:1024]
                tgb = 't_b%d' % (bi % 2)
                mm(PS[5][:, 0:n], BreT[:, pr_, :], uT[:, qc, off:off + n], True, True, ['BreT', 'uT'], ['ps5'])
                mm(PS[6][:, 0:n], BimT[:, pr_, :], uT[:, qc, off:off + n], True, True, ['BimT', 'uT'], ['ps6'])
                if bi < 4:
                    cs, sn = cosT[:, 0:n], sinT[:, 0:n]
                    vw = lambda a: a
                else:
                    cs = cosT[:, 0:8].unsqueeze(1).to_broadcast([128, 4, 8])
                    sn = sinT[:, 0:8].unsqueeze(1).to_broadcast([128, 4, 8])
                    vw = lambda a: a.rearrange('p (s t) -> p s t', s=4)
                TB = [tcs, tsn]
                PE_ = 'pool' if (bi < 4 and os.environ.get('KS5POOL', '0') == '1') else 'dve'
                tt(vw(m1[:, 0:n]), vw(PS[5][:, 0:n]), cs, ALU.mult, ['ps5'] + TB, ['t_a'])
                tt(vw(m2[:, 0:n]), vw(PS[6][:, 0:n]), sn, ALU.mult, ['ps6'] + TB, ['t_a'])
                tt(g_re[:, 0:n], m1[:, 0:n], m2[:, 0:n], ALU.add, ['t_a'], [tgb], eng=PE_)
                tt(vw(m3[:, 0:n]), vw(PS[6][:, 0:n]), cs, ALU.mult, ['ps6'] + TB, ['t_ang'])
                tt(vw(m4[:, 0:n]), vw(PS[5][:, 0:n]), sn, ALU.mult, ['ps5'] + TB, ['t_ang'])
                tt(g_im[:, 0:n], m3[:, 0:n], m4[:, 0:n], ALU.subtract, ['t_ang'], [tgb], eng=PE_)

            def s5post(bi):
                off, n = BLOCKS[bi]
                g_re, g_im = G2[bi % 2][:, 0:512], G2[bi % 2][:, 512:1024]
                tgb = 't_b%d' % (bi % 2)
                if bi < 4:
                    cs, sn = cosT[:, 0:n], sinT[:, 0:n]
                    vw = lambda a: a
                else:
                    cs = cosT[:, 0:8].unsqueeze(1).to_broadcast([128, 4, 8])
                    sn = sinT[:, 0:8].unsqueeze(1).to_broadcast([128, 4, 8])
                    vw = lambda a: a.rearrange('p (s t) -> p s t', s=4)
                TB = [tcs, tsn]
                PE_ = 'dve'
                prev = state['prev']
                sl = bi % 2
                if bi < 4:
                    d0 = rho_p.to_broadcast([128, n])
                    if prev is None:
                        i_re = i_im = 0.0
                        rr = [tgb, 'prm']
                    else:
                        c5, s5 = P_['c512'][:, pr_:pr_ + 1], P_['s512'][:, pr_:pr_ + 1]
                        pr_l, pi_l = Sre[prev][:, 511:512], Sim[prev][:, 511:512]
                        ts(cry[:, 2:3], pi_l, s5, None, ALU.mult, None, ['S%d' % prev, 'prm'], ['cry'])
                        stt(cry[:, 0:1], pr_l, c5, cry[:, 2:3], ALU.mult, ALU.subtract, ['S%d' % prev, 'prm', 'cry'], ['cry'])
                        ts(cry[:, 3:4], pr_l, s5, None, ALU.mult, None, ['S%d' % prev, 'prm'], ['cry'])
                        stt(cry[:, 1:2], pi_l, c5, cry[:, 3:4], ALU.mult, ALU.add, ['S%d' % prev, 'prm', 'cry'], ['cry'])
                        i_re, i_im = cry[:, 0:1], cry[:, 1:2]
                        rr = [tgb, 'prm', 'cry']
                else:
                    ts(t4a, H0im[:, :, pr_], P_['abi'][:, pr_:pr_ + 1], None, ALU.mult, None, ['H0', 'prm'], ['t4a'])
                    stt(ah_re, H0re[:, :, pr_], P_['abr'][:, pr_:pr_ + 1], t4a, ALU.mult, ALU.subtract, ['H0', 'prm', 't4a'], ['ah'])
                    ts(t4b, H0re[:, :, pr_], P_['abi'][:, pr_:pr_ + 1], None, ALU.mult, None, ['H0', 'prm'], ['t4b'])
                    stt(ah_im, H0im[:, :, pr_], P_['abr'][:, pr_:pr_ + 1], t4b, ALU.mult, ALU.add, ['H0', 'prm', 't4b'], ['ah'])
                    gv_re = g_re[:, 0:32].rearrange('p (s t) -> p s t', s=4)[:, :, 0]
                    gv_im = g_im[:, 0:32].rearrange('p (s t) -> p s t', s=4)[:, :, 0]
                    tt(gv_re, gv_re, ah_re, ALU.add, [tgb, 'ah'], [tgb])
                    tt(gv_im, gv_im, ah_im, ALU.add, [tgb, 'ah'], [tgb])
                    d0 = rhoS[:, 0:32]
                    i_re = i_im = 0.0
                    rr = [tgb, 'rhoS']
                scan(Sre[sl][:, 0:n], d0, g_re[:, 0:n], i_re, rr, ['S%d' % sl])
                scan(Sim[sl][:, 0:n], d0, g_im[:, 0:n], i_im, rr, ['S%d' % sl])
                TS = ['S%d' % sl] + TB
                tt(vw(Zb[:, 0, 0:n]), vw(Sre[sl][:, 0:n]), cs, ALU.mult, TS, ['Zb'], eng=PE_)
                tt(vw(Zb[:, 1, 0:n]), vw(Sim[sl][:, 0:n]), sn, ALU.mult, TS, ['Zb'], eng=PE_)
                tt(vw(Zb[:, 2, 0:n]), vw(Sim[sl][:, 0:n]), cs, ALU.mult, TS, ['Zb'], eng=PE_)
                tt(vw(Zb[:, 3, 0:n]), vw(Sre[sl][:, 0:n]), sn, ALU.mult, TS, ['Zb'], eng=PE_)
                for k, (W, wt_) in enumerate(((CreT, 'CreT'), (NCreT, 'NCreT'), (NCimT, 'NCimT'), (NCimT, 'NCimT'))):
                    mm(PS[YB[bi]][:, 0:n], W[:, pr_, :], Zb[:, k, 0:n], pr_ % 4 == 0 and k == 0, pr_ % 4 == 3 and k == 3,
                       [wt_, 'Zb'], ['ps%d' % YB[bi]])
                if bi == 3:
                    cl, sl_ = cosT[:, 511:512], sinT[:, 511:512]
                    sr, si = Sre[sl][:, 511:512], Sim[sl][:, 511:512]
                    tt(c1t[:, 0:1], cl, sr, ALU.mult, TS, ['c1t']); tt(c1t[:, 1:2], sl_, si, ALU.mult, TS, ['c1t'])
                    tt(HSre[:, pr_:pr_ + 1], c1t[:, 0:1], c1t[:, 1:2], ALU.subtract, ['c1t'], ['HS'])
                    tt(c1t[:, 2:3], cl, si, ALU.mult, TS, ['c1t']); tt(c1t[:, 3:4], sl_, sr, ALU.mult, TS, ['c1t'])
                    tt(HSim[:, pr_:pr_ + 1], c1t[:, 2:3], c1t[:, 3:4], ALU.add, ['c1t'], ['HS'])
                if bi == 4:
                    sr = Sre[sl][:, 0:32].rearrange('p (s t) -> p s t', s=4)[:, :, 7]
                    si = Sim[sl][:, 0:32].rearrange('p (s t) -> p s t', s=4)[:, :, 7]
                    c7, s7 = cosT[:, 7:8], sinT[:, 7:8]
                    ts(t4a, si, s7, None, ALU.mult, None, TS, ['t4a'])
                    stt(HSsre[:, :, pr_], sr, c7, t4a, ALU.mult, ALU.subtract, TS + ['t4a'], ['HSs'])
                    ts(t4b, sr, s7, None, ALU.mult, None, TS, ['t4b'])
                    stt(HSsim[:, :, pr_], si, c7, t4b, ALU.mult, ALU.add, TS + ['t4b'], ['HSs'])
                state['prev'] = sl if bi < 4 else None

            s5pre(0)
            for bi in range(5):
                if bi + 1 < 5:
                    s5pre(bi + 1)
                s5post(bi)
            if pr_ % 4 == 3:
                for bi, (off, n) in enumerate(BLOCKS):
                    yf, x2 = t_ang[:, 0:n], t_ang[:, 512:512 + n]
                    stt(yf, uT[:, qc, off:off + n], G[:, GO['d'] + qc:GO['d'] + qc + 1], PS[YB[bi]][:, 0:n], ALU.mult, ALU.add,
                        ['uT', 'G', 'ps%d' % YB[bi]], ['t_ang'])
                    act(x2, yf, AF.Square, ['t_ang'], ['t_ang'])
                    ts(x2, x2, 0.044715, 1.0, ALU.mult, ALU.add, ['t_ang'], ['t_ang'])
                    tt(x2, x2, yf, ALU.mult, ['t_ang'], ['t_ang'])
                    act(x2, x2, AF.Sigmoid, ['t_ang'], ['t_ang'], scale=2.0 * math.sqrt(2.0 / math.pi))
                    tt(ssm_o[:, qc, off:off + n], yf, x2, ALU.mult, ['t_ang'], ['ssm_o'])
        if LVL == -4:
            raise StopS5()
        hso = t_ang[:, 0:512].rearrange('p (a b) -> p a b', a=4)
        pb = nxt('pj', 4)
        tr(PS[pb][0:16, 0:128], HSre, ident, ['HS', 'ident'], ['ps%d' % pb])
        tr(PS[pb][0:16, 128:256], HSim, ident, ['HS', 'ident'], ['ps%d' % pb])
        tr(PS[pb][0:64, 256:384], HSsre[:].rearrange('p s r -> p (s r)'), ident, ['HSs', 'ident'], ['ps%d' % pb])
        tr(PS[pb][0:64, 384:512], HSsim[:].rearrange('p s r -> p (s r)'), ident, ['HSs', 'ident'], ['ps%d' % pb])
        cp('act', hso[0:64, :, :], PS[pb][0:64, :].rearrange('p (a b) -> p a b', a=4), ['ps%d' % pb], ['t_ang'])
        dma_out(o_sre_p, hso[0:16, 0, :], ['t_ang']); dma_out(o_sim_p, hso[0:16, 1, :], ['t_ang'])
        dma_out(o_sre_s, hso[0:64, 2, :], ['t_ang']); dma_out(o_sim_s, hso[0:64, 3, :], ['t_ang'])
        pr.barrier()
        AR.off = mS5
        if LVL == -5:
            raise StopS5()
        Wglu = AR.alloc([4, 512], BF16); gate = AR.alloc([4, 512], BF16)
        cast_load(Wglu, w_glu.rearrange('(c p) n -> p c n', p=128), 'Wglu')
        for bi, (off, n) in enumerate(BLOCKS):
            for oc in range(4):
                pb = nxt('pj', 4)
                for c in range(4):
                    mm(PS[pb][:, 0:n], Wglu[:, c, oc * 128:(oc + 1) * 128], ssm_o[:, c, off:off + n], c == 0, c == 3,
                       ['Wglu', 'ssm_o'], ['ps%d' % pb])
                act(gate[:, oc, 0:n], PS[pb][:, 0:n], AF.Sigmoid, ['ps%d' % pb], ['gate'])
            for oc in range(4):
                tt(ssm_o[:, oc, off:off + n], ssm_o[:, oc, off:off + n], gate[:, oc, 0:n], ALU.mult, ['ssm_o', 'gate'], ['ssm_o'])
        if debug:
            dbg_ssm = dout('dbg_ssm', [128, 4, NTOK], BF16)
            dma_out(dbg_ssm, ssm_o, ['ssm_o'])
        pr.barrier()
        AR.off = mS5


    except StopS5:
        pr.barrier()
        AR.off = mS5

    mM0 = AR.off
    if '2' in KOLD:
        KmT = AR.alloc([4, 256], BF16); Vm = AR.alloc([2, 512], BF16)
    memtm = AR.alloc([2, 1024]); memT = AR.alloc([8, 256]); mnT = AR.alloc([8, 256], BF16)
    Wkm = AR.alloc([8, 512], BF16); Wvm = AR.alloc([8, 512], BF16); mko = AR.alloc([2, 512])
    cast_load(Wkm, w_km.rearrange('(c p) n -> p c n', p=128), 'Wkm')
    cast_load(Wvm, w_vm.rearrange('(c p) n -> p c n', p=128), 'Wvm')
    for t in range(2):
        dma(memtm[:, t, :], mem_p[t * 128:(t + 1) * 128, :], [], ['memtm%d' % t])
        for half in range(2):
            pb = nxt('pj', 4)
            for c4 in range(4):
                c = half * 4 + c4
                tr(PS[pb][:, c4 * 128:(c4 + 1) * 128], memtm[:, t, c * 128:(c + 1) * 128], ident, ['memtm%d' % t, 'ident'],
                   ['ps%d' % pb])
            cp('act' if half else 'dve', memT[:, half * 4:half * 4 + 4, t * 128:(t + 1) * 128],
               PS[pb][:].rearrange('p (a b) -> p a b', a=4), ['ps%d' % pb], ['memT'])
    NORM(memT, 8, 256, 'mem', 1024, mnT, ['memT'], ['mnT'])
    for t in range(2):
        for wi, (W, wtok, dst) in enumerate(((Wkm, 'Wkm', o_mk_p), (Wvm, 'Wvm', o_mv_p))):
            pb = nxt('pj', 4)
            sl = nxt('mko', 2)
            for k in range(8):
                mm(PS[pb][:, :], mnT[:, k, t * 128:(t + 1) * 128], W[:, k, :], k == 0, k == 7, ['mnT', wtok], ['ps%d' % pb])
            cp('act', mko[:, sl, :], PS[pb][:, :], ['ps%d' % pb], ['mko%d' % sl])
            if wi == 1 and os.environ.get('KM0', '1') == '1':
                cp('dve', Vm[:, t, :], mko[:, sl, :], ['mko%d' % sl], ['Vm'])
            dma_out(dst[t * 128:(t + 1) * 128, :], mko[:, sl, :], ['mko%d' % sl])
    for hd in range(4 if os.environ.get('KM0', '1') == '1' else 0):
        pb = nxt('pj', 4)
        for k in range(8):
            mm(PS[pb][:, 0:256], Wkm[:, k, hd * 128:(hd + 1) * 128], mnT[:, k, :], k == 0, k == 7, ['Wkm', 'mnT'], ['ps%d' % pb])
        cp('act', KmT[:, hd, :], PS[pb][:, 0:256], ['ps%d' % pb], ['KmT'])
    pr.barrier()
    AR.off = mM0

    class StopX(Exception):
        pass

    KCUT = int(os.environ.get('KCUT', '99'))
    try:
        AR.off = mSSM
        if KCUT == 0:
            raise StopX()
        Wout = AR.alloc([8, 1024], BF16)
        mixin2 = [AR.alloc([8, 512], BF16) for _ in range(2)]; f_sb2 = [AR.alloc([8, 512])] * 2
        NSB = norm_set()
        cast_load(Wout, w_out.rearrange('(c p) n -> p c n', p=128), 'Wout')
        SKIP = os.environ.get('KSKIP', '')

        def mixA(bi):
            off, n = BLOCKS[bi]
            NORM(attn_o[:, :, off:off + n], 4, n, 'attn', 512, mixin2[bi % 2][:, 0:4, 0:n], ['attn_o'], ['mixin%d' % (bi % 2)])
            NORM(ssm_o[:, :, off:off + n], 4, n, 'ssm', 512, mixin2[bi % 2][:, 4:8, 0:n], ['ssm_o'], ['mixin%d' % (bi % 2)])

        def mixB(bi):
            off, n = BLOCKS[bi]
            for oc in range(8):
                pb = nxt('pj', 4)
                for k in range(8):
                    mm(PS[pb][:, 0:n], Wout[:, k, oc * 128:(oc + 1) * 128], mixin2[bi % 2][:, k, 0:n], k == 0, k == 7,
                       ['Wout', 'mixin%d' % (bi % 2)], ['ps%d' % pb])
                cp('act' if oc % 2 else 'dve', f_sb2[bi % 2][:, oc, 0:n], PS[pb][:, 0:n], ['ps%d' % pb], ['f_sbm'])

        def mixC(bi):
            off, n = BLOCKS[bi]
            NORM(f_sb2[bi % 2][:, :, 0:n], 8, n, 'mix_post', 1024, None, ['f_sbm'], [t_xT[bi]],
                 resid=xT[:, :, off:off + n], ns=NSB)

        if 'x' not in SKIP:
            mixA(0)
            for bi in range(5):
                if bi + 1 < 5:
                    mixA(bi + 1)
                mixB(bi)
                mixC(bi)
        pr.barrier()
        AR.off = mUT

        if KCUT == 1:
            raise StopX()
        Wqm = AR.alloc([8, 512], BF16); Wom = AR.alloc([4, 1024], BF16)
        KmTs = AR.alloc([4, 4, 256], BF16); Vms = AR.alloc([4, 2, 512], BF16)
        mkr = AR.alloc([2, 2, 512])
        hT2 = AR.alloc([8, 512], BF16); qmT = AR.alloc([4, 512], BF16); omT = AR.alloc([4, 512], BF16)
        osm = AR.alloc([512]); rsm = AR.alloc([512]); f_sb = AR.alloc([8, 512])
        cast_load(Wqm, w_qm.rearrange('(c p) n -> p c n', p=128), 'Wqm')
        cast_load(Wom, w_om.rearrange('(c p) n -> p c n', p=128), 'Wom')
        for s_ in range(4):
            sl = s_ % 2
            dma(mkr[:, sl, :, :], memk[s_].rearrange('(t p) d -> p t d', p=128), [], ['mkr%d' % sl])
            cast_load(Vms[:, s_, :, :], memv[s_].rearrange('(t p) d -> p t d', p=128), 'Vms')
            for t in range(2):
                pb = nxt('pj', 4)
                for hd in range(4):
                    tr(PS[pb][:, hd * 128:(hd + 1) * 128], mkr[:, sl, t, hd * 128:(hd + 1) * 128], ident, ['mkr%d' % sl, 'ident'],
                       ['ps%d' % pb])
                cp('act' if t else 'dve', KmTs[:, s_, :, t * 128:(t + 1) * 128], PS[pb][:].rearrange('p (a b) -> p a b', a=4),
                   ['ps%d' % pb], ['KmTs'])
        if KCUT == 2:
            raise StopX()
        hT2b = [hT2, AR.alloc([8, 512], BF16)]; qmTb = [qmT, AR.alloc([4, 512], BF16)]
        PTm = [AR.alloc([2, 512], BF16) for _ in range(2)]
        NSB2 = norm_set()

        def memA(bi):
            off, n = BLOCKS[bi]
            h2, q2 = hT2b[bi % 2], qmTb[bi % 2]
            NORM(xT[:, :, off:off + n], 8, n, 'mem_pre', 1024, h2[:, :, 0:n], [t_xT[bi]], ['hT2%d' % (bi % 2)])
            for hd in range(4):
                pb = nxt('pj', 4)
                for k in range(8):
                    mm(PS[pb][:, 0:n], Wqm[:, k, hd * 128:(hd + 1) * 128], h2[:, k, 0:n], k == 0, k == 7,
                       ['Wqm', 'hT2%d' % (bi % 2)], ['ps%d' % pb])
                cp('act', q2[:, hd, 0:n], PS[pb][:, 0:n], ['ps%d' % pb], ['qmT%d' % (bi % 2)])

        def memS(bi, hd):
            off, n = BLOCKS[bi]
            q2, tq2 = qmTb[bi % 2], 'qmT%d' % (bi % 2)
            pi = hd % 2
            if bi < 4:
                for t in range(2):
                    sb = 6 + t
                    mm(PS[sb][:, 0:n], KmT[:, hd, t * 128:(t + 1) * 128], q2[:, hd, 0:n], True, True, ['KmT', tq2], ['ps%d' % sb])
                    act(PTm[pi][:, t, 0:n], PS[sb][:, 0:n], AF.Exp, ['ps%d' % sb], ['PTm%d' % pi], scale=MEM_SCALE)
            else:
                sb = 6 + hd % 2
                for s_ in range(4):
                    for t in range(2):
                        c_ = (s_ * 2 + t) * 8
                        mm(PS[sb][:, c_:c_ + 8], KmTs[:, s_, hd, t * 128:(t + 1) * 128], q2[:, hd, 8 * s_:8 * s_ + 8], True, True,
                           ['KmTs', tq2], ['ps%d' % sb])
                act(PTm[pi][:, 0, 0:64], PS[sb][:, 0:64], AF.Exp, ['ps%d' % sb], ['PTm%d' % pi], scale=MEM_SCALE)

        def memO(bi, hd):
            off, n = BLOCKS[bi]
            pi = hd % 2
            bo, bs = (4, 5)
            if bi < 4:
                for t in range(2):
                    mm(PS[bo][:, 0:n], Vm[:, t, hd * 128:(hd + 1) * 128], PTm[pi][:, t, 0:n], t == 0, t == 1, ['Vm', 'PTm%d' % pi],
                       ['ps%d' % bo])
                    mm(PS[bs][:, 0:n], ones_b, PTm[pi][:, t, 0:n], t == 0, t == 1, ['ones_b', 'PTm%d' % pi], ['ps%d' % bs])
            else:
                for s_ in range(4):
                    for t in range(2):
                        c_ = (s_ * 2 + t) * 8
                        mm(PS[bo][:, 8 * s_:8 * s_ + 8], Vms[:, s_, t, hd * 128:(hd + 1) * 128], PTm[pi][:, 0, c_:c_ + 8],
                           t == 0, t == 1, ['Vms', 'PTm%d' % pi], ['ps%d' % bo])
                        mm(PS[bs][:, 8 * s_:8 * s_ + 8], ones_b, PTm[pi][:, 0, c_:c_ + 8], t == 0, t == 1,
                           ['ones_b', 'PTm%d' % pi], ['ps%d' % bs])
            recip(rsm[:, 0:n], PS[bs][:, 0:n], ['ps%d' % bs], ['rsm'])
            cp('act', osm[:, 0:n], PS[bo][:, 0:n], ['ps%d' % bo], ['osm'])
            tt(omT[:, hd, 0:n], osm[:, 0:n], rsm[:, 0:n], ALU.mult, ['osm', 'rsm'], ['omT'])

        def memB(bi):
            off, n = BLOCKS[bi]
            memS(bi, 0)
            for hd in range(4):
                if hd + 1 < 4:
                    memS(bi, hd + 1)
                memO(bi, hd)
            for oc in range(8):
                pb = nxt('pj', 4)
                for k in range(4):
                    mm(PS[pb][:, 0:n], Wom[:, k, oc * 128:(oc + 1) * 128], omT[:, k, 0:n], k == 0, k == 3, ['Wom', 'omT'],
                       ['ps%d' % pb])
                cp('act' if oc % 2 else 'dve', f_sb[:, oc, 0:n], PS[pb][:, 0:n], ['ps%d' % pb], ['f_sb'])

        def memC(bi):
            off, n = BLOCKS[bi]
            NORM(f_sb[:, :, 0:n], 8, n, 'mem_post', 1024, None, ['f_sb'], [t_xT[bi]], resid=xT[:, :, off:off + n], ns=NSB2)

        if 'm' not in SKIP:
            memA(0)
            for bi in range(5):
                if bi + 1 < 5:
                    memA(bi + 1)
                memB(bi)
                memC(bi)
        pr.barrier()
        AR.off = mUT

        if KCUT == 3:
            raise StopX()
        gprev = AR.alloc([22, 2]); Gst = AR.alloc([22, 8]); GoutP = AR.alloc([2, 22]); GoutS = AR.alloc([4, 2, 22])
        cvo = AR.alloc([128])
        mF = AR.off
        stc = AR.alloc([2816])
        dma(stc[0:8, :], st_cv, [], ['stc'])
        pb = nxt('pj', 4)
        for j in range(22):
            tr(PS[pb][:, j * 8:(j + 1) * 8], stc[0:8, j * 128:(j + 1) * 128], ident[0:8, 0:8], ['stc', 'ident'], ['ps%d' % pb])
        cp('dve', Gst, PS[pb][:, 0:176].rearrange('p (j a) -> p j a', j=22), ['ps%d' % pb], ['Gst'])
        pr.barrier()
        AR.off = mF
        if KCUT == 4:
            raise StopX()
        hid = AR.alloc([22, 1056], BF16); f_all = AR.alloc([8, 1056])
        hT3 = f_all[:, 0:4, :].bitcast(BF16)
        hT3 = hT3.rearrange('p a b -> p (a b)')[:, 0:8 * 1056].rearrange('p (a b) -> p a b', a=8)
        Wg = [AR.alloc([8, 128], BF16) for _ in range(2)]; Wu = [AR.alloc([8, 128], BF16) for _ in range(2)]
        Wd = [AR.alloc([22, 128], BF16) for _ in range(2)]
        gsb = [AR.alloc([516]) for _ in range(2)]; a1 = [AR.alloc([512]) for _ in range(2)]
        memset('pool', gprev, 0.0, ['gprev'])
        for sbi, blks in enumerate(((0, 1), (2, 3, 4)) if 'f' not in SKIP else ()):
            sb_off = BLOCKS[blks[0]][0]
            for bi in blks:
                off, n = BLOCKS[bi]
                loc = off - sb_off
                NORM(xT[:, :, off:off + n], 8, n, 'ffn_pre', 1024, hT3[:, :, loc:loc + n], [t_xT[bi]], ['hT3'])
            for j in range(22):
                ws = j % 2
                cast_load(Wg[ws].rearrange('p k f -> p (k f)'), w_gate[j], 'Wg%d' % ws)
                cast_load(Wu[ws].rearrange('p k f -> p (k f)'), w_up[j], 'Wu%d' % ws)
                cw0 = G[:, GO['cw0'] + j:GO['cw0'] + j + 1]; cw1 = G[:, GO['cw1'] + j:GO['cw1'] + j + 1]
                cw2 = G[:, GO['cw2'] + j:GO['cw2'] + j + 1]; cb = G[:, GO['cb'] + j:GO['cb'] + j + 1]
                for bi in blks:
                    off, n = BLOCKS[bi]
                    loc = off - sb_off
                    pg = nxt('fg', 2); pu = 2 + nxt('fu', 2)
                    for k in range(8):
                        mm(PS[pg][:, 0:n], Wg[ws][:, k, :], hT3[:, k, loc:loc + n], k == 0, k == 7, ['Wg%d' % ws, 'hT3'], ['ps%d' % pg])
                    for k in range(8):
                        mm(PS[pu][:, 0:n], Wu[ws][:, k, :], hT3[:, k, loc:loc + n], k == 0, k == 7, ['Wu%d' % ws, 'hT3'], ['ps%d' % pu])
                    gs = nxt('gsb', 2)
                    tg, ta1 = 'gsb%d' % gs, 'a1%d' % gs
                    if bi < 4:
                        cp('act', gsb[gs][:, 2:2 + n], PS[pg][:, 0:n], ['ps%d' % pg], [tg])
                        cp('dve', gsb[gs][:, 0:2], gprev[:, j, :], ['gprev'], [tg])
                        cp('dve', gprev[:, j, :], gsb[gs][:, n:n + 2], [tg], ['gprev'])
                        if bi == 3:
                            cp('dve', GoutP[:, :, j], gsb[gs][:, n:n + 2], [tg], ['GoutP'])
                        v0, v1, v2 = gsb[gs][:, 0:n], gsb[gs][:, 1:n + 1], gsb[gs][:, 2:n + 2]
                        va = a1[gs][:, 0:n]; vu = PS[pu][:, 0:n]; vh = hid[:, j, loc:loc + n]
                    else:
                        g3 = gsb[gs][:, 0:40].rearrange('p (s t) -> p s t', s=4)
                        cp('act', g3[:, :, 2:10], PS[pg][:, 0:32].rearrange('p (s t) -> p s t', s=4), ['ps%d' % pg], [tg])
                        cp('dve', g3[:, :, 0:2], Gst[:, j, :].rearrange('p (s r) -> p s r', s=4), ['Gst'], [tg])
                        cp('dve', GoutS[:, :, :, j], g3[:, :, 8:10], [tg], ['GoutS'])
                        v0, v1, v2 = g3[:, :, 0:8], g3[:, :, 1:9], g3[:, :, 2:10]
                        va = a1[gs][:, 0:32].rearrange('p (s t) -> p s t', s=4)
                        vu = PS[pu][:, 0:32].rearrange('p (s t) -> p s t', s=4)
                        vh = hid[:, j, loc:loc + 32].rearrange('p (s t) -> p s t', s=4)
                    ts(va, v0, cw0, cb, ALU.mult, ALU.add, [tg, 'G'], [ta1])
                    stt(va, v1, cw1, va, ALU.mult, ALU.add, [tg, 'G', ta1], [ta1])
                    stt(va, v2, cw2, va, ALU.mult, ALU.add, [tg, 'G', ta1], [ta1])
                    act(va, va, AF.Silu, [ta1], [ta1])
                    tt(vh, va, vu, ALU.mult, [ta1, 'ps%d' % pu], ['hid'])
            pr.barrier()
            for c in range(8):
                ws = c % 2
                cast_load(Wd[ws].rearrange('p j d -> p (j d)'), w_down[c], 'Wd%d' % ws)
                for bi in blks:
                    off, n = BLOCKS[bi]
                    loc = off - sb_off
                    pb = nxt('pj', 4)
                    for j in range(22):
                        mm(PS[pb][:, 0:n], Wd[ws][:, j, :], hid[:, j, loc:loc + n], j == 0, j == 21, ['Wd%d' % ws, 'hid'], ['ps%d' % pb])
                    cp('act' if c % 2 else 'dve', f_all[:, c, loc:loc + n], PS[pb][:, 0:n], ['ps%d' % pb], ['f_all'])
            for bi in blks:
                off, n = BLOCKS[bi]
                loc = off - sb_off
                NORM(f_all[:, :, loc:loc + n], 8, n, 'ffn_post', 1024, None, ['f_all'], [t_xT[bi]], resid=xT[:, :, off:off + n])
            pr.barrier()
        if KCUT == 5:
            raise StopX()
        pb = nxt('pj', 4)
        tr(PS[pb][0:44, 0:128], GoutP[:].rearrange('p r j -> p (r j)'), ident, ['GoutP', 'ident'], ['ps%d' % pb])
        cp('act', cvo[0:44, :], PS[pb][0:44, 0:128], ['ps%d' % pb], ['cvo'])
        dma_out(o_cv_p.rearrange('r (j p) -> (r j) p', p=128), cvo[0:44, :], ['cvo'])
        gs2 = GoutS[:].rearrange('p s r j -> p (s r j)')
        for hf in range(2):
            pb = nxt('pj', 4)
            tr(PS[pb][0:88, 0:128], gs2[:, hf * 88:(hf + 1) * 88], ident, ['GoutS', 'ident'], ['ps%d' % pb])
            cp('act', cvo[0:88, :], PS[pb][0:88, 0:128], ['ps%d' % pb], ['cvo'])
            dma_out(o_cv_s.rearrange('a (j p) -> (a j) p', p=128)[hf * 88:(hf + 1) * 88, :], cvo[0:88, :], ['cvo'])
        pr.barrier()
        AR.off = mUT


    except StopX:
        pr.barrier()
        AR.off = mUT

    ytm = AR.alloc([2, 1024])
    for bi, (off, n) in enumerate(BLOCKS):
        for t0 in range(0, n, 128):
            nt = min(128, n - t0)
            sl = nxt('ytm', 2)
            for half in range(2):
                pb = nxt('pj', 4)
                for c4 in range(4):
                    c = half * 4 + c4
                    tr(PS[pb][0:nt, c4 * 128:(c4 + 1) * 128], xT[:, c, off + t0:off + t0 + nt], ident, [t_xT[bi], 'ident'],
                       ['ps%d' % pb])
                cp('act' if half else 'dve', ytm[0:nt, sl, half * 512:(half + 1) * 512], PS[pb][0:nt, :], ['ps%d' % pb],
                   ['ytm%d' % sl])
            dst = y_p[off + t0:off + t0 + nt, :] if bi < 4 else y_s
            dma_out(dst, ytm[0:nt, sl, :], ['ytm%d' % sl])
    for i_ in range(int(os.environ.get('KPAD', '0'))):
        pb = nxt('pj', 4)
        tr(PS[pb][:, 0:128], ident, ident, ['ident'], ['ps%d' % pb])
        cp('act', ytm[:, 0, 0:128], PS[pb][:, 0:128], ['ps%d' % pb], ['ytm0'])
    if os.environ.get('KFB', '1') == '1':
        pr.barrier()
    pr.add('sp', None, ['__out%d' % i for i in range(len(out_dmas))], [])
    pr.emit(nc, es)
    return nc, es


_GO = [('norm_mix_pre', 8), ('q_norm', 2), ('kv_norm', 2), ('norm_attn_out', 4), ('norm_ssm_out', 4),
       ('norm_mix_post', 8), ('norm_mem_pre', 8), ('mem_norm', 8), ('norm_mem_post', 8), ('norm_ffn_pre', 8),
       ('norm_ffn_post', 8), ('ssm_d', 4)]


def _consts():
    half = 16
    inv = (10000.0 ** (-np.arange(half, dtype=np.float32) * np.float32(2.0 / 32))).astype(np.float32)
    pos = np.concatenate([np.arange(2048), np.tile(16384 + np.arange(8), 4)]).astype(np.float32)
    ang = (pos[None, :] * inv[:, None]).astype(np.float32)
    rope = np.zeros((32, 2, NTOK), np.float32)
    rope[:, 0, :] = np.concatenate([np.cos(ang), np.cos(ang)], 0)
    rope[:, 1, :] = np.concatenate([np.sin(ang), np.sin(ang)], 0)
    tri = (np.arange(128)[None, :] >= np.arange(128)[:, None]).astype(np.float32)
    mskn = np.zeros((8, 64), np.float32)
    for i in range(8):
        for h in range(8):
            for t in range(8):
                mskn[i, h * 8 + t] = 1.0 if i <= t else 0.0
    smask = np.ones((1, 32), np.float32); smask[0, ::8] = 0.0
    return dict(k_ident=np.eye(128, dtype=np.float32), k_rope=rope, k_tri=tri, k_mskn=mskn,
                k_iota=np.arange(2048, dtype=np.float32)[None, :], k_smask=smask,
                k_pidx32=(np.arange(128) % 32).astype(np.float32)[:, None])


_CACHE = {}


def make_maps(inp, ncores=NCORES):
    f = lambda a: np.ascontiguousarray(np.asarray(a))
    g = [f(inp[k])[0].reshape(c, 128) for k, c in _GO]
    g.append(f(inp['ffn_conv_w'])[0].reshape(66, 128))
    g.append(f(inp['ffn_conv_b'])[0].reshape(22, 128))
    gains = np.concatenate(g, 0).astype(np.float32)
    assert gains.shape == (160, 128)
    shared = dict(
        c_lat=f(inp['cache_kv_latent'])[0], c_pe=f(inp['cache_k_rope'])[0], gains=gains,
        w_in=f(inp['w_in'])[0], w_uq=f(inp['w_uq'])[0].reshape(256, 768), w_uk=f(inp['w_uk'])[0].reshape(256, 512),
        w_uv=f(inp['w_uv'])[0].reshape(256, 512), a_re=f(inp['ssm_a_re'])[0].reshape(16, 128),
        a_im=f(inp['ssm_a_im'])[0].reshape(16, 128), log_dt=f(inp['ssm_log_dt'])[0].reshape(16, 2),
        b_re=f(inp['ssm_b_re'])[0].reshape(2048, 16), b_im=f(inp['ssm_b_im'])[0].reshape(2048, 16),
        cc_re=f(inp['ssm_c_re'])[0], cc_im=f(inp['ssm_c_im'])[0], w_glu=f(inp['ssm_w_glu'])[0],
        w_out=f(inp['w_out'])[0], w_qm=f(inp['w_q_mem'])[0], w_km=f(inp['w_k_mem'])[0], w_vm=f(inp['w_v_mem'])[0],
        w_om=f(inp['w_o_mem'])[0],
        w_gate=f(f(inp['w_gate'])[0].reshape(8, 128, 22, 128).transpose(2, 1, 0, 3)).reshape(22, 128, 1024),
        w_up=f(f(inp['w_up'])[0].reshape(8, 128, 22, 128).transpose(2, 1, 0, 3)).reshape(22, 128, 1024),
        w_down=f(f(inp['w_down'])[0].reshape(22, 128, 8, 128).transpose(2, 1, 0, 3)).reshape(8, 128, 2816))
    shared.update(_consts())
    in_maps = []
    for c in range(ncores):
        sl = slice(4 * c, 4 * c + 4)
        m = dict(shared)
        m.update(x_p=f(inp['x_prompt'])[c], x_s=f(inp['x_sample'])[sl].reshape(32, 1024), mem_p=f(inp['mem_prompt'])[c],
                 ptab=f(inp['page_table'])[sl].reshape(1, 512).astype(np.int32),
                 st_re=f(inp['state_ssm_re'])[0, sl].reshape(64, 128), st_im=f(inp['state_ssm_im'])[0, sl].reshape(64, 128),
                 st_cv=f(inp['state_ffn_conv'])[0, sl].reshape(8, 2816),
                 memk=f(inp['cache_mem_k'])[0, sl].reshape(4, 256, 512), memv=f(inp['cache_mem_v'])[0, sl].reshape(4, 256, 512))
        in_maps.append({k: np.ascontiguousarray(v) for k, v in m.items()})
    return in_maps


def kernel(**inp):
    if 'nc' not in _CACHE:
        _CACHE['nc'] = build()
    nc, es = _CACHE['nc']
    in_maps = make_maps(inp)
    res = run_bass_kernel_spmd(nc, in_maps, core_ids=list(range(NCORES)))
    R = res.results
    cat = lambda k: np.stack([np.asarray(R[c][k]) for c in range(NCORES)], 0)
    y_p = cat('y_p'); y_s = cat('y_s').reshape(32, 8, 1024)
    outs = (y_p, y_s,
            cat('o_kvl_p')[None], cat('o_kr_p')[None],
            cat('o_sre_p').reshape(1, 8, 32, 64), cat('o_sim_p').reshape(1, 8, 32, 64),
            cat('o_cv_p')[None],
            cat('o_mk_p').reshape(1, 8, 256, 4, 128), cat('o_mv_p').reshape(1, 8, 256, 4, 128),
            cat('o_kvl_s').reshape(1, 32, 8, 256), cat('o_kr_s').reshape(1, 32, 8, 32),
            cat('o_sre_s').reshape(1, 32, 32, 64), cat('o_sim_s').reshape(1, 32, 32, 64),
            cat('o_cv_s').reshape(1, 32, 2, 2816))
    return tuple(np.ascontiguousarray(o, dtype=np.float32) for o in outs)
```

```python
import math, os
import os
from contextlib import ExitStack
import numpy as np
import concourse.bass as bass
import concourse.mybir as mybir
from concourse.bass_utils import run_bass_kernel_spmd

F32 = mybir.dt.float32
BF16 = mybir.dt.bfloat16
I32 = mybir.dt.int32
AF = mybir.ActivationFunctionType
ALU = mybir.AluOpType

NCORES = 8
NP_, NSMP, NTOK = 2048, 32, 2080
BLOCKS = [(0, 512), (512, 512), (1024, 512), (1536, 512), (2048, 32)]
ATTN_SCALE = 96.0 ** -0.5
MEM_SCALE = 128.0 ** -0.5
EPS = 1e-6
DSIZE = {F32: 4, BF16: 2, I32: 4}
ENGS = ['pe', 'act', 'dve', 'pool', 'sp']
NSQ = {'sp': int(os.environ.get('KNSP', '24')), 'pool': int(os.environ.get('KNPL', '24'))}


class Prog:
    def __init__(self):
        self.ops = []
        self.per = {e: [] for e in ENGS}
        self.lastw = {}
        self.readers = {}
        self.dma_since = []

    def add(self, eng, fn, r=(), w=(), dma=False):
        oid = len(self.ops)
        deps = set()
        for t in list(r) + list(w):
            if t in self.lastw:
                deps.add(self.lastw[t])
        for t in w:
            deps.update(self.readers.get(t, {}).values())
            deps.update(self.readers.get((t, 'dma'), []))
        op = dict(id=oid, eng=eng, fn=fn, deps=deps, dma=dma)
        self.ops.append(op)
        self.per[eng].append(op)
        if dma:
            self.dma_since.append(oid)
        for t in w:
            self.lastw[t] = oid
            self.readers[t] = {}
            self.readers[(t, 'dma')] = []
        for t in r:
            if dma:
                self.readers.setdefault((t, 'dma'), []).append(oid)
            else:
                self.readers.setdefault(t, {})[eng] = oid
        return oid

    def barrier(self):
        last = set()
        for e in ENGS:
            comp = [op['id'] for op in self.per[e] if not op['dma'] and op['fn'] is not None]
            if comp:
                last.add(comp[-1])
        last.update(self.dma_since)
        self.dma_since = []
        for e in ENGS:
            oid = len(self.ops)
            op = dict(id=oid, eng=e, fn=None, deps=set(last), dma=False)
            self.ops.append(op)
            self.per[e].append(op)

    def emit(self, nc, es):
        ops = self.ops
        need = set()
        for op in ops:
            for d in op['deps']:
                dop = ops[d]
                if dop['dma']:
                    continue
                if dop['eng'] == 'pe' and op['eng'] == 'pe' and not op['dma']:
                    continue
                need.add(d)
        esem = {e: es.enter_context(nc.semaphore('se_' + e)) for e in ENGS}
        dsem = {q: [es.enter_context(nc.semaphore('sd_%s%d' % (q, i))) for i in range(n)]
                for q, n in NSQ.items()}
        for e in ENGS:
            c = 0
            k = 0
            for op in self.per[e]:
                if op['dma']:
                    op['sem'] = dsem[e][k % NSQ[e]]
                    op['val'] = 16 * (k // NSQ[e] + 1)
                    k += 1
                elif op['id'] in need:
                    c += 1
                    op['sig'] = c
        block = es.enter_context(nc.Block())

        know = {e: {} for e in ENGS}
        opknow = {}
        for op in ops:
            e = op['eng']
            waits = {}
            for d in op['deps']:
                dop = ops[d]
                if dop['dma']:
                    s, v = dop['sem'], dop['val']
                else:
                    if dop['eng'] == 'pe' and e == 'pe' and not op['dma']:
                        continue
                    s, v = esem[dop['eng']], dop['sig']
                if waits.get(s, (None, 0, None))[1] < v:
                    waits[s] = (s, v, d)
            if op['dma'] and op['val'] > 16:
                s, v = op['sem'], op['val'] - 16
                if waits.get(s, (None, 0, None))[1] < v:
                    waits[s] = (s, v, None)
            K = know[e]
            wl = []
            for s, v, d in waits.values():
                if K.get(s, 0) >= v:
                    continue
                wl.append((s, v))
                pk = opknow.get(d) if d is not None else None
                if pk:
                    for ks, kv in pk.items():
                        if K.get(ks, 0) < kv:
                            K[ks] = kv
                if K.get(s, 0) < v:
                    K[s] = v
            op['wl'] = wl
            if op['dma']:
                snap = dict(K)
                snap[op['sem']] = max(snap.get(op['sem'], 0), op['val'])
                opknow[op['id']] = snap
            elif op['id'] in need:
                snap = dict(K)
                snap[esem[e]] = max(snap.get(esem[e], 0), op['sig'])
                opknow[op['id']] = snap

        def run(e, eng):
            last_tsz = [None]
            for op in self.per[e]:
                wl = list(op['wl'])
                attach = None
                if wl and e in os.environ.get('KATTENG', 'act,dve,pe,pool').split(',') and not op['dma'] and op['fn'] is not None \
                        and os.environ.get('KATTACH', '1') == '1':
                    attach = wl.pop()
                for s, v in wl:
                    eng.wait_ge(s, v)
                if op['fn'] is None:
                    continue
                if e == 'pe' and 'tsz' in op and os.environ.get('KDRAIN', '0') == '1':
                    if last_tsz[0] is not None and last_tsz[0] != op['tsz']:
                        eng.drain()
                    last_tsz[0] = op['tsz']
                ins = op['fn'](eng)
                if attach is not None:
                    ins._wait_ge(attach[0], attach[1])
                if op['dma']:
                    ins.then_inc(op['sem'], 16)
                elif op['id'] in need:
                    ins.then_inc(esem[e], 1)

        @block.tensor
        def _(eng):
            run('pe', eng)

        @block.scalar
        def _(eng):
            run('act', eng)

        @block.vector
        def _(eng):
            run('dve', eng)

        @block.gpsimd
        def _(eng):
            run('pool', eng)

        @block.sync
        def _(eng):
            run('sp', eng)


class Arena:
    def __init__(self, ap, nwords):
        self.ap, self.n, self.off = ap, nwords, 0
        self.hi = 0

    def alloc(self, free, dtype=F32):
        n = 1
        for s in free:
            n *= s
        words = (n * DSIZE[dtype] + 3) // 4
        words = (words + 15) // 16 * 16
        a = self.off
        self.off += words
        self.hi = max(self.hi, self.off)
        assert self.off <= self.n, ('arena overflow', self.off, self.n)
        v = self.ap[:, a:a + words]
        if dtype != F32:
            v = v.bitcast(dtype)
        v = v[:, 0:n]
        if len(free) == 2:
            v = v.rearrange('p (a b) -> p a b', a=free[0])
        elif len(free) == 3:
            v = v.rearrange('p (a b c) -> p a b c', a=free[0], b=free[1])
        elif len(free) == 4:
            v = v.rearrange('p (a b c d) -> p a b c d', a=free[0], b=free[1], c=free[2])
        return v


def build(npool=5120, debug=False):
    nc = bass.Bass('TRN2', target_bir_lowering=False)
    pr = Prog()

    def din(name, shape, dt=F32):
        return nc.dram_tensor(name, list(shape), dt, kind='ExternalInput').ap()

    def dout(name, shape, dt=F32):
        return nc.dram_tensor(name, list(shape), dt, kind='ExternalOutput').ap()

    x_p = din('x_p', [NP_, 1024]); x_s = din('x_s', [NSMP, 1024]); mem_p = din('mem_p', [256, 1024])
    c_lat = din('c_lat', [npool, 128, 256]); c_pe = din('c_pe', [npool, 128, 32])
    ptab = din('ptab', [1, 512], I32)
    st_re = din('st_re', [64, 128]); st_im = din('st_im', [64, 128])
    st_cv = din('st_cv', [8, 2816])
    memk = din('memk', [4, 256, 512]); memv = din('memv', [4, 256, 512])
    gains = din('gains', [160, 128])
    w_in = din('w_in', [1024, 1056]); w_uq = din('w_uq', [256, 768]); w_uk = din('w_uk', [256, 512])
    w_uv = din('w_uv', [256, 512])
    a_re = din('a_re', [16, 128]); a_im = din('a_im', [16, 128]); log_dt = din('log_dt', [16, 2])
    b_re = din('b_re', [2048, 16]); b_im = din('b_im', [2048, 16])
    cc_re = din('cc_re', [32, 16, 64]); cc_im = din('cc_im', [32, 16, 64])
    w_glu = din('w_glu', [512, 512]); w_out = din('w_out', [1024, 1024])
    w_qm = din('w_qm', [1024, 512]); w_km = din('w_km', [1024, 512]); w_vm = din('w_vm', [1024, 512])
    w_om = din('w_om', [512, 1024])
    w_gate = din('w_gate', [22, 128, 1024]); w_up = din('w_up', [22, 128, 1024]); w_down = din('w_down', [8, 128, 2816])
    k_ident = din('k_ident', [128, 128]); k_rope = din('k_rope', [32, 2, NTOK])
    k_tri = din('k_tri', [128, 128]); k_mskn = din('k_mskn', [8, 64]); k_iota = din('k_iota', [1, 2048])
    k_smask = din('k_smask', [1, 32]); k_pidx32 = din('k_pidx32', [128, 1])

    y_p = dout('y_p', [NP_, 1024]); y_s = dout('y_s', [NSMP, 1024])
    o_kvl_p = dout('o_kvl_p', [NP_, 256]); o_kr_p = dout('o_kr_p', [NP_, 32])
    o_sre_p = dout('o_sre_p', [16, 128]); o_sim_p = dout('o_sim_p', [16, 128])
    o_cv_p = dout('o_cv_p', [2, 2816])
    o_mk_p = dout('o_mk_p', [256, 512]); o_mv_p = dout('o_mv_p', [256, 512])
    o_kvl_s = dout('o_kvl_s', [NSMP, 256]); o_kr_s = dout('o_kr_s', [NSMP, 32])
    o_sre_s = dout('o_sre_s', [64, 128]); o_sim_s = dout('o_sim_s', [64, 128])
    o_cv_s = dout('o_cv_s', [8, 2816])

    es = ExitStack()
    NW = 53000
    arena_t = es.enter_context(nc.sbuf_tensor('arena', [128, NW], F32))
    AR = Arena(arena_t[:], NW)
    PS = [es.enter_context(nc.psum_tensor('ps%d' % i, [128, 512], F32)) for i in range(8)]
    rg = es.enter_context(nc.sync.register('rg'))
    out_dmas = []

    def _rnd(x):
        return 32 if x <= 32 else (64 if x <= 64 else 128)

    def _tsz(ap):
        sh = ap.shape
        fsz = 1
        for d_ in sh[1:]:
            fsz *= d_
        return (_rnd(sh[0]), _rnd(fsz))

    def mm(out, lhsT, rhs, start, stop, r, w):
        oid = pr.add('pe', lambda e: e.matmul(out, lhsT=lhsT, rhs=rhs, start=start, stop=stop), r, w)
        pr.ops[oid]['tsz'] = _tsz(lhsT)

    def tr(out, in_, ident, r, w):
        oid = pr.add('pe', lambda e: e.transpose(out, in_, ident), r, w)
        pr.ops[oid]['tsz'] = _tsz(in_)

    def act(out, in_, func, r, w, scale=1.0, bias=None):
        if bias is None:
            pr.add('act', lambda e: e.activation(out=out, in_=in_, func=func, scale=scale), r, w)
        else:
            pr.add('act', lambda e: e.activation(out=out, in_=in_, func=func, scale=scale, bias=bias), r, w)

    def cp(eng, out, in_, r, w):
        if eng == 'pool' and os.environ.get('KNOPOOL', '0') == '1':
            eng = 'dve'
        if eng == 'act':
            pr.add('act', lambda e: e.activation(out=out, in_=in_, func=AF.Copy), r, w)
        else:
            pr.add(eng, lambda e: e.tensor_copy(out=out, in_=in_), r, w)

    def tt(out, in0, in1, op, r, w, eng='dve'):
        pr.add(eng, lambda e: e.tensor_tensor(out=out, in0=in0, in1=in1, op=op), r, w)

    def ts(out, in0, s1, s2, op0, op1, r, w, eng='dve'):
        if s2 is None:
            pr.add(eng, lambda e: e.tensor_scalar(out=out, in0=in0, scalar1=s1, scalar2=None, op0=op0), r, w)
        else:
            pr.add(eng, lambda e: e.tensor_scalar(out=out, in0=in0, scalar1=s1, scalar2=s2, op0=op0, op1=op1), r, w)

    def stt(out, in0, scalar, in1, op0, op1, r, w):
        pr.add('dve', lambda e: e.scalar_tensor_tensor(out=out, in0=in0, scalar=scalar, in1=in1, op0=op0, op1=op1), r, w)

    def scan(out, d0, d1, init, r, w):
        pr.add('dve', lambda e: e.tensor_tensor_scan(out=out, data0=d0, data1=d1, initial=init,
                                                     op0=ALU.mult, op1=ALU.add), r, w)

    def recip(out, in_, r, w):
        pr.add('dve', lambda e: e.reciprocal(out=out, in_=in_), r, w)

    def memset(eng, ap, val, w):
        if eng == 'pool' and os.environ.get('KNOPOOL', '0') == '1':
            eng = 'dve'
        pr.add(eng, lambda e: e.memset(ap, val), (), w)

    def dma(out, in_, r, w, q='sp', slow=False):
        if slow:
            return pr.add(q, lambda e: e.dma_start(out=out, in_=in_, allow_slow_non_contiguous=True), r, w, dma=True)
        return pr.add(q, lambda e: e.dma_start(out=out, in_=in_), r, w, dma=True)

    def dma_out(out, in_, r):
        out_dmas.append(dma(out, in_, r, ['__out%d' % len(out_dmas)]))

    rot = {}

    def nxt(key, n):
        rot[key] = (rot.get(key, -1) + 1) % n
        return rot[key]

    xT = AR.alloc([8, NTOK]); t_xT = ['xT%d' % b for b in range(5)]
    ident = AR.alloc([128]); identb = AR.alloc([128], BF16)
    ones_b = AR.alloc([128], BF16); ones_f = AR.alloc([128])
    G = AR.alloc([160])
    epsT = AR.alloc([1]); hpiT = AR.alloc([1])
    tri = AR.alloc([128], BF16); mskn = AR.alloc([64], BF16)
    GO = dict(mix_pre=0, q=8, kv=10, attn=12, ssm=16, mix_post=20, mem_pre=28, mem=36, mem_post=44,
              ffn_pre=52, ffn_post=60, d=68, cw0=72, cw1=94, cw2=116, cb=138)

    dma(ident, k_ident, [], ['ident'])
    cp('act', identb, ident, ['ident'], ['identb'])
    memset('pool', ones_b, 1.0, ['ones_b']); memset('pool', ones_f, 1.0, ['ones_f'])
    memset('pool', epsT, EPS, ['epsT']); memset('pool', hpiT, math.pi / 2, ['hpiT'])
    KOLD = os.environ.get('KOLD', '0')
    if '1' in KOLD:
        trf = AR.alloc([128]); mnf = AR.alloc([64]); graw = AR.alloc([2, 128])
    if '2' not in KOLD:
        KmT = AR.alloc([4, 256], BF16); Vm = AR.alloc([2, 512], BF16)
    mark_persist = AR.off

    def cast_load(dst, src, tok, q='pool'):
        dma(dst, src, [], [tok], q=q)

    def norm_fm(src, C, n, gcol, D, dst, r, w, sq, ssb, rstd, psb, resid=None, tmp=None, sfx=''):
        tq_, ts_, tr_, tt_ = 'nsq' + sfx, 'nss' + sfx, 'nrstd' + sfx, 'ntmp' + sfx
        act(sq[:, 0:C, 0:n], src, AF.Square, r, [tq_])
        for c in range(C):
            mm(PS[psb][:, 0:n], ones_b, sq[:, c, 0:n], c == 0, c == C - 1, [tq_, 'ones_b'], ['ps%d' % psb])
        act(ssb[:, 0:n], PS[psb][:, 0:n], AF.Ln, ['ps%d' % psb, 'epsT'], [ts_], scale=1.0 / D, bias=epsT)
        act(rstd[:, 0:n], ssb[:, 0:n], AF.Exp, [ts_], [tr_], scale=-0.5)
        for c in range(C):
            if resid is None:
                stt(dst[:, c, :], src[:, c, :], G[:, gcol + c:gcol + c + 1], rstd[:, 0:n], ALU.mult, ALU.mult,
                    list(r) + [tr_, 'G'], w)
            else:
                stt(tmp[:, 0:n], src[:, c, :], G[:, gcol + c:gcol + c + 1], rstd[:, 0:n], ALU.mult, ALU.mult,
                    list(r) + [tr_, 'G'], [tt_])
                tt(resid[:, c, :], resid[:, c, :], tmp[:, 0:n], ALU.add, [tt_] + list(w), w)

    n_sq = AR.alloc([8, 512], BF16); n_ss = AR.alloc([512]); n_rstd = AR.alloc([512]); n_tmp = AR.alloc([512])

    def NORM(src, C, n, gname, D, dst, r, w, psb=7, resid=None, ns=None):
        if ns is None:
            norm_fm(src, C, n, GO[gname], D, dst, r, w, n_sq, n_ss, n_rstd, psb, resid=resid, tmp=n_tmp)
        else:
            norm_fm(src, C, n, GO[gname], D, dst, r, w, ns[0], ns[1], ns[2], 6, resid=resid, tmp=ns[3], sfx='B')

    def norm_set():
        return (AR.alloc([8, 512], BF16), AR.alloc([512]), AR.alloc([512]), AR.alloc([512]))

    m0 = AR.off
    if '1' not in KOLD:
        trf = AR.alloc([128]); mnf = AR.alloc([64])
    dma(trf, k_tri, [], ['trf']); dma(mnf[0:8, :], k_mskn, [], ['mnf'])
    cp('dve', tri, trf, ['trf'], ['tri']); cp('dve', mskn[0:8, :], mnf[0:8, :], ['mnf'], ['mskn'])
    if '1' not in KOLD:
        graw = AR.alloc([2, 128])
    dma(graw[:, 0, :], gains[0:128, :], [], ['graw0']); dma(graw[0:32, 1, :], gains[128:160, :], [], ['graw1'])
    tr(PS[0][:, 0:128], graw[:, 0, :], ident, ['graw0', 'ident'], ['ps0'])
    tr(PS[0][:, 128:160], graw[0:32, 1, :], ident[0:32, 0:32], ['graw1', 'ident'], ['ps0'])
    cp('dve', G, PS[0][:, 0:160], ['ps0'], ['G'])
    xtm = AR.alloc([2, 1024])
    for t in range(17):
        sl = t % 2
        nt = 128 if t < 16 else 32
        src = x_p[t * 128:(t + 1) * 128, :] if t < 16 else x_s[:, :]
        dma(xtm[0:nt, sl, :], src, [], ['xtm%d' % sl])
        b = min(t // 4, 4)
        col0 = t * 128
        for half in range(2):
            pb = nxt('x0', 2)
            for c4 in range(4):
                c = half * 4 + c4
                tr(PS[pb][:, c4 * 128:c4 * 128 + nt], xtm[0:nt, sl, c * 128:(c + 1) * 128], ident[0:nt, 0:nt],
                   ['xtm%d' % sl, 'ident'], ['ps%d' % pb])
            cp('act' if half == 0 else 'dve', xT[:, half * 4:half * 4 + 4, col0:col0 + nt],
               PS[pb][:].rearrange('p (a b) -> p a b', a=4)[:, :, 0:nt], ['ps%d' % pb], [t_xT[b]])
    pr.barrier()
    AR.off = m0

    def out_rows(src_fm, C, n, dst_rows, width, r, base=0):
        for t0 in range(0, n, 128):
            nt = min(128, n - t0)
            pb = nxt('orow', 2)
            sl = nxt('otm', 2)
            for c in range(C):
                if base == 0:
                    tr(PS[pb][0:nt, c * 128:(c + 1) * 128], src_fm[:, c, t0:t0 + nt], ident, list(r) + ['ident'], ['ps%d' % pb])
                else:
                    tr(PS[pb][0:nt, 0:32], src_fm[base:base + 32, c, t0:t0 + nt], ident[base:base + 32, base:base + 32],
                       list(r) + ['ident'], ['ps%d' % pb])
            cp('act', otm[0:nt, sl, 0:width], PS[pb][0:nt, 0:width], ['ps%d' % pb], ['otm%d' % sl])
            dma_out(dst_rows[t0:t0 + nt, :], otm[0:nt, sl, 0:width], ['otm%d' % sl])

    otm = AR.alloc([2, 512])

    mUT = AR.off
    uT = AR.alloc([4, NTOK], BF16); attn_o = AR.alloc([4, NTOK], BF16)
    mATT = AR.off
    ropeT = AR.alloc([2, NTOK])
    dma(ropeT[64:96, :, :], k_rope, [], ['ropeT'])
    cqn = AR.alloc([2, NTOK], BF16); ckvn = AR.alloc([2, NTOK], BF16)
    KpeT = AR.alloc([NTOK], BF16)
    mA = AR.off
    Win = AR.alloc([8, 1056], BF16); WinSw = AR.alloc([8, 32], BF16)
    w_in_v = w_in.rearrange('(c p) n -> p c n', p=128)
    for c in range(8):
        cast_load(Win[:, c, :], w_in_v[:, c, :], 'Win')
    cast_load(WinSw[:, :, 0:16], w_in_v[:, :, 528:544], 'WinSw')
    cast_load(WinSw[:, :, 16:32], w_in_v[:, :, 512:528], 'WinSw')
    ts(WinSw[:, :, 0:16], WinSw[:, :, 0:16], -1.0, None, ALU.mult, None, ['WinSw'], ['WinSw'])
    hT = AR.alloc([8, 512], BF16)
    zq = AR.alloc([2, 512]); zkv = zq; zkvn = AR.alloc([2, 512])
    krf = AR.alloc([1, 512]); rt1 = AR.alloc([512]); rt2 = AR.alloc([512])
    for bi, (off, n) in enumerate(BLOCKS):
        NORM(xT[:, :, off:off + n], 8, n, 'mix_pre', 1024, hT[:, :, 0:n], [t_xT[bi]], ['hT'])

        def proj(col0, M, outp, pb, wt=Win, wtok='Win'):
            for k in range(8):
                mm(outp, wt[:, k, col0:col0 + M], hT[:, k, 0:n], k == 0, k == 7, [wtok, 'hT'], ['ps%d' % pb])
        for c in range(2):
            pb = nxt('pj', 4)
            proj(c * 128, 128, PS[pb][:, 0:n], pb)
            cp('act', zq[:, c, 0:n], PS[pb][:, 0:n], ['ps%d' % pb], ['zq'])
        NORM(zq[:, :, 0:n], 2, n, 'q', 256, cqn[:, :, off:off + n], ['zq'], ['cqn'])
        for c in range(2):
            pb = nxt('pj', 4)
            proj(256 + c * 128, 128, PS[pb][:, 0:n], pb)
            cp('act', zkv[:, c, 0:n], PS[pb][:, 0:n], ['ps%d' % pb], ['zq'])
        NORM(zkv[:, :, 0:n], 2, n, 'kv', 256, zkvn[:, :, 0:n], ['zq'], ['zkvn'])
        cp('act', ckvn[:, :, off:off + n], zkvn[:, :, 0:n], ['zkvn'], ['ckvn'])
        out_rows(zkvn, 2, n, (o_kvl_p[off:off + n, :] if bi < 4 else o_kvl_s), 256, ['zkvn'])
        pb = nxt('pj', 4); pb2 = nxt('pj', 4)
        proj(512, 32, PS[pb][64:96, 0:n], pb)
        proj(0, 32, PS[pb2][64:96, 0:n], pb2, wt=WinSw, wtok='WinSw')
        tt(rt1[64:96, 0:n], PS[pb][64:96, 0:n], ropeT[64:96, 0, off:off + n], ALU.mult, ['ps%d' % pb, 'ropeT'], ['rt1'])
        tt(rt2[64:96, 0:n], PS[pb2][64:96, 0:n], ropeT[64:96, 1, off:off + n], ALU.mult, ['ps%d' % pb2, 'ropeT'], ['rt2'])
        tt(krf[64:96, 0, 0:n], rt1[64:96, 0:n], rt2[64:96, 0:n], ALU.add, ['rt1', 'rt2'], ['krf'])
        cp('act', KpeT[64:96, off:off + n], krf[64:96, 0, 0:n], ['krf'], ['KpeT'])
        out_rows(krf, 1, n, (o_kr_p[off:off + n, :] if bi < 4 else o_kr_s), 32, ['krf'], base=64)
        for c in range(4):
            pb = nxt('pj', 4)
            proj(544 + c * 128, 128, PS[pb][:, 0:n], pb)
            cp('act' if c % 2 else 'dve', uT[:, c, off:off + n], PS[pb][:, 0:n], ['ps%d' % pb], ['uT'])
    pr.barrier()
    AR.off = mA

    Wuv = AR.alloc([2, 512], BF16); QA = AR.alloc([4, 2, 64], BF16); QAp = AR.alloc([4, 64], BF16)
    mSA = AR.off
    Wuq = AR.alloc([2, 768], BF16); WuqSw = AR.alloc([2, 8, 32], BF16)
    Wuk = AR.alloc([2, 512], BF16); WukT = AR.alloc([8, 256], BF16)
    w_uq_v = w_uq.rearrange('(c p) n -> p c n', p=128)
    cast_load(Wuq, w_uq_v, 'Wuq')
    w_uq_4 = w_uq.rearrange('(c p) (h e) -> p c h e', p=128, e=96)
    for c in range(2):
        cast_load(WuqSw[:, c, :, 0:16], w_uq_4[:, c, :, 80:96], 'WuqSw')
        cast_load(WuqSw[:, c, :, 16:32], w_uq_4[:, c, :, 64:80], 'WuqSw')
    ts(WuqSw[:, :, :, 0:16], WuqSw[:, :, :, 0:16], -1.0, None, ALU.mult, None, ['WuqSw'], ['WuqSw'])
    cast_load(Wuk, w_uk.rearrange('(c p) n -> p c n', p=128), 'Wuk')
    cast_load(Wuv, w_uv.rearrange('(c p) n -> p c n', p=128), 'Wuv')
    for h in range(8):
        pb = nxt('pj', 4)
        psb = PS[pb][:].bitcast(BF16)
        for c in range(2):
            tr(psb[0:64, c * 128:(c + 1) * 128], Wuk[:, c, h * 64:(h + 1) * 64], identb, ['Wuk', 'identb'], ['ps%d' % pb])
        cp('dve', WukT[0:64, h, :], psb[0:64, 0:256], ['ps%d' % pb], ['WukT'])
    QT = [AR.alloc([NTOK], BF16) for _ in range(2)]
    KT = [AR.alloc([NTOK], BF16) for _ in range(2)]
    Vh = [AR.alloc([17, 65], BF16) for _ in range(2)]
    PT = [AR.alloc([512], BF16) for _ in range(3)]
    qr1 = AR.alloc([512]); qr2 = AR.alloc([512])
    osb = AR.alloc([512]); rsb = AR.alloc([512]); otmpb = AR.alloc([512], BF16)
    for i in range(2):
        memset('pool', Vh[i][:, :, 64:65], 1.0, ['Vh%d' % i])
    for h in range(8):
        hs = h % 2
        tq, tk, tv = 'QT%d' % hs, 'KT%d' % hs, 'Vh%d' % hs
        for bi, (off, n) in enumerate(BLOCKS):
            pb = nxt('pj', 4); pb2 = nxt('pj', 4); pb3 = nxt('pj', 4)
            for c in range(2):
                mm(PS[pb][0:96, 0:n], Wuq[:, c, h * 96:(h + 1) * 96], cqn[:, c, off:off + n], c == 0, c == 1,
                   ['Wuq', 'cqn'], ['ps%d' % pb])
            for c in range(2):
                mm(PS[pb2][64:96, 0:n], WuqSw[:, c, h, :], cqn[:, c, off:off + n], c == 0, c == 1,
                   ['WuqSw', 'cqn'], ['ps%d' % pb2])
            cp('act', QT[hs][0:64, off:off + n], PS[pb][0:64, 0:n], ['ps%d' % pb], [tq])
            tt(qr1[64:96, 0:n], PS[pb][64:96, 0:n], ropeT[64:96, 0, off:off + n], ALU.mult, ['ps%d' % pb, 'ropeT'], ['qr1'])
            tt(qr2[64:96, 0:n], PS[pb2][64:96, 0:n], ropeT[64:96, 1, off:off + n], ALU.mult, ['ps%d' % pb2, 'ropeT'], ['qr2'])
            tt(QT[hs][64:96, off:off + n], qr1[64:96, 0:n], qr2[64:96, 0:n], ALU.add, ['qr1', 'qr2'], [tq])
            for c in range(2):
                mm(PS[pb3][0:64, 0:n], Wuk[:, c, h * 64:(h + 1) * 64], ckvn[:, c, off:off + n], c == 0, c == 1,
                   ['Wuk', 'ckvn'], ['ps%d' % pb3])
            cp('act', KT[hs][0:64, off:off + n], PS[pb3][0:64, 0:n], ['ps%d' % pb3], [tk])
        cp('pool', KT[hs][64:96, :], KpeT[64:96, :], ['KpeT'], [tk])
        for g4 in range(5):
            pb = nxt('pj', 4)
            ntl = 4 if g4 < 4 else 1
            for j in range(ntl):
                t = g4 * 4 + j
                nt = 128 if t < 16 else 32
                for c in range(2):
                    mm(PS[pb][0:nt, j * 64:(j + 1) * 64], ckvn[:, c, t * 128:t * 128 + nt], Wuv[:, c, h * 64:(h + 1) * 64],
                       c == 0, c == 1, ['ckvn', 'Wuv'], ['ps%d' % pb])
            npart = 128 if g4 < 4 else 32
            cp('dve', Vh[hs][0:npart, g4 * 4:g4 * 4 + ntl, 0:64],
               PS[pb][0:npart, 0:ntl * 64].rearrange('p (a b) -> p a b', a=ntl), ['ps%d' % pb], [tv])
        for c in range(2):
            pb = nxt('pj', 4)
            mm(PS[pb][:, 0:32], WukT[0:64, h, c * 128:(c + 1) * 128], QT[hs][0:64, 2048:2080], True, True,
               ['WukT', tq], ['ps%d' % pb])
            cp('dve', QA[:, :, c, h * 8:(h + 1) * 8], PS[pb][:, 0:32].rearrange('p (s t) -> p s t', s=4),
               ['ps%d' % pb], ['QA'])
        cp('dve', QAp[64:96, :, h * 8:(h + 1) * 8], QT[hs][64:96, 2048:2080].rearrange('p (s t) -> p s t', s=4), [tq], ['QA'])
        for qt in range(4):
            ob = 4 + nxt('ob', 2)
            nk = 4 * qt + 4
            pend = []
            for kt in range(nk):
                d = kt - 4 * qt
                c0 = 128 * d if d > 0 else 0
                sb = 6 + nxt('sb', 2)
                pi = nxt('PT', 3)
                qs = qt * 512
                mm(PS[sb][:, c0:512], KT[hs][0:96, kt * 128:(kt + 1) * 128], QT[hs][0:96, qs + c0:qs + 512], True, True,
                   [tk, tq], ['ps%d' % sb])
                act(PT[pi][:, c0:512], PS[sb][:, c0:512], AF.Exp, ['ps%d' % sb], ['PT%d' % pi], scale=ATTN_SCALE)
                if d >= 0:
                    tt(PT[pi][:, c0:c0 + 128], PT[pi][:, c0:c0 + 128], tri, ALU.mult, ['PT%d' % pi, 'tri'], ['PT%d' % pi])

                def pv(kt=kt, c0=c0, pi=pi):
                    mm(PS[ob][0:65, c0:512], Vh[hs][:, kt, :], PT[pi][:, c0:512], kt == 0, kt == nk - 1,
                       [tv, 'PT%d' % pi], ['ps%d' % ob])
                pend.append(pv)
                if len(pend) > 1:
                    pend.pop(0)()
            while pend:
                pend.pop(0)()
            recip(rsb[64:65, :], PS[ob][64:65, :], ['ps%d' % ob], ['rsb'])
            cp('act', osb[0:64, :], PS[ob][0:64, :], ['ps%d' % ob], ['osb'])
            pb = nxt('pj', 4)
            mm(PS[pb][0:64, :], ones_f[64:65, 0:64], rsb[64:65, :], True, True, ['ones_f', 'rsb'], ['ps%d' % pb])
            if h % 2 == 0:
                tt(attn_o[0:64, h // 2, qs:qs + 512], osb[0:64, :], PS[pb][0:64, :], ALU.mult, ['osb', 'ps%d' % pb], ['attn_o'])
            else:
                tt(otmpb[0:64, :], osb[0:64, :], PS[pb][0:64, :], ALU.mult, ['osb', 'ps%d' % pb], ['otmpb'])
                cp('dve', attn_o[64:128, h // 2, qs:qs + 512], otmpb[0:64, :], ['otmpb'], ['attn_o'])
    pr.barrier()
    AR.off = mSA
    NSL = 4
    ptS = AR.alloc([512], I32); ptF = AR.alloc([512]); pidx = AR.alloc([1]); idxG = AR.alloc([128], I32); idxF = AR.alloc([128])
    dma(ptS, ptab.partition_broadcast(128), [], ['ptS'])
    dma(pidx, k_pidx32, [], ['pidx'])
    cp('dve', ptF, ptS, ['ptS'], ['ptF'])
    for a in range(4):
        rows = slice(32 * a, 32 * a + 32)
        ts(idxF[rows, :], ptF[rows, :].rearrange('p (j a) -> p j a', a=4)[:, :, a], 32.0, pidx[rows, 0:1], ALU.mult, ALU.add,
           ['ptF', 'pidx'], ['idxF'])
    cp('dve', idxG, idxF, ['idxF'], ['idxG'])
    Kl = [AR.alloc([4, 256], BF16) for _ in range(NSL)]; Kp = [AR.alloc([4, 32], BF16) for _ in range(NSL)]
    KTl = [AR.alloc([4, 2, 128], BF16) for _ in range(NSL)]
    KTp = [AR.alloc([4, 128], BF16) for _ in range(NSL)]
    PTs = [AR.alloc([4, 64], BF16) for _ in range(NSL)]
    KnN = AR.alloc([257], BF16); PTn = AR.alloc([64], BF16)
    onb = AR.alloc([256], BF16); olT = AR.alloc([2, 64], BF16); rs1 = AR.alloc([1])
    memset('pool', KnN[0:8, 256:257], 1.0, ['KnN'])
    c_lat_r = c_lat.rearrange('a (r t) d -> (a r) (t d)', t=4); c_pe_r = c_pe.rearrange('a (r t) d -> (a r) (t d)', t=4)

    def page_dma(dst, src_rows, col, w):
        pr.add('pool', lambda e: e.indirect_dma_start(out=dst, out_offset=None, in_=src_rows,
                                                      in_offset=bass.IndirectOffsetOnAxis(ap=idxG[:, col:col + 1], axis=0)),
               ['idxG'], w, dma=True)

    memset('pool', attn_o[:, :, 2048:2080], 0.0, ['attn_o'])
    NSEQ = int(os.environ.get('KSEQ', '4'))

    def st0(s_, g, sl):
        col = s_ * 32 + g
        page_dma(Kl[sl].rearrange('p t d -> p (t d)'), c_lat_r, col, ['Kb%d' % sl])
        page_dma(Kp[sl].rearrange('p t d -> p (t d)'), c_pe_r, col, ['Kb%d' % sl])
        pa = nxt('pj', 4); pbb = nxt('pj', 4)
        psa = PS[pa][:].bitcast(BF16); psp = PS[pbb][:].bitcast(BF16)
        for p in range(4):
            for c in range(2):
                tr(psa[:, (p * 2 + c) * 128:(p * 2 + c + 1) * 128], Kl[sl][:, p, c * 128:(c + 1) * 128], identb,
                   ['Kb%d' % sl, 'identb'], ['ps%d' % pa])
            tr(psp[64:96, p * 128:(p + 1) * 128], Kp[sl][:, p, :], identb, ['Kb%d' % sl, 'identb'], ['ps%d' % pbb])
        cp('act', KTl[sl], psa[:, 0:1024].rearrange('p (a c k) -> p a c k', a=4, c=2), ['ps%d' % pa], ['KTl%d' % sl])
        cp('dve', KTp[sl][64:96, :, :], psp[64:96, 0:512].rearrange('p (a k) -> p a k', a=4), ['ps%d' % pbb], ['KTp%d' % sl])

    def st1(s_, g, sl):
        sb = 6 + nxt('sb', 2)
        for p in range(4):
            o = PS[sb][:, p * 64:(p + 1) * 64]
            mm(o, KTl[sl][:, p, 0, :], QA[:, s_, 0, :], True, False, ['KTl%d' % sl, 'QA'], ['ps%d' % sb])
            mm(o, KTl[sl][:, p, 1, :], QA[:, s_, 1, :], False, False, ['KTl%d' % sl, 'QA'], ['ps%d' % sb])
            mm(o, KTp[sl][64:96, p, :], QAp[64:96, s_, :], False, True, ['KTp%d' % sl, 'QA'], ['ps%d' % sb])
        act(PTs[sl], PS[sb][:, 0:256].rearrange('p (a q) -> p a q', a=4), AF.Exp, ['ps%d' % sb], ['PTs%d' % sl],
            scale=ATTN_SCALE)

    def st2(s_, g, sl):
        for p in range(4):
            mm(PS[OBS[s_ % 2]][0:64, 0:256], PTs[sl][:, p, :], Kl[sl][:, p, :], g == 0 and p == 0, False,
               ['PTs%d' % sl, 'Kb%d' % sl], ['ps%d' % OBS[s_ % 2]])
            mm(PS[OBS[s_ % 2]][0:64, 256:257], PTs[sl][:, p, :], ones_b[:, 0:1], False, False,
               ['PTs%d' % sl, 'ones_b'], ['ps%d' % OBS[s_ % 2]])
        if g == 31:
            fin(s_)

    OBS = [4, 5]
    groups = [(s_, g, i % NSL) for i, (s_, g) in enumerate((s_, g) for s_ in range(NSEQ) for g in range(32))]

    def fin(s_):
        OB = OBS[s_ % 2]
        c0 = 2048 + 8 * s_
        pb = nxt('pj', 4)
        psb = PS[pb][:].bitcast(BF16)
        for c in range(2):
            tr(psb[0:8, c * 128:(c + 1) * 128], ckvn[:, c, c0:c0 + 8], identb, ['ckvn', 'identb'], ['ps%d' % pb])
        cp('dve', KnN[0:8, 0:256], psb[0:8, 0:256], ['ps%d' % pb], ['KnN'])
        sb = 6 + nxt('sb', 2)
        mm(PS[sb][0:8, 0:64], ckvn[:, 0, c0:c0 + 8], QA[:, s_, 0, :], True, False, ['ckvn', 'QA'], ['ps%d' % sb])
        mm(PS[sb][0:8, 0:64], ckvn[:, 1, c0:c0 + 8], QA[:, s_, 1, :], False, False, ['ckvn', 'QA'], ['ps%d' % sb])
        mm(PS[sb][0:8, 0:64], KpeT[64:96, c0:c0 + 8], QAp[64:96, s_, :], False, True, ['KpeT', 'QA'], ['ps%d' % sb])
        act(PTn[0:8, :], PS[sb][0:8, 0:64], AF.Exp, ['ps%d' % sb], ['PTn'], scale=ATTN_SCALE)
        tt(PTn[0:8, :], PTn[0:8, :], mskn[0:8, :], ALU.mult, ['PTn', 'mskn'], ['PTn'])
        mm(PS[OB][0:64, 0:257], PTn[0:8, :], KnN[0:8, 0:257], False, True, ['PTn', 'KnN'], ['ps%d' % OB])
        recip(rs1[0:64, :], PS[OB][0:64, 256:257], ['ps%d' % OB], ['rs1'])
        ts(onb[0:64, :], PS[OB][0:64, 0:256], rs1[0:64, 0:1], None, ALU.mult, None, ['ps%d' % OB, 'rs1'], ['onb'])
        pb = nxt('pj', 4)
        psb = PS[pb][:].bitcast(BF16)
        for c in range(2):
            tr(psb[:, c * 64:(c + 1) * 64], onb[0:64, c * 128:(c + 1) * 128], identb[0:64, 0:64], ['onb', 'identb'], ['ps%d' % pb])
        cp('dve', olT, psb[:, 0:128].rearrange('p (c q) -> p c q', c=2), ['ps%d' % pb], ['olT'])
        pb = nxt('pj', 4)
        for h in range(8):
            for c in range(2):
                mm(PS[pb][(h % 2) * 64:(h % 2) * 64 + 64, (h // 2) * 8:(h // 2) * 8 + 8], Wuv[:, c, h * 64:(h + 1) * 64],
                   olT[:, c, h * 8:(h + 1) * 8], c == 0, c == 1, ['Wuv', 'olT'], ['ps%d' % pb])
        cp('dve', attn_o[:, :, c0:c0 + 8], PS[pb][:, 0:32].rearrange('p (a t) -> p a t', a=4), ['ps%d' % pb], ['attn_o'])

    for i in range(len(groups) + 2):
        if i < len(groups):
            st0(*groups[i])
        if 0 <= i - 1 < len(groups):
            st1(*groups[i - 1])
        if 0 <= i - 2 < len(groups):
            st2(*groups[i - 2])
    if debug:
        dbg_attn = dout('dbg_attn', [128, 4, NTOK], BF16)
        dma_out(dbg_attn, attn_o, ['attn_o'])
    pr.barrier()
    AR.off = mATT

    class StopS5(Exception):
        pass

    try:
        LVL = int(os.environ.get('KLVL', '99'))
        ssm_o = AR.alloc([4, NTOK], BF16)
        mSSM = AR.off
        BreT = AR.alloc([16, 128], BF16); BimT = AR.alloc([16, 128], BF16)
        CreT = AR.alloc([16, 128], BF16); NCreT = AR.alloc([16, 128], BF16); NCimT = AR.alloc([16, 128], BF16)
        prm = AR.alloc([24, 16])
        H0re = AR.alloc([4, 16]); H0im = AR.alloc([4, 16]); HSre = AR.alloc([16]); HSim = AR.alloc([16])
        HSsre = AR.alloc([4, 16]); HSsim = AR.alloc([4, 16]); smT = AR.alloc([32]); rhoS = AR.alloc([32])
        cry = AR.alloc([4]); t4a = AR.alloc([4]); t4b = AR.alloc([4]); ah_re = AR.alloc([4]); ah_im = AR.alloc([4]); c1t = AR.alloc([4])
        PN = ['are', 'aim', 'ldt', 'dt', 'lam', 'th', 'rho', 'sth', 'cth', 'abr', 'abi', 'den', 'rden', 'nr', 'fr', 'fi',
              'u1', 'u2', 'u3', 'u4', 'a512', 's512', 'c512']
        P_ = {nm: prm[:, i, :] for i, nm in enumerate(PN)}
        mS5 = AR.off
        INV2PI = 1.0 / (2 * math.pi); MAGIC = 12582912.0; C1 = 6.28125; C2 = 2 * math.pi - 6.28125

        def sincos(s_out, c_out, ang, ta, tb, r, w_s, w_c, tok):
            ts(ta, ang, INV2PI, None, ALU.mult, None, r, [tok + 'a'])
            ts(ta, ta, MAGIC, None, ALU.add, None, [tok + 'a'], [tok + 'a'])
            ts(ta, ta, -MAGIC, None, ALU.add, None, [tok + 'a'], [tok + 'a'])
            stt(tb, ta, -C1, ang, ALU.mult, ALU.add, list(r) + [tok + 'a'], [tok + 'b'])
            stt(tb, ta, -C2, tb, ALU.mult, ALU.add, [tok + 'a', tok + 'b'], [tok + 'b'])
            ts(tb, tb, math.pi, -math.pi, ALU.min, ALU.max, [tok + 'b'], [tok + 'b'])
            act(s_out, tb, AF.Sin, [tok + 'b'], w_s)
            stt(ta, tb, -1.0, tb, ALU.mult, ALU.max, [tok + 'b'], [tok + 'a'])
            act(c_out, ta, AF.Sin, [tok + 'a', 'hpiT'], w_c, scale=-1.0, bias=hpiT)

        araw = AR.alloc([3, 128]); ldr = AR.alloc([2])
        dma(araw[0:16, 0, :], a_re, [], ['araw']); dma(araw[0:16, 1, :], a_im, [], ['araw'])
        dma(ldr[0:16, :], log_dt, [], ['ldr'])
        cp('dve', araw[0:16, 2, :].rearrange('p (g n) -> p g n', g=2), ldr[0:16, :].unsqueeze(2).to_broadcast([16, 2, 64]),
           ['ldr', 'araw'], ['araw'])
        pb = nxt('pj', 4)
        for i in range(3):
            tr(PS[pb][:, i * 16:(i + 1) * 16], araw[0:16, i, :], ident[0:16, 0:16], ['araw', 'ident'], ['ps%d' % pb])
        cp('dve', prm[:, 0:3, :], PS[pb][:, 0:48].rearrange('p (a b) -> p a b', a=3), ['ps%d' % pb], ['prm'])
        TP = ['prm']
        act(P_['dt'], P_['ldt'], AF.Exp, TP, TP)
        tt(P_['lam'], P_['dt'], P_['are'], ALU.mult, TP, TP)
        tt(P_['th'], P_['dt'], P_['aim'], ALU.mult, TP, TP)
        act(P_['rho'], P_['lam'], AF.Exp, TP, TP)
        sincos(P_['sth'], P_['cth'], P_['th'], P_['u1'], P_['u2'], TP, TP, TP, 'prm')
        ts(P_['a512'], P_['th'], 512.0, None, ALU.mult, None, TP, TP)
        sincos(P_['s512'], P_['c512'], P_['a512'], P_['u3'], P_['u4'], TP, TP, TP, 'prm')
        tt(P_['abr'], P_['rho'], P_['cth'], ALU.mult, TP, TP)
        tt(P_['abi'], P_['rho'], P_['sth'], ALU.mult, TP, TP)
        tt(P_['u1'], P_['are'], P_['are'], ALU.mult, TP, TP)
        tt(P_['u2'], P_['aim'], P_['aim'], ALU.mult, TP, TP)
        tt(P_['den'], P_['u1'], P_['u2'], ALU.add, TP, TP)
        recip(P_['rden'], P_['den'], TP, TP)
        ts(P_['nr'], P_['abr'], -1.0, None, ALU.add, None, TP, TP)
        tt(P_['u1'], P_['nr'], P_['are'], ALU.mult, TP, TP)
        tt(P_['u2'], P_['abi'], P_['aim'], ALU.mult, TP, TP)
        tt(P_['u1'], P_['u1'], P_['u2'], ALU.add, TP, TP)
        tt(P_['fr'], P_['u1'], P_['rden'], ALU.mult, TP, TP)
        tt(P_['u1'], P_['abi'], P_['are'], ALU.mult, TP, TP)
        tt(P_['u2'], P_['nr'], P_['aim'], ALU.mult, TP, TP)
        tt(P_['u1'], P_['u1'], P_['u2'], ALU.subtract, TP, TP)
        tt(P_['fi'], P_['u1'], P_['rden'], ALU.mult, TP, TP)
        if LVL == 0:
            raise StopS5()
        sraw = AR.alloc([2, 128])
        dma(sraw[0:64, 0, :], st_re, [], ['sraw']); dma(sraw[0:64, 1, :], st_im, [], ['sraw'])
        pb = nxt('pj', 4)
        tr(PS[pb][:, 0:64], sraw[0:64, 0, :], ident[0:64, 0:64], ['sraw', 'ident'], ['ps%d' % pb])
        tr(PS[pb][:, 64:128], sraw[0:64, 1, :], ident[0:64, 0:64], ['sraw', 'ident'], ['ps%d' % pb])
        cp('dve', H0re, PS[pb][:, 0:64].rearrange('p (s r) -> p s r', s=4), ['ps%d' % pb], ['H0'])
        cp('dve', H0im, PS[pb][:, 64:128].rearrange('p (s r) -> p s r', s=4), ['ps%d' % pb], ['H0'])
        dma(smT, k_smask.partition_broadcast(128), [], ['smT'])
        if LVL == -1:
            raise StopS5()
        Braw_re = AR.alloc([16, 16]); Braw_im = AR.alloc([16, 16]); t16a = AR.alloc([16]); t16b = AR.alloc([16])
        Bexp_re = AR.alloc([16, 128]); Bexp_im = AR.alloc([16, 128])
        dma(Braw_re, b_re.rearrange('(r q) c -> q r c', q=128), [], ['Braw'])
        dma(Braw_im, b_im.rearrange('(r q) c -> q r c', q=128), [], ['Braw'])
        memset('pool', Bexp_re, 0.0, ['Bexp']); memset('pool', Bexp_im, 0.0, ['Bexp'])
        for pr_ in range(16 if LVL >= 2 else 0):
            for gi in range(2):
                rows = slice(64 * gi, 64 * gi + 64)
                col0 = (pr_ % 4) * 32 + gi * 16
                fr_, fi_ = P_['fr'][rows, pr_:pr_ + 1], P_['fi'][rows, pr_:pr_ + 1]
                ts(t16a[rows, :], Braw_im[rows, pr_, :], fi_, None, ALU.mult, None, ['Braw', 'prm'], ['t16a'])
                stt(Bexp_re[rows, pr_, col0:col0 + 16], Braw_re[rows, pr_, :], fr_, t16a[rows, :], ALU.mult, ALU.subtract,
                    ['Braw', 'prm', 't16a'], ['Bexp'])
                ts(t16b[rows, :], Braw_re[rows, pr_, :], fi_, None, ALU.mult, None, ['Braw', 'prm'], ['t16b'])
                stt(Bexp_im[rows, pr_, col0:col0 + 16], Braw_im[rows, pr_, :], fr_, t16b[rows, :], ALU.mult, ALU.add,
                    ['Braw', 'prm', 't16b'], ['Bexp'])
        for src_, dst_, tok in ((Bexp_re, BreT, 'BreT'), (Bexp_im, BimT, 'BimT')):
            for q4 in range(4):
                pb = nxt('pj', 4)
                for j in range(4):
                    tr(PS[pb][:, j * 128:(j + 1) * 128], src_[:, q4 * 4 + j, :], ident, ['Bexp', 'ident'], ['ps%d' % pb])
                cp('act', dst_[:, q4 * 4:q4 * 4 + 4, :], PS[pb][:].rearrange('p (a b) -> p a b', a=4), ['ps%d' % pb], [tok])
        pr.barrier()
        AR.off = mS5
        if LVL == -2:
            raise StopS5()
        X_re = AR.alloc([4, 512]); X_im = AR.alloc([4, 512])
        memset('pool', X_re, 0.0, ['X_re']); memset('pool', X_im, 0.0, ['X_im'])
        for g_ in range(32 if LVL >= 3 else 0):
            r0 = 16 * (g_ % 8)
            cc0 = ((g_ % 8) // 2) * 128 + (g_ % 2) * 64
            dma(X_re[r0:r0 + 16, g_ // 8, cc0:cc0 + 64], cc_re[g_], [], ['X_re'])
            dma(X_im[r0:r0 + 16, g_ // 8, cc0:cc0 + 64], cc_im[g_], [], ['X_im'])
        for q4 in range(4):
            pb = nxt('pj', 4)
            for j in range(4):
                tr(PS[pb][:, j * 128:(j + 1) * 128], X_re[:, q4, j * 128:(j + 1) * 128], ident, ['X_re', 'ident'], ['ps%d' % pb])
            v = PS[pb][:].rearrange('p (a b) -> p a b', a=4)
            cp('act', CreT[:, q4 * 4:q4 * 4 + 4, :], v, ['ps%d' % pb], ['CreT'])
            act(NCreT[:, q4 * 4:q4 * 4 + 4, :], v, AF.Copy, ['ps%d' % pb], ['NCreT'], scale=-1.0)
            pb = nxt('pj', 4)
            for j in range(4):
                tr(PS[pb][:, j * 128:(j + 1) * 128], X_im[:, q4, j * 128:(j + 1) * 128], ident, ['X_im', 'ident'], ['ps%d' % pb])
            act(NCimT[:, q4 * 4:q4 * 4 + 4, :], PS[pb][:].rearrange('p (a b) -> p a b', a=4), AF.Copy, ['ps%d' % pb], ['NCimT'],
                scale=-1.0)
        pr.barrier()
        AR.off = mS5
        if LVL == -3:
            raise StopS5()
        cosT2 = [AR.alloc([512]) for _ in range(2)]; sinT2 = [AR.alloc([512]) for _ in range(2)]; iotaT = AR.alloc([512])
        tg_ang = AR.alloc([512]); tg_a = AR.alloc([512]); tg_b = AR.alloc([512])
        t_ang = AR.alloc([1024]); t_a = AR.alloc([1024]); t_b = AR.alloc([1024])
        Sre = [AR.alloc([512]) for _ in range(2)]; Sim = [AR.alloc([512]) for _ in range(2)]
        Zb = AR.alloc([4, 512], BF16)
        dma(iotaT, k_iota[0:1, 0:512].partition_broadcast(128), [], ['iotaT'])
        m1, m2, g_re, g_im = t_a[:, 0:512], t_a[:, 512:1024], t_b[:, 0:512], t_b[:, 512:1024]
        m3, m4 = t_ang[:, 0:512], t_ang[:, 512:1024]
        G2 = [t_b, AR.alloc([1024])]
        YB = [0, 1, 2, 3, 4]

        def gen_tables(p_):
            ts(tg_ang, iotaT, P_['th'][:, p_:p_ + 1], None, ALU.mult, None, ['iotaT', 'prm'], ['tg_ang'])
            sincos(sinT2[p_ % 2], cosT2[p_ % 2], tg_ang, tg_a, tg_b, ['tg_ang'], ['sinT%d' % (p_ % 2)], ['cosT%d' % (p_ % 2)], 'tg_')
        for pr_ in range({4: 1, 5: 4}.get(LVL, 16) if LVL >= 4 else 0):
            qc = pr_ // 4
            thp = P_['th'][:, pr_:pr_ + 1]
            rho_p = P_['rho'][:, pr_:pr_ + 1]
            if pr_ == 0:
                gen_tables(0)
            if pr_ + 1 < 16:
                gen_tables(pr_ + 1)
            cosT, sinT = cosT2[pr_ % 2], sinT2[pr_ % 2]
            tcs, tsn = 'cosT%d' % (pr_ % 2), 'sinT%d' % (pr_ % 2)
            ts(rhoS, smT, rho_p, None, ALU.mult, None, ['smT', 'prm'], ['rhoS'])
            state = {'prev': None}

            def s5pre(bi):
                off, n = BLOCKS[bi]
                g_re, g_im = G2[bi % 2][:, 0:512], G2[bi % 2][:, 512:1024]
                tgb = 't_b%d' % (bi % 2)
                mm(PS[5][:, 0:n], BreT[:, pr_, :], uT[:, qc, off:off + n], True, True, ['BreT', 'uT'], ['ps5'])
                mm(PS[6][:, 0:n], BimT[:, pr_, :], uT[:, qc, off:off + n], True, True, ['BimT', 'uT'], ['ps6'])
                if bi < 4:
                    cs, sn = cosT[:, 0:n], sinT[:, 0:n]
                    vw = lambda a: a
                else:
                    cs = cosT[:, 0:8].unsqueeze(1).to_broadcast([128, 4, 8])
                    sn = sinT[:, 0:8].unsqueeze(1).to_broadcast([128, 4, 8])
                    vw = lambda a: a.rearrange('p (s t) -> p s t', s=4)
                TB = [tcs, tsn]
                PE_ = 'pool' if (bi < 4 and os.environ.get('KS5POOL', '0') == '1') else 'dve'
                tt(vw(m1[:, 0:n]), vw(PS[5][:, 0:n]), cs, ALU.mult, ['ps5'] + TB, ['t_a'])
                tt(vw(m2[:, 0:n]), vw(PS[6][:, 0:n]), sn, ALU.mult, ['ps6'] + TB, ['t_a'])
                tt(g_re[:, 0:n], m1[:, 0:n], m2[:, 0:n], ALU.add, ['t_a'], [tgb], eng=PE_)
                tt(vw(m3[:, 0:n]), vw(PS[6][:, 0:n]), cs, ALU.mult, ['ps6'] + TB, ['t_ang'])
                tt(vw(m4[:, 0:n]), vw(PS[5][:, 0:n]), sn, ALU.mult, ['ps5'] + TB, ['t_ang'])
                tt(g_im[:, 0:n], m3[:, 0:n], m4[:, 0:n], ALU.subtract, ['t_ang'], [tgb], eng=PE_)

            def s5post(bi):
                off, n = BLOCKS[bi]
                g_re, g_im = G2[bi % 2][:, 0:512], G2[bi % 2][:, 512:1024]
                tgb = 't_b%d' % (bi % 2)
                if bi < 4:
                    cs, sn = cosT[:, 0:n], sinT[:, 0:n]
                    vw = lambda a: a
                else:
                    cs = cosT[:, 0:8].unsqueeze(1).to_broadcast([128, 4, 8])
                    sn = sinT[:, 0:8].unsqueeze(1).to_broadcast([128, 4, 8])
                    vw = lambda a: a.rearrange('p (s t) -> p s t', s=4)
                TB = [tcs, tsn]
                PE_ = 'dve'
                prev = state['prev']
                sl = bi % 2
                if bi < 4:
                    d0 = rho_p.to_broadcast([128, n])
                    if prev is None:
                        i_re = i_im = 0.0
                        rr = [tgb, 'prm']
                    else:
                        c5, s5 = P_['c512'][:, pr_:pr_ + 1], P_['s512'][:, pr_:pr_ + 1]
                        pr_l, pi_l = Sre[prev][:, 511:512], Sim[prev][:, 511:512]
                        ts(cry[:, 2:3], pi_l, s5, None, ALU.mult, None, ['S%d' % prev, 'prm'], ['cry'])
                        stt(cry[:, 0:1], pr_l, c5, cry[:, 2:3], ALU.mult, ALU.subtract, ['S%d' % prev, 'prm', 'cry'], ['cry'])
                        ts(cry[:, 3:4], pr_l, s5, None, ALU.mult, None, ['S%d' % prev, 'prm'], ['cry'])
                        stt(cry[:, 1:2], pi_l, c5, cry[:, 3:4], ALU.mult, ALU.add, ['S%d' % prev, 'prm', 'cry'], ['cry'])
                        i_re, i_im = cry[:, 0:1], cry[:, 1:2]
                        rr = [tgb, 'prm', 'cry']
                else:
                    ts(t4a, H0im[:, :, pr_], P_['abi'][:, pr_:pr_ + 1], None, ALU.mult, None, ['H0', 'prm'], ['t4a'])
                    stt(ah_re, H0re[:, :, pr_], P_['abr'][:, pr_:pr_ + 1], t4a, ALU.mult, ALU.subtract, ['H0', 'prm', 't4a'], ['ah'])
                    ts(t4b, H0re[:, :, pr_], P_['abi'][:, pr_:pr_ + 1], None, ALU.mult, None, ['H0', 'prm'], ['t4b'])
                    stt(ah_im, H0im[:, :, pr_], P_['abr'][:, pr_:pr_ + 1], t4b, ALU.mult, ALU.add, ['H0', 'prm', 't4b'], ['ah'])
                    gv_re = g_re[:, 0:32].rearrange('p (s t) -> p s t', s=4)[:, :, 0]
                    gv_im = g_im[:, 0:32].rearrange('p (s t) -> p s t', s=4)[:, :, 0]
                    tt(gv_re, gv_re, ah_re, ALU.add, [tgb, 'ah'], [tgb])
                    tt(gv_im, gv_im, ah_im, ALU.add, [tgb, 'ah'], [tgb])
                    d0 = rhoS[:, 0:32]
                    i_re = i_im = 0.0
                    rr = [tgb, 'rhoS']
                scan(Sre[sl][:, 0:n], d0, g_re[:, 0:n], i_re, rr, ['S%d' % sl])
                scan(Sim[sl][:, 0:n], d0, g_im[:, 0:n], i_im, rr, ['S%d' % sl])
                TS = ['S%d' % sl] + TB
                tt(vw(Zb[:, 0, 0:n]), vw(Sre[sl][:, 0:n]), cs, ALU.mult, TS, ['Zb'], eng=PE_)
                tt(vw(Zb[:, 1, 0:n]), vw(Sim[sl][:, 0:n]), sn, ALU.mult, TS, ['Zb'], eng=PE_)
                tt(vw(Zb[:, 2, 0:n]), vw(Sim[sl][:, 0:n]), cs, ALU.mult, TS, ['Zb'], eng=PE_)
                tt(vw(Zb[:, 3, 0:n]), vw(Sre[sl][:, 0:n]), sn, ALU.mult, TS, ['Zb'], eng=PE_)
                for k, (W, wt_) in enumerate(((CreT, 'CreT'), (NCreT, 'NCreT'), (NCimT, 'NCimT'), (NCimT, 'NCimT'))):
                    mm(PS[YB[bi]][:, 0:n], W[:, pr_, :], Zb[:, k, 0:n], pr_ % 4 == 0 and k == 0, pr_ % 4 == 3 and k == 3,
                       [wt_, 'Zb'], ['ps%d' % YB[bi]])
                if bi == 3:
                    cl, sl_ = cosT[:, 511:512], sinT[:, 511:512]
                    sr, si = Sre[sl][:, 511:512], Sim[sl][:, 511:512]
                    tt(c1t[:, 0:1], cl, sr, ALU.mult, TS, ['c1t']); tt(c1t[:, 1:2], sl_, si, ALU.mult, TS, ['c1t'])
                    tt(HSre[:, pr_:pr_ + 1], c1t[:, 0:1], c1t[:, 1:2], ALU.subtract, ['c1t'], ['HS'])
                    tt(c1t[:, 2:3], cl, si, ALU.mult, TS, ['c1t']); tt(c1t[:, 3:4], sl_, sr, ALU.mult, TS, ['c1t'])
                    tt(HSim[:, pr_:pr_ + 1], c1t[:, 2:3], c1t[:, 3:4], ALU.add, ['c1t'], ['HS'])
                if bi == 4:
                    sr = Sre[sl][:, 0:32].rearrange('p (s t) -> p s t', s=4)[:, :, 7]
                    si = Sim[sl][:, 0:32].rearrange('p (s t) -> p s t', s=4)[:, :, 7]
                    c7, s7 = cosT[:, 7:8], sinT[:, 7:8]
                    ts(t4a, si, s7, None, ALU.mult, None, TS, ['t4a'])
                    stt(HSsre[:, :, pr_], sr, c7, t4a, ALU.mult, ALU.subtract, TS + ['t4a'], ['HSs'])
                    ts(t4b, sr, s7, None, ALU.mult, None, TS, ['t4b'])
                    stt(HSsim[:, :, pr_], si, c7, t4b, ALU.mult, ALU.add, TS + ['t4b'], ['HSs'])
                state['prev'] = sl if bi < 4 else None

            s5pre(0)
            for bi in range(5):
                if bi + 1 < 5:
                    s5pre(bi + 1)
                s5post(bi)
            if pr_ % 4 == 3:
                for bi, (off, n) in enumerate(BLOCKS):
                    yf, x2 = t_ang[:, 0:n], t_ang[:, 512:512 + n]
                    stt(yf, uT[:, qc, off:off + n], G[:, GO['d'] + qc:GO['d'] + qc + 1], PS[YB[bi]][:, 0:n], ALU.mult, ALU.add,
                        ['uT', 'G', 'ps%d' % YB[bi]], ['t_ang'])
                    act(x2, yf, AF.Square, ['t_ang'], ['t_ang'])
                    ts(x2, x2, 0.044715, 1.0, ALU.mult, ALU.add, ['t_ang'], ['t_ang'])
                    tt(x2, x2, yf, ALU.mult, ['t_ang'], ['t_ang'])
                    act(x2, x2, AF.Sigmoid, ['t_ang'], ['t_ang'], scale=2.0 * math.sqrt(2.0 / math.pi))
                    tt(ssm_o[:, qc, off:off + n], yf, x2, ALU.mult, ['t_ang'], ['ssm_o'])
        if LVL == -4:
            raise StopS5()
        hso = t_ang[:, 0:512].rearrange('p (a b) -> p a b', a=4)
        pb = nxt('pj', 4)
        tr(PS[pb][0:16, 0:128], HSre, ident, ['HS', 'ident'], ['ps%d' % pb])
        tr(PS[pb][0:16, 128:256], HSim, ident, ['HS', 'ident'], ['ps%d' % pb])
        tr(PS[pb][0:64, 256:384], HSsre[:].rearrange('p s r -> p (s r)'), ident, ['HSs', 'ident'], ['ps%d' % pb])
        tr(PS[pb][0:64, 384:512], HSsim[:].rearrange('p s r -> p (s r)'), ident, ['HSs', 'ident'], ['ps%d' % pb])
        cp('act', hso[0:64, :, :], PS[pb][0:64, :].rearrange('p (a b) -> p a b', a=4), ['ps%d' % pb], ['t_ang'])
        dma_out(o_sre_p, hso[0:16, 0, :], ['t_ang']); dma_out(o_sim_p, hso[0:16, 1, :], ['t_ang'])
        dma_out(o_sre_s, hso[0:64, 2, :], ['t_ang']); dma_out(o_sim_s, hso[0:64, 3, :], ['t_ang'])
        pr.barrier()
        AR.off = mS5
        if LVL == -5:
            raise StopS5()
        Wglu = AR.alloc([4, 512], BF16); gate = AR.alloc([4, 512], BF16)
        cast_load(Wglu, w_glu.rearrange('(c p) n -> p c n', p=128), 'Wglu')
        for bi, (off, n) in enumerate(BLOCKS):
            for oc in range(4):
                pb = nxt('pj', 4)
                for c in range(4):
                    mm(PS[pb][:, 0:n], Wglu[:, c, oc * 128:(oc + 1) * 128], ssm_o[:, c, off:off + n], c == 0, c == 3,
                       ['Wglu', 'ssm_o'], ['ps%d' % pb])
                act(gate[:, oc, 0:n], PS[pb][:, 0:n], AF.Sigmoid, ['ps%d' % pb], ['gate'])
            for oc in range(4):
                tt(ssm_o[:, oc, off:off + n], ssm_o[:, oc, off:off + n], gate[:, oc, 0:n], ALU.mult, ['ssm_o', 'gate'], ['ssm_o'])
        if debug:
            dbg_ssm = dout('dbg_ssm', [128, 4, NTOK], BF16)
            dma_out(dbg_ssm, ssm_o, ['ssm_o'])
        pr.barrier()
        AR.off = mS5


    except StopS5:
        pr.barrier()
        AR.off = mS5

    mM0 = AR.off
    if '2' in KOLD:
        KmT = AR.alloc([4, 256], BF16); Vm = AR.alloc([2, 512], BF16)
    memtm = AR.alloc([2, 1024]); memT = AR.alloc([8, 256]); mnT = AR.alloc([8, 256], BF16)
    Wkm = AR.alloc([8, 512], BF16); Wvm = AR.alloc([8, 512], BF16); mko = AR.alloc([2, 512])
    cast_load(Wkm, w_km.rearrange('(c p) n -> p c n', p=128), 'Wkm')
    cast_load(Wvm, w_vm.rearrange('(c p) n -> p c n', p=128), 'Wvm')
    for t in range(2):
        dma(memtm[:, t, :], mem_p[t * 128:(t + 1) * 128, :], [], ['memtm%d' % t])
        for half in range(2):
            pb = nxt('pj', 4)
            for c4 in range(4):
                c = half * 4 + c4
                tr(PS[pb][:, c4 * 128:(c4 + 1) * 128], memtm[:, t, c * 128:(c + 1) * 128], ident, ['memtm%d' % t, 'ident'],
                   ['ps%d' % pb])
            cp('act' if half else 'dve', memT[:, half * 4:half * 4 + 4, t * 128:(t + 1) * 128],
               PS[pb][:].rearrange('p (a b) -> p a b', a=4), ['ps%d' % pb], ['memT'])
    NORM(memT, 8, 256, 'mem', 1024, mnT, ['memT'], ['mnT'])
    for t in range(2):
        for wi, (W, wtok, dst) in enumerate(((Wkm, 'Wkm', o_mk_p), (Wvm, 'Wvm', o_mv_p))):
            pb = nxt('pj', 4)
            sl = nxt('mko', 2)
            for k in range(8):
                mm(PS[pb][:, :], mnT[:, k, t * 128:(t + 1) * 128], W[:, k, :], k == 0, k == 7, ['mnT', wtok], ['ps%d' % pb])
            cp('act', mko[:, sl, :], PS[pb][:, :], ['ps%d' % pb], ['mko%d' % sl])
            if wi == 1 and os.environ.get('KM0', '1') == '1':
                cp('dve', Vm[:, t, :], mko[:, sl, :], ['mko%d' % sl], ['Vm'])
            dma_out(dst[t * 128:(t + 1) * 128, :], mko[:, sl, :], ['mko%d' % sl])
    for hd in range(4 if os.environ.get('KM0', '1') == '1' else 0):
        pb = nxt('pj', 4)
        for k in range(8):
            mm(PS[pb][:, 0:256], Wkm[:, k, hd * 128:(hd + 1) * 128], mnT[:, k, :], k == 0, k == 7, ['Wkm', 'mnT'], ['ps%d' % pb])
        cp('act', KmT[:, hd, :], PS[pb][:, 0:256], ['ps%d' % pb], ['KmT'])
    pr.barrier()
    AR.off = mM0

    class StopX(Exception):
        pass

    KCUT = int(os.environ.get('KCUT', '99'))
    try:
        AR.off = mSSM
        if KCUT == 0:
            raise StopX()
        Wout = AR.alloc([8, 1024], BF16)
        mixin2 = [AR.alloc([8, 512], BF16) for _ in range(2)]; f_sb2 = [AR.alloc([8, 512])] * 2
        NSB = norm_set()
        cast_load(Wout, w_out.rearrange('(c p) n -> p c n', p=128), 'Wout')
        SKIP = os.environ.get('KSKIP', '')

        def mixA(bi):
            off, n = BLOCKS[bi]
            NORM(attn_o[:, :, off:off + n], 4, n, 'attn', 512, mixin2[bi % 2][:, 0:4, 0:n], ['attn_o'], ['mixin%d' % (bi % 2)])
            NORM(ssm_o[:, :, off:off + n], 4, n, 'ssm', 512, mixin2[bi % 2][:, 4:8, 0:n], ['ssm_o'], ['mixin%d' % (bi % 2)])

        def mixB(bi):
            off, n = BLOCKS[bi]
            for oc in range(8):
                pb = nxt('pj', 4)
                for k in range(8):
                    mm(PS[pb][:, 0:n], Wout[:, k, oc * 128:(oc + 1) * 128], mixin2[bi % 2][:, k, 0:n], k == 0, k == 7,
                       ['Wout', 'mixin%d' % (bi % 2)], ['ps%d' % pb])
                cp('act' if oc % 2 else 'dve', f_sb2[bi % 2][:, oc, 0:n], PS[pb][:, 0:n], ['ps%d' % pb], ['f_sbm'])

        def mixC(bi):
            off, n = BLOCKS[bi]
            NORM(f_sb2[bi % 2][:, :, 0:n], 8, n, 'mix_post', 1024, None, ['f_sbm'], [t_xT[bi]],
                 resid=xT[:, :, off:off + n], ns=NSB)

        if 'x' not in SKIP:
            mixA(0)
            for bi in range(5):
                if bi + 1 < 5:
                    mixA(bi + 1)
                mixB(bi)
                mixC(bi)
        pr.barrier()
        AR.off = mUT

        if KCUT == 1:
            raise StopX()
        Wqm = AR.alloc([8, 512], BF16); Wom = AR.alloc([4, 1024], BF16)
        KmTs = AR.alloc([4, 4, 256], BF16); Vms = AR.alloc([4, 2, 512], BF16)
        mkr = AR.alloc([2, 2, 512])
        hT2 = AR.alloc([8, 512], BF16); qmT = AR.alloc([4, 512], BF16); omT = AR.alloc([4, 512], BF16)
        osm = AR.alloc([512]); rsm = AR.alloc([512]); f_sb = AR.alloc([8, 512])
        cast_load(Wqm, w_qm.rearrange('(c p) n -> p c n', p=128), 'Wqm')
        cast_load(Wom, w_om.rearrange('(c p) n -> p c n', p=128), 'Wom')
        for s_ in range(4):
            sl = s_ % 2
            dma(mkr[:, sl, :, :], memk[s_].rearrange('(t p) d -> p t d', p=128), [], ['mkr%d' % sl])
            cast_load(Vms[:, s_, :, :], memv[s_].rearrange('(t p) d -> p t d', p=128), 'Vms')
            for t in range(2):
                pb = nxt('pj', 4)
                for hd in range(4):
                    tr(PS[pb][:, hd * 128:(hd + 1) * 128], mkr[:, sl, t, hd * 128:(hd + 1) * 128], ident, ['mkr%d' % sl, 'ident'],
                       ['ps%d' % pb])
                cp('act' if t else 'dve', KmTs[:, s_, :, t * 128:(t + 1) * 128], PS[pb][:].rearrange('p (a b) -> p a b', a=4),
                   ['ps%d' % pb], ['KmTs'])
        if KCUT == 2:
            raise StopX()
        hT2b = [hT2, AR.alloc([8, 512], BF16)]; qmTb = [qmT, AR.alloc([4, 512], BF16)]
        PTm = [AR.alloc([2, 512], BF16) for _ in range(2)]
        NSB2 = norm_set()

        def memA(bi):
            off, n = BLOCKS[bi]
            h2, q2 = hT2b[bi % 2], qmTb[bi % 2]
            NORM(xT[:, :, off:off + n], 8, n, 'mem_pre', 1024, h2[:, :, 0:n], [t_xT[bi]], ['hT2%d' % (bi % 2)])
            for hd in range(4):
                pb = nxt('pj', 4)
                for k in range(8):
                    mm(PS[pb][:, 0:n], Wqm[:, k, hd * 128:(hd + 1) * 128], h2[:, k, 0:n], k == 0, k == 7,
                       ['Wqm', 'hT2%d' % (bi % 2)], ['ps%d' % pb])
                cp('act', q2[:, hd, 0:n], PS[pb][:, 0:n], ['ps%d' % pb], ['qmT%d' % (bi % 2)])

        def memS(bi, hd):
            off, n = BLOCKS[bi]
            q2, tq2 = qmTb[bi % 2], 'qmT%d' % (bi % 2)
            pi = hd % 2
            if bi < 4:
                for t in range(2):
                    sb = 6 + t
                    mm(PS[sb][:, 0:n], KmT[:, hd, t * 128:(t + 1) * 128], q2[:, hd, 0:n], True, True, ['KmT', tq2], ['ps%d' % sb])
                    act(PTm[pi][:, t, 0:n], PS[sb][:, 0:n], AF.Exp, ['ps%d' % sb], ['PTm%d' % pi], scale=MEM_SCALE)
            else:
                sb = 6 + hd % 2
                for s_ in range(4):
                    for t in range(2):
                        c_ = (s_ * 2 + t) * 8
                        mm(PS[sb][:, c_:c_ + 8], KmTs[:, s_, hd, t * 128:(t + 1) * 128], q2[:, hd, 8 * s_:8 * s_ + 8], True, True,
                           ['KmTs', tq2], ['ps%d' % sb])
                act(PTm[pi][:, 0, 0:64], PS[sb][:, 0:64], AF.Exp, ['ps%d' % sb], ['PTm%d' % pi], scale=MEM_SCALE)

        def memO(bi, hd):
            off, n = BLOCKS[bi]
            pi = hd % 2
            bo, bs = (4, 5)
            if bi < 4:
                for t in range(2):
                    mm(PS[bo][:, 0:n], Vm[:, t, hd * 128:(hd + 1) * 128], PTm[pi][:, t, 0:n], t == 0, t == 1, ['Vm', 'PTm%d' % pi],
                       ['ps%d' % bo])
                    mm(PS[bs][:, 0:n], ones_b, PTm[pi][:, t, 0:n], t == 0, t == 1, ['ones_b', 'PTm%d' % pi], ['ps%d' % bs])
            else:
                for s_ in range(4):
                    for t in range(2):
                        c_ = (s_ * 2 + t) * 8
                        mm(PS[bo][:, 8 * s_:8 * s_ + 8], Vms[:, s_, t, hd * 128:(hd + 1) * 128], PTm[pi][:, 0, c_:c_ + 8],
                           t == 0, t == 1, ['Vms', 'PTm%d' % pi], ['ps%d' % bo])
                        mm(PS[bs][:, 8 * s_:8 * s_ + 8], ones_b, PTm[pi][:, 0, c_:c_ + 8], t == 0, t == 1,
                           ['ones_b', 'PTm%d' % pi], ['ps%d' % bs])
            recip(rsm[:, 0:n], PS[bs][:, 0:n], ['ps%d' % bs], ['rsm'])
            cp('act', osm[:, 0:n], PS[bo][:, 0:n], ['ps%d' % bo], ['osm'])
            tt(omT[:, hd, 0:n], osm[:, 0:n], rsm[:, 0:n], ALU.mult, ['osm', 'rsm'], ['omT'])

        def memB(bi):
            off, n = BLOCKS[bi]
            memS(bi, 0)
            for hd in range(4):
                if hd + 1 < 4:
                    memS(bi, hd + 1)
                memO(bi, hd)
            for oc in range(8):
                pb = nxt('pj', 4)
                for k in range(4):
                    mm(PS[pb][:, 0:n], Wom[:, k, oc * 128:(oc + 1) * 128], omT[:, k, 0:n], k == 0, k == 3, ['Wom', 'omT'],
                       ['ps%d' % pb])
                cp('act' if oc % 2 else 'dve', f_sb[:, oc, 0:n], PS[pb][:, 0:n], ['ps%d' % pb], ['f_sb'])

        def memC(bi):
            off, n = BLOCKS[bi]
            NORM(f_sb[:, :, 0:n], 8, n, 'mem_post', 1024, None, ['f_sb'], [t_xT[bi]], resid=xT[:, :, off:off + n], ns=NSB2)

        if 'm' not in SKIP:
            memA(0)
            for bi in range(5):
                if bi + 1 < 5:
                    memA(bi + 1)
                memB(bi)
                memC(bi)
        pr.barrier()
        AR.off = mUT

        if KCUT == 3:
            raise StopX()
        gprev = AR.alloc([22, 2]); Gst = AR.alloc([22, 8]); GoutP = AR.alloc([2, 22]); GoutS = AR.alloc([4, 2, 22])
        cvo = AR.alloc([128])
        mF = AR.off
        stc = AR.alloc([2816])
        dma(stc[0:8, :], st_cv, [], ['stc'])
        pb = nxt('pj', 4)
        for j in range(22):
            tr(PS[pb][:, j * 8:(j + 1) * 8], stc[0:8, j * 128:(j + 1) * 128], ident[0:8, 0:8], ['stc', 'ident'], ['ps%d' % pb])
        cp('dve', Gst, PS[pb][:, 0:176].rearrange('p (j a) -> p j a', j=22), ['ps%d' % pb], ['Gst'])
        pr.barrier()
        AR.off = mF
        if KCUT == 4:
            raise StopX()
        hid = AR.alloc([22, 1056], BF16); f_all = AR.alloc([8, 1056])
        hT3 = f_all[:, 0:4, :].bitcast(BF16)
        hT3 = hT3.rearrange('p a b -> p (a b)')[:, 0:8 * 1056].rearrange('p (a b) -> p a b', a=8)
        Wg = [AR.alloc([8, 128], BF16) for _ in range(2)]; Wu = [AR.alloc([8, 128], BF16) for _ in range(2)]
        Wd = [AR.alloc([22, 128], BF16) for _ in range(2)]
        gsb = [AR.alloc([516]) for _ in range(2)]; a1 = [AR.alloc([512]) for _ in range(2)]
        memset('pool', gprev, 0.0, ['gprev'])
        for sbi, blks in enumerate(((0, 1), (2, 3, 4)) if 'f' not in SKIP else ()):
            sb_off = BLOCKS[blks[0]][0]
            for bi in blks:
                off, n = BLOCKS[bi]
                loc = off - sb_off
                NORM(xT[:, :, off:off + n], 8, n, 'ffn_pre', 1024, hT3[:, :, loc:loc + n], [t_xT[bi]], ['hT3'])
            for j in range(22):
                ws = j % 2
                cast_load(Wg[ws].rearrange('p k f -> p (k f)'), w_gate[j], 'Wg%d' % ws)
                cast_load(Wu[ws].rearrange('p k f -> p (k f)'), w_up[j], 'Wu%d' % ws)
                cw0 = G[:, GO['cw0'] + j:GO['cw0'] + j + 1]; cw1 = G[:, GO['cw1'] + j:GO['cw1'] + j + 1]
                cw2 = G[:, GO['cw2'] + j:GO['cw2'] + j + 1]; cb = G[:, GO['cb'] + j:GO['cb'] + j + 1]
                for bi in blks:
                    off, n = BLOCKS[bi]
                    loc = off - sb_off
                    pg = nxt('fg', 2); pu = 2 + nxt('fu', 2)
                    for k in range(8):
                        mm(PS[pg][:, 0:n], Wg[ws][:, k, :], hT3[:, k, loc:loc + n], k == 0, k == 7, ['Wg%d' % ws, 'hT3'], ['ps%d' % pg])
                    for k in range(8):
                        mm(PS[pu][:, 0:n], Wu[ws][:, k, :], hT3[:, k, loc:loc + n], k == 0, k == 7, ['Wu%d' % ws, 'hT3'], ['ps%d' % pu])
                    gs = nxt('gsb', 2)
                    tg, ta1 = 'gsb%d' % gs, 'a1%d' % gs
                    if bi < 4:
                        cp('act', gsb[gs][:, 2:2 + n], PS[pg][:, 0:n], ['ps%d' % pg], [tg])
                        cp('dve', gsb[gs][:, 0:2], gprev[:, j, :], ['gprev'], [tg])
                        cp('dve', gprev[:, j, :], gsb[gs][:, n:n + 2], [tg], ['gprev'])
                        if bi == 3:
                            cp('dve', GoutP[:, :, j], gsb[gs][:, n:n + 2], [tg], ['GoutP'])
                        v0, v1, v2 = gsb[gs][:, 0:n], gsb[gs][:, 1:n + 1], gsb[gs][:, 2:n + 2]
                        va = a1[gs][:, 0:n]; vu = PS[pu][:, 0:n]; vh = hid[:, j, loc:loc + n]
                    else:
                        g3 = gsb[gs][:, 0:40].rearrange('p (s t) -> p s t', s=4)
                        cp('act', g3[:, :, 2:10], PS[pg][:, 0:32].rearrange('p (s t) -> p s t', s=4), ['ps%d' % pg], [tg])
                        cp('dve', g3[:, :, 0:2], Gst[:, j, :].rearrange('p (s r) -> p s r', s=4), ['Gst'], [tg])
                        cp('dve', GoutS[:, :, :, j], g3[:, :, 8:10], [tg], ['GoutS'])
                        v0, v1, v2 = g3[:, :, 0:8], g3[:, :, 1:9], g3[:, :, 2:10]
                        va = a1[gs][:, 0:32].rearrange('p (s t) -> p s t', s=4)
                        vu = PS[pu][:, 0:32].rearrange('p (s t) -> p s t', s=4)
                        vh = hid[:, j, loc:loc + 32].rearrange('p (s t) -> p s t', s=4)
                    ts(va, v0, cw0, cb, ALU.mult, ALU.add, [tg, 'G'], [ta1])
                    stt(va, v1, cw1, va, ALU.mult, ALU.add, [tg, 'G', ta1], [ta1])
                    stt(va, v2, cw2, va, ALU.mult, ALU.add, [tg, 'G', ta1], [ta1])
                    act(va, va, AF.Silu, [ta1], [ta1])
                    tt(vh, va, vu, ALU.mult, [ta1, 'ps%d' % pu], ['hid'])
            pr.barrier()
            for c in range(8):
                ws = c % 2
                cast_load(Wd[ws].rearrange('p j d -> p (j d)'), w_down[c], 'Wd%d' % ws)
                for bi in blks:
                    off, n = BLOCKS[bi]
                    loc = off - sb_off
                    pb = nxt('pj', 4)
                    for j in range(22):
                        mm(PS[pb][:, 0:n], Wd[ws][:, j, :], hid[:, j, loc:loc + n], j == 0, j == 21, ['Wd%d' % ws, 'hid'], ['ps%d' % pb])
                    cp('act' if c % 2 else 'dve', f_all[:, c, loc:loc + n], PS[pb][:, 0:n], ['ps%d' % pb], ['f_all'])
            for bi in blks:
                off, n = BLOCKS[bi]
                loc = off - sb_off
                NORM(f_all[:, :, loc:loc + n], 8, n, 'ffn_post', 1024, None, ['f_all'], [t_xT[bi]], resid=xT[:, :, off:off + n])
            pr.barrier()
        if KCUT == 5:
            raise StopX()
        pb = nxt('pj', 4)
        tr(PS[pb][0:44, 0:128], GoutP[:].rearrange('p r j -> p (r j)'), ident, ['GoutP', 'ident'], ['ps%d' % pb])
        cp('act', cvo[0:44, :], PS[pb][0:44, 0:128], ['ps%d' % pb], ['cvo'])
        dma_out(o_cv_p.rearrange('r (j p) -> (r j) p', p=128), cvo[0:44, :], ['cvo'])
        gs2 = GoutS[:].rearrange('p s r j -> p (s r j)')
        for hf in range(2):
            pb = nxt('pj', 4)
            tr(PS[pb][0:88, 0:128], gs2[:, hf * 88:(hf + 1) * 88], ident, ['GoutS', 'ident'], ['ps%d' % pb])
            cp('act', cvo[0:88, :], PS[pb][0:88, 0:128], ['ps%d' % pb], ['cvo'])
            dma_out(o_cv_s.rearrange('a (j p) -> (a j) p', p=128)[hf * 88:(hf + 1) * 88, :], cvo[0:88, :], ['cvo'])
        pr.barrier()
        AR.off = mUT


    except StopX:
        pr.barrier()
        AR.off = mUT

    ytm = AR.alloc([2, 1024])
    for bi, (off, n) in enumerate(BLOCKS):
        for t0 in range(0, n, 128):
            nt = min(128, n - t0)
            sl = nxt('ytm', 2)
            for half in range(2):
                pb = nxt('pj', 4)
                for c4 in range(4):
                    c = half * 4 + c4
                    tr(PS[pb][0:nt, c4 * 128:(c4 + 1) * 128], xT[:, c, off + t0:off + t0 + nt], ident, [t_xT[bi], 'ident'],
                       ['ps%d' % pb])
                cp('act' if half else 'dve', ytm[0:nt, sl, half * 512:(half + 1) * 512], PS[pb][0:nt, :], ['ps%d' % pb],
                   ['ytm%d' % sl])
            dst = y_p[off + t0:off + t0 + nt, :] if bi < 4 else y_s
            dma_out(dst, ytm[0:nt, sl, :], ['ytm%d' % sl])
    for i_ in range(int(os.environ.get('KPAD', '0'))):
        pb = nxt('pj', 4)
        tr(PS[pb][:, 0:128], ident, ident, ['ident'], ['ps%d' % pb])
        cp('act', ytm[:, 0, 0:128], PS[pb][:, 0:128], ['ps%d' % pb], ['ytm0'])
    if os.environ.get('KFB', '1') == '1':
        pr.barrier()
    pr.add('sp', None, ['__out%d' % i for i in range(len(out_dmas))], [])
    pr.emit(nc, es)
    return nc, es


_GO = [('norm_mix_pre', 8), ('q_norm', 2), ('kv_norm', 2), ('norm_attn_out', 4), ('norm_ssm_out', 4),
       ('norm_mix_post', 8), ('norm_mem_pre', 8), ('mem_norm', 8), ('norm_mem_post', 8), ('norm_ffn_pre', 8),
       ('norm_ffn_post', 8), ('ssm_d', 4)]


def _consts():
    half = 16
    inv = (10000.0 ** (-np.arange(half, dtype=np.float32) * np.float32(2.0 / 32))).astype(np.float32)
    pos = np.concatenate([np.arange(2048), np.tile(16384 + np.arange(8), 4)]).astype(np.float32)
    ang = (pos[None, :] * inv[:, None]).astype(np.float32)
    rope = np.zeros((32, 2, NTOK), np.float32)
    rope[:, 0, :] = np.concatenate([np.cos(ang), np.cos(ang)], 0)
    rope[:, 1, :] = np.concatenate([np.sin(ang), np.sin(ang)], 0)
    tri = (np.arange(128)[None, :] >= np.arange(128)[:, None]).astype(np.float32)
    mskn = np.zeros((8, 64), np.float32)
    for i in range(8):
        for h in range(8):
            for t in range(8):
                mskn[i, h * 8 + t] = 1.0 if i <= t else 0.0
    smask = np.ones((1, 32), np.float32); smask[0, ::8] = 0.0
    return dict(k_ident=np.eye(128, dtype=np.float32), k_rope=rope, k_tri=tri, k_mskn=mskn,
                k_iota=np.arange(2048, dtype=np.float32)[None, :], k_smask=smask,
                k_pidx32=(np.arange(128) % 32).astype(np.float32)[:, None])


_CACHE = {}


def make_maps(inp, ncores=NCORES):
    f = lambda a: np.ascontiguousarray(np.asarray(a))
    g = [f(inp[k])[0].reshape(c, 128) for k, c in _GO]
    g.append(f(inp['ffn_conv_w'])[0].reshape(66, 128))
    g.append(f(inp['ffn_conv_b'])[0].reshape(22, 128))
    gains = np.concatenate(g, 0).astype(np.float32)
    assert gains.shape == (160, 128)
    shared = dict(
        c_lat=f(inp['cache_kv_latent'])[0], c_pe=f(inp['cache_k_rope'])[0], gains=gains,
        w_in=f(inp['w_in'])[0], w_uq=f(inp['w_uq'])[0].reshape(256, 768), w_uk=f(inp['w_uk'])[0].reshape(256, 512),
        w_uv=f(inp['w_uv'])[0].reshape(256, 512), a_re=f(inp['ssm_a_re'])[0].reshape(16, 128),
        a_im=f(inp['ssm_a_im'])[0].reshape(16, 128), log_dt=f(inp['ssm_log_dt'])[0].reshape(16, 2),
        b_re=f(inp['ssm_b_re'])[0].reshape(2048, 16), b_im=f(inp['ssm_b_im'])[0].reshape(2048, 16),
        cc_re=f(inp['ssm_c_re'])[0], cc_im=f(inp['ssm_c_im'])[0], w_glu=f(inp['ssm_w_glu'])[0],
        w_out=f(inp['w_out'])[0], w_qm=f(inp['w_q_mem'])[0], w_km=f(inp['w_k_mem'])[0], w_vm=f(inp['w_v_mem'])[0],
        w_om=f(inp['w_o_mem'])[0],
        w_gate=f(f(inp['w_gate'])[0].reshape(8, 128, 22, 128).transpose(2, 1, 0, 3)).reshape(22, 128, 1024),
        w_up=f(f(inp['w_up'])[0].reshape(8, 128, 22, 128).transpose(2, 1, 0, 3)).reshape(22, 128, 1024),
        w_down=f(f(inp['w_down'])[0].reshape(22, 128, 8, 128).transpose(2, 1, 0, 3)).reshape(8, 128, 2816))
    shared.update(_consts())
    in_maps = []
    for c in range(ncores):
        sl = slice(4 * c, 4 * c + 4)
        m = dict(shared)
        m.update(x_p=f(inp['x_prompt'])[c], x_s=f(inp['x_sample'])[sl].reshape(32, 1024), mem_p=f(inp['mem_prompt'])[c],
                 ptab=f(inp['page_table'])[sl].reshape(1, 512).astype(np.int32),
                 st_re=f(inp['state_ssm_re'])[0, sl].reshape(64, 128), st_im=f(inp['state_ssm_im'])[0, sl].reshape(64, 128),
                 st_cv=f(inp['state_ffn_conv'])[0, sl].reshape(8, 2816),
                 memk=f(inp['cache_mem_k'])[0, sl].reshape(4, 256, 512), memv=f(inp['cache_mem_v'])[0, sl].reshape(4, 256, 512))
        in_maps.append({k: np.ascontiguousarray(v) for k, v in m.items()})
    return in_maps


def kernel(**inp):
    if 'nc' not in _CACHE:
        _CACHE['nc'] = build()
    nc, es = _CACHE['nc']
    in_maps = make_maps(inp)
    res = run_bass_kernel_spmd(nc, in_maps, core_ids=list(range(NCORES)))
    R = res.results
    cat = lambda k: np.stack([np.asarray(R[c][k]) for c in range(NCORES)], 0)
    y_p = cat('y_p'); y_s = cat('y_s').reshape(32, 8, 1024)
    outs = (y_p, y_s,
            cat('o_kvl_p')[None], cat('o_kr_p')[None],
            cat('o_sre_p').reshape(1, 8, 32, 64), cat('o_sim_p').reshape(1, 8, 32, 64),
            cat('o_cv_p')[None],
            cat('o_mk_p').reshape(1, 8, 256, 4, 128), cat('o_mv_p').reshape(1, 8, 256, 4, 128),
            cat('o_kvl_s').reshape(1, 32, 8, 256), cat('o_kr_s').reshape(1, 32, 8, 32),
            cat('o_sre_s').reshape(1, 32, 32, 64), cat('o_sim_s').reshape(1, 32, 32, 64),
            cat('o_cv_s').reshape(1, 32, 2, 2816))
    return tuple(np.ascontiguousarray(o, dtype=np.float32) for o in outs)
```
